# Optimizing a Trainium2 kernel written in Bass

```python
import jax
import jax.numpy as jnp
from jax import lax
import numpy as np

D_MODEL = 2048
BATCH = 4
SEQ = 4096
DEPTH = 2

GRID_W = 64
CTX_LEN = 256
D_FF = 5632
N_MOD = 9
EPS = 1e-6
ROPE_BASE = 10000.0
BLOCK = 128

CONV_WIDTH = 1024
CONV_K = 3

MLA_HEADS = 8
MLA_Q_LORA = 512
MLA_KV_LORA = 256
MLA_NOPE = 128
MLA_ROPE = 64
MLA_V = 128
MLA_SCALE = (MLA_NOPE + MLA_ROPE) ** -0.5

GQA_HEADS = 8
GQA_KV_HEADS = 2
GQA_GROUP = GQA_HEADS // GQA_KV_HEADS
GQA_HEAD_DIM = 128
GQA_SCALE = GQA_HEAD_DIM ** -0.5
WINDOW = 128

N_BRANCH = 3
COLS_CONV = 3 * CONV_WIDTH
COLS_MLA = MLA_Q_LORA + MLA_KV_LORA + MLA_ROPE
COLS_GQA = (GQA_HEADS + 2 * GQA_KV_HEADS) * GQA_HEAD_DIM
COLS_GATE = N_BRANCH * D_MODEL
IN_COLS = COLS_CONV + COLS_MLA + COLS_GQA + COLS_GATE
IN_SPLITS = [COLS_CONV, COLS_CONV + COLS_MLA, COLS_CONV + COLS_MLA + COLS_GQA]

kernel_name = "hybrid_dit_conv_mla_swa_macaron"


def rmsnorm(x, g):
    xf = x.astype(jnp.float32)
    y = xf * lax.rsqrt(jnp.mean(xf * xf, axis=-1, keepdims=True) + EPS)
    return (y * g.astype(jnp.float32)).astype(x.dtype)


def modulate(x, shift, scale):
    return x * (1 + scale) + shift


def swiglu(x, w_gu, w_down):
    g, u = jnp.split(x @ w_gu, 2, axis=-1)
    return (jax.nn.silu(g) * u) @ w_down


def rope_1d(x, pos, dim):
    inv = ROPE_BASE ** (-jnp.arange(0, dim, 2, dtype=jnp.float32) / dim)
    ang = pos.astype(jnp.float32)[:, None] * inv[None, :]
    cos = jnp.cos(ang)[:, None, :].astype(x.dtype)
    sin = jnp.sin(ang)[:, None, :].astype(x.dtype)
    x1, x2 = jnp.split(x, 2, axis=-1)
    return jnp.concatenate([x1 * cos - x2 * sin, x1 * sin + x2 * cos], axis=-1)


def rope_2d(x, row, col):
    half = x.shape[-1] // 2
    return jnp.concatenate([rope_1d(x[..., :half], row, half),
                            rope_1d(x[..., half:], col, half)], axis=-1)


def short_conv(u, w):
    s = u.shape[1]
    half = CONV_K // 2
    up = jnp.pad(u, ((0, 0), (half, half), (0, 0)))
    return sum(up[:, k:k + s] * w[k] for k in range(CONV_K))


def gated_short_conv(p, w_conv):
    gb, gc, v = jnp.split(p, 3, axis=-1)
    return gb * short_conv(gc * v, w_conv)


def mla_project(p, q_norm, w_qb, kv_norm, w_kvb):
    b, s = p.shape[:2]
    cq, ckv, k_rope = jnp.split(p, [MLA_Q_LORA, MLA_Q_LORA + MLA_KV_LORA], axis=-1)
    q = (rmsnorm(cq, q_norm) @ w_qb).reshape(b, s, MLA_HEADS, MLA_NOPE + MLA_ROPE)
    kv = (rmsnorm(ckv, kv_norm) @ w_kvb).reshape(b, s, MLA_HEADS, MLA_NOPE + MLA_V)
    return q[..., :MLA_NOPE], q[..., MLA_NOPE:], kv[..., :MLA_NOPE], k_rope, kv[..., MLA_NOPE:]


def mla_core(qn, qr, kn, kr, v):
    s = (jnp.einsum("bqhd,bkhd->bhqk", qn, kn)
         + jnp.einsum("bqhd,bkd->bhqk", qr, kr)).astype(jnp.float32) * MLA_SCALE
    pr = jax.nn.softmax(s, axis=-1).astype(v.dtype)
    return jnp.einsum("bhqk,bkhd->bqhd", pr, v)


def mla_latent(qn, qr, kn, kr, v):
    b, s = qn.shape[:2]
    nblk = s // BLOCK

    def to_blocks(t):
        return jnp.moveaxis(t.reshape(b, nblk, BLOCK, *t.shape[2:]), 1, 0)

    out = lax.map(lambda qb: mla_core(qb[0], qb[1], kn, kr, v), (to_blocks(qn), to_blocks(qr)))
    return jnp.moveaxis(out, 0, 1).reshape(b, s, MLA_HEADS * MLA_V)


def gqa_project(p):
    b, s = p.shape[:2]
    q, k, v = jnp.split(p, [GQA_HEADS * GQA_HEAD_DIM, (GQA_HEADS + GQA_KV_HEADS) * GQA_HEAD_DIM], axis=-1)
    return (q.reshape(b, s, GQA_HEADS, GQA_HEAD_DIM),
            k.reshape(b, s, GQA_KV_HEADS, GQA_HEAD_DIM),
            v.reshape(b, s, GQA_KV_HEADS, GQA_HEAD_DIM))


def sink_logits(sink, shape):
    return jnp.broadcast_to(sink.astype(jnp.float32).reshape(GQA_KV_HEADS, GQA_GROUP, 1, 1), shape)


def window_attend(q, k, v, ck, cv, sink):
    b, s = q.shape[:2]
    nblk = s // BLOCK
    qb = q.reshape(b, nblk, BLOCK, GQA_KV_HEADS, GQA_GROUP, GQA_HEAD_DIM)

    def band(t):
        tp = jnp.pad(t, ((0, 0), (BLOCK, BLOCK), (0, 0), (0, 0)))
        tp = tp.reshape(b, nblk + 2, BLOCK, GQA_KV_HEADS, GQA_HEAD_DIM)
        return jnp.concatenate([tp[:, :-2], tp[:, 1:-1], tp[:, 2:]], axis=2)

    kb, vb = band(k), band(v)
    blk = jnp.arange(nblk)[:, None, None]
    qi = blk * BLOCK + jnp.arange(BLOCK)[None, :, None]
    kj = (blk - 1) * BLOCK + jnp.arange(3 * BLOCK)[None, None, :]
    valid = (jnp.abs(kj - qi) <= WINDOW) & (kj >= 0) & (kj < s)
    s_loc = jnp.einsum("bnqkgd,bnpkd->bnkgqp", qb, kb).astype(jnp.float32) * GQA_SCALE
    s_loc = jnp.where(valid[None, :, None, None], s_loc, -jnp.inf)
    s_ctx = jnp.einsum("bnqkgd,bpkd->bnkgqp", qb, ck).astype(jnp.float32) * GQA_SCALE
    s_snk = sink_logits(sink, s_loc.shape[:-1] + (1,))
    pr = jax.nn.softmax(jnp.concatenate([s_loc, s_ctx, s_snk], axis=-1), axis=-1).astype(v.dtype)
    n_loc = 3 * BLOCK
    o = (jnp.einsum("bnkgqp,bnpkd->bnqkgd", pr[..., :n_loc], vb)
         + jnp.einsum("bnkgqp,bpkd->bnqkgd", pr[..., n_loc:n_loc + ck.shape[1]], cv))
    return o.reshape(b, s, GQA_HEADS * GQA_HEAD_DIM)


def ctx_window_attend(cq, ck, cv, sink):
    b, n = cq.shape[:2]
    qg = cq.reshape(b, n, GQA_KV_HEADS, GQA_GROUP, GQA_HEAD_DIM)
    sc = jnp.einsum("bqkgd,bpkd->bkgqp", qg, ck).astype(jnp.float32) * GQA_SCALE
    s_snk = sink_logits(sink, sc.shape[:-1] + (1,))
    pr = jax.nn.softmax(jnp.concatenate([sc, s_snk], axis=-1), axis=-1)[..., :n].astype(cv.dtype)
    o = jnp.einsum("bkgqp,bpkd->bqkgd", pr, cv)
    return o.reshape(b, n, GQA_HEADS * GQA_HEAD_DIM)


def merge_branches(y_conv, y_mla, y_gqa, p_gate, w_bc, w_bm, w_bg, w_out):
    g_c, g_m, g_g = jnp.split(p_gate, N_BRANCH, axis=-1)
    merged = (jax.nn.sigmoid(g_c) * (y_conv @ w_bc)
              + jax.nn.sigmoid(g_m) * (y_mla @ w_bm)
              + jax.nn.sigmoid(g_g) * (y_gqa @ w_bg))
    return merged @ w_out


def token_mixing(px, pc, row, col, conv_w, mla_q_norm, mla_w_qb, mla_kv_norm, mla_w_kvb,
                 gqa_sink, w_bc, w_bm, w_bg, w_out, with_ctx):
    x_conv, x_mla, x_gqa, x_gate = jnp.split(px, IN_SPLITS, axis=-1)
    c_conv, c_mla, c_gqa, c_gate = jnp.split(pc, IN_SPLITS, axis=-1)

    ya_x = gated_short_conv(x_conv, conv_w)

    qn, qr, kn, kr, v = mla_project(x_mla, mla_q_norm, mla_w_qb, mla_kv_norm, mla_w_kvb)
    qr = rope_2d(qr, row, col)
    kr = rope_2d(kr[:, :, None, :], row, col)[:, :, 0, :]
    cqn, cqr, ckn, ckr, cv = mla_project(c_mla, mla_q_norm, mla_w_qb, mla_kv_norm, mla_w_kvb)
    yb_x = mla_latent(qn, qr,
                      jnp.concatenate([kn, ckn], axis=1),
                      jnp.concatenate([kr, ckr], axis=1),
                      jnp.concatenate([v, cv], axis=1))

    q, k, vg = gqa_project(x_gqa)
    q, k = rope_2d(q, row, col), rope_2d(k, row, col)
    cq, ck, cvg = gqa_project(c_gqa)
    yc_x = window_attend(q, k, vg, ck, cvg, gqa_sink)

    out_x = merge_branches(ya_x, yb_x, yc_x, x_gate, w_bc, w_bm, w_bg, w_out)
    if not with_ctx:
        return out_x, None

    b, n = pc.shape[:2]
    ya_c = gated_short_conv(c_conv, conv_w)
    yb_c = mla_core(cqn, cqr, ckn, ckr, cv).reshape(b, n, MLA_HEADS * MLA_V)
    yc_c = ctx_window_attend(cq, ck, cvg, gqa_sink)
    out_c = merge_branches(ya_c, yb_c, yc_c, c_gate, w_bc, w_bm, w_bg, w_out)
    return out_x, out_c


def setup_inputs(seed: int = 0) -> dict:
    key = jax.random.key(seed)
    ks = jax.random.split(key, 25)

    def nrm(i, shape, scale):
        return jax.random.normal(ks[i], shape, jnp.float32) * scale

    def gain(i, shape):
        return 1.0 + nrm(i, shape, 0.01)

    L, D = DEPTH, D_MODEL
    return {
        "x": nrm(0, (BATCH, SEQ, D), 1.0),
        "c": nrm(1, (BATCH, D), 1.0),
        "ctx": nrm(2, (BATCH, CTX_LEN, D), 1.0),
        "c_ctx": nrm(3, (D,), 1.0),
        "ada_w": nrm(4, (L, D, N_MOD * D), 0.5 * D ** -0.5),
        "ada_b": nrm(5, (L, N_MOD * D), 0.01),
        "ffn1_norm": gain(6, (L, D)),
        "ffn1_w_gu": nrm(7, (L, D, 2 * D_FF), D ** -0.5),
        "ffn1_w_down": nrm(8, (L, D_FF, D), D_FF ** -0.5),
        "mix_norm": gain(9, (L, D)),
        "w_in": nrm(10, (L, D, IN_COLS), D ** -0.5),
        "conv_w": nrm(11, (L, CONV_K, CONV_WIDTH), CONV_K ** -0.5),
        "mla_q_norm": gain(12, (L, MLA_Q_LORA)),
        "mla_w_qb": nrm(13, (L, MLA_Q_LORA, MLA_HEADS * (MLA_NOPE + MLA_ROPE)), MLA_Q_LORA ** -0.5),
        "mla_kv_norm": gain(14, (L, MLA_KV_LORA)),
        "mla_w_kvb": nrm(15, (L, MLA_KV_LORA, MLA_HEADS * (MLA_NOPE + MLA_V)), MLA_KV_LORA ** -0.5),
        "gqa_sink": nrm(16, (L, GQA_HEADS), 0.5),
        "w_branch_conv": nrm(17, (L, CONV_WIDTH, D), CONV_WIDTH ** -0.5),
        "w_branch_mla": nrm(18, (L, MLA_HEADS * MLA_V, D), (MLA_HEADS * MLA_V) ** -0.5),
        "w_branch_gqa": nrm(19, (L, GQA_HEADS * GQA_HEAD_DIM, D), (GQA_HEADS * GQA_HEAD_DIM) ** -0.5),
        "w_out": nrm(20, (L, D, D), D ** -0.5),
        "ffn2_norm": gain(21, (L, D)),
        "ffn2_w_gu": nrm(22, (L, D, 2 * D_FF), D ** -0.5),
        "ffn2_w_down": nrm(23, (L, D_FF, D), D_FF ** -0.5),
        "final_norm": gain(24, (D,)),
    }


def reference(x, c, ctx, c_ctx, ada_w, ada_b, ffn1_norm, ffn1_w_gu, ffn1_w_down, mix_norm, w_in,
              conv_w, mla_q_norm, mla_w_qb, mla_kv_norm, mla_w_kvb, gqa_sink, w_branch_conv,
              w_branch_mla, w_branch_gqa, w_out, ffn2_norm, ffn2_w_gu, ffn2_w_down, final_norm):
    seq = x.shape[1]
    n_rows = seq // GRID_W
    row = jnp.repeat(jnp.arange(n_rows, dtype=jnp.int32), GRID_W)
    col = jnp.tile(jnp.arange(GRID_W, dtype=jnp.int32), n_rows)

    silu_c = jax.nn.silu(c)
    silu_cc = jax.nn.silu(c_ctx)[None, :]
    hx, hc = x, ctx
    for l in range(DEPTH):
        with_ctx = l < DEPTH - 1
        mx = [m[:, None, :] for m in jnp.split(silu_c @ ada_w[l] + ada_b[l], N_MOD, axis=-1)]
        mc = [m[:, None, :] for m in jnp.split(silu_cc @ ada_w[l] + ada_b[l], N_MOD, axis=-1)]

        hx = hx + 0.5 * mx[2] * swiglu(modulate(rmsnorm(hx, ffn1_norm[l]), mx[0], mx[1]),
                                       ffn1_w_gu[l], ffn1_w_down[l])
        hc = hc + 0.5 * mc[2] * swiglu(modulate(rmsnorm(hc, ffn1_norm[l]), mc[0], mc[1]),
                                       ffn1_w_gu[l], ffn1_w_down[l])

        px = modulate(rmsnorm(hx, mix_norm[l]), mx[3], mx[4]) @ w_in[l]
        pc = modulate(rmsnorm(hc, mix_norm[l]), mc[3], mc[4]) @ w_in[l]
        yx, yc = token_mixing(px, pc, row, col, conv_w[l], mla_q_norm[l], mla_w_qb[l],
                              mla_kv_norm[l], mla_w_kvb[l], gqa_sink[l], w_branch_conv[l],
                              w_branch_mla[l], w_branch_gqa[l], w_out[l], with_ctx)
        hx = hx + mx[5] * yx

        hx = hx + 0.5 * mx[8] * swiglu(modulate(rmsnorm(hx, ffn2_norm[l]), mx[6], mx[7]),
                                       ffn2_w_gu[l], ffn2_w_down[l])
        if with_ctx:
            hc = hc + mc[5] * yc
            hc = hc + 0.5 * mc[8] * swiglu(modulate(rmsnorm(hc, ffn2_norm[l]), mc[6], mc[7]),
                                           ffn2_w_gu[l], ffn2_w_down[l])
    return rmsnorm(hx, final_norm)
```

```python
import numpy as np
import ml_dtypes
import concourse.bass as bass
import concourse.mybir as mybir
from concourse.bass_utils import run_bass_kernel_spmd

F32 = mybir.dt.float32
BF16 = mybir.dt.bfloat16
ACT = mybir.ActivationFunctionType
ALU = mybir.AluOpType

COMPUTE = ("tensor", "vector", "scalar", "gpsimd")
ALLENG = ("tensor", "vector", "scalar", "gpsimd", "sync")

D = 2048
NCH = 16
DFF = 5632
FCH = 44
L = 2
SEQ = 4096
HALF = 2048
CTX = 256
TOK = HALF + CTX
EPS = 1e-6
MLA_SCALE = 192 ** -0.5
GQA_SCALE = 128 ** -0.5
C_GB, C_GC, C_V = 0, 1024, 2048
C_CQ, C_CKV, C_KR = 3072, 3584, 3840
C_GQ, C_GK, C_GV = 3904, 4928, 5184
C_GATE = 5440
NKEY = 2 * HALF + CTX
NKC = NKEY // 128
GKEYS = 128 + HALF + 128 + CTX
TILES = [(0, 512), (512, 512), (1024, 512), (1536, 512), (2048, 256)]
SLOT = 4096
NSLOT = 8
LS = 374
S_ADAB, S_N1, S_NM, S_N2, S_QN, S_KVN, S_CONV, S_SINK = 0, 288, 304, 320, 336, 340, 342, 366
S_FIN = 2 * LS
S_CVEC = S_FIN + 16
S_FLAG = S_CVEC + 32
S_EPS = S_FLAG + 2
NS = S_EPS + 3


class T:
    __slots__ = ("ap", "w", "r", "ds", "name")

    def __init__(self, ap, name="", ds=None):
        self.ap = ap
        self.w = None
        self.r = {}
        self.ds = ds
        self.name = name


class Prog:
    def __init__(self, nc):
        self.nc = nc
        self.ops = {e: [] for e in ALLENG}
        self.cnt = {e: 0 for e in COMPUTE}
        self.known = {e: {} for e in ALLENG}
        self.waited = {e: set() for e in COMPUTE}
        self.dval = []
        self.free_ds = []
        self.n_inst = 0

    def new_ds(self):
        if self.free_ds:
            return self.free_ds.pop()
        self.dval.append(0)
        return len(self.dval) - 1

    def _emit_waits(self, X, evs):
        k = self.known[X]
        for ev in evs:
            if ev is None:
                continue
            if ev[0] == 'e':
                if ev[1] == X and X == "tensor":
                    continue
                key, val = ev[1], ev[2]
            else:
                key, val = ev[1] + 1000, self.dval[ev[1]]
            if k.get(key, 0) >= val:
                continue
            k[key] = val
            if ev[0] == 'e':
                self.waited[ev[1]].add(val)
                self.ops[X].append(('w', ev[1], val))
            else:
                self.ops[X].append(('wd', ev[1], val))

    @staticmethod
    def _deps(reads, writes):
        evs = []
        for t in reads:
            evs.append(t.w)
        for t in writes:
            evs.append(t.w)
            evs.extend(t.r.values())
        return evs

    @staticmethod
    def _mark(reads, writes, ev):
        key = ev[1] if ev[0] == 'e' else ev[1] + 1000
        for t in reads:
            t.r[key] = ev
        for t in writes:
            t.w = ev
            t.r = {}

    def op(self, X, fn, reads=(), writes=()):
        self._emit_waits(X, self._deps(reads, writes))
        self.cnt[X] += 1
        seq = self.cnt[X]
        ev = ('e', X, seq)
        self.ops[X].append(('i', fn, seq))
        self._mark(reads, writes, ev)
        self.n_inst += 1
        return ev

    def mm(self, ps, pairs, reads, start=True, stop=True, fresh=None):
        X = "tensor"
        if fresh is None:
            fresh = start
        self._emit_waits(X, self._deps(reads, [ps] if fresh else []))
        n = len(pairs)
        ev = None
        for i, (o, l, r) in enumerate(pairs):
            self.cnt[X] += 1
            seq = self.cnt[X]
            st = start and i == 0
            sp = stop and i == n - 1
            self.ops[X].append(('i', (lambda e, o=o, l=l, r=r, st=st, sp=sp:
                                      e.matmul(o, l, r, start=st, stop=sp)), seq))
            ev = ('e', X, seq)
        self.n_inst += n
        self._mark(reads, [ps], ev)
        return ev

    def dma(self, Q, dst, src, out_ap, in_ap, slow=False):
        dsts = dst if isinstance(dst, (list, tuple)) else [dst]
        srcs = src if isinstance(src, (list, tuple)) else [src]
        self._emit_waits(Q, self._deps(srcs, dsts))
        d0 = dsts[0]
        if d0.ds is None:
            d0.ds = self.new_ds()
        ds = d0.ds
        self.dval[ds] += 16
        ev = ('d', ds, self.dval[ds])
        self.ops[Q].append(('dma', out_ap, in_ap, ds, slow))
        self._mark(srcs, dsts, ev)
        self.n_inst += 1
        return ev

    def cc(self, dst, src, out_ap, in_ap, groups):
        Q = "gpsimd"
        self._emit_waits(Q, self._deps([src], [dst]))
        self.dval.append(1)
        ds = len(self.dval) - 1
        ev = ('d', ds, 1)
        self.ops[Q].append(('cc', out_ap, in_ap, ds, groups))
        self._mark([src], [dst], ev)
        self.n_inst += 1
        return ev

    def barrier(self, tiles):
        evs = []
        for t in tiles:
            evs.append(t.w)
            evs.extend(t.r.values())
        for X in ALLENG:
            if X == "gpsimd":
                continue
            self._emit_waits(X, evs)

    def wait_all(self, X, tiles):
        evs = []
        for t in tiles:
            evs.append(t.w)
            evs.extend(t.r.values())
        self._emit_waits(X, evs)

    def emit(self):
        nc = self.nc
        esem = {e: nc.alloc_semaphore(name=f"es_{e}") for e in COMPUTE}
        dsems = [nc.alloc_semaphore(name=f"ds_{i}") for i in range(len(self.dval))]
        rank = {}
        for e in COMPUTE:
            rank[e] = {q: i + 1 for i, q in enumerate(sorted(self.waited[e]))}

        def run(X, eng):
            myrank = rank.get(X, {})
            for o in self.ops[X]:
                k = o[0]
                if k == 'i':
                    ins = o[1](eng)
                    if o[2] in myrank:
                        ins.then_inc(esem[X], 1)
                elif k == 'w':
                    eng.wait_ge(esem[o[1]], rank[o[1]][o[2]])
                elif k == 'wd':
                    eng.wait_ge(dsems[o[1]], o[2])
                elif k == 'cc':
                    eng.collective_compute("AllGather", ALU.bypass, replica_groups=o[4],
                                           ins=[o[2]], outs=[o[1]]).then_inc(dsems[o[3]])
                elif o[4]:
                    eng.dma_start(out=o[1], in_=o[2], allow_slow_non_contiguous=True).then_inc(dsems[o[3]], 16)
                else:
                    eng.dma_start(out=o[1], in_=o[2]).then_inc(dsems[o[3]], 16)

        with nc.Block() as block:
            @block.tensor
            def _(e):
                run("tensor", e)

            @block.vector
            def _(e):
                run("vector", e)

            @block.scalar
            def _(e):
                run("scalar", e)

            @block.gpsimd
            def _(e):
                run("gpsimd", e)

            @block.sync
            def _(e):
                run("sync", e)


class Arena:
    def __init__(self, P, ap, nbytes):
        self.P = P
        self.ap = ap
        self.nbytes = nbytes
        self.top = 0
        self.live = []

    def _carve(self, nelem, dt):
        bpe = 4 if dt == F32 else 2
        nb = (nelem * bpe + 63) // 64 * 64
        assert self.top + nb <= self.nbytes, f"arena overflow {self.top}+{nb}>{self.nbytes}"
        a = self.ap[:, self.top // 4:(self.top + nb) // 4]
        if dt != F32:
            a = a.bitcast(dt)
        self.top += nb
        return a[:, 0:nelem]

    def tile(self, n, dt, name=""):
        t = T(self._carve(n, dt), name)
        self.live.append(t)
        return t

    def chunks(self, k, n, dt, name="", shared_ds=False):
        a = self._carve(k * n, dt)
        ds = self.P.new_ds() if shared_ds else None
        ts = [T(a[:, i * n:(i + 1) * n], f"{name}{i}", ds) for i in range(k)]
        self.live.extend(ts)
        return ts, a

    def mark(self):
        return (self.top, len(self.live))

    def release(self, m):
        dead = self.live[m[1]:]
        self.P.barrier(dead)
        freed = set()
        for t in dead:
            if t.ds is not None and t.ds not in freed:
                freed.add(t.ds)
                self.P.free_ds.append(t.ds)
            t.ds = None
        del self.live[m[1]:]
        self.top = m[0]


class Rot:
    def __init__(self, ts):
        self.ts = ts
        self.i = 0

    def next(self):
        t = self.ts[self.i % len(self.ts)]
        self.i += 1
        return t


WEIGHT_SPECS = [
    ("ada_w", [L, D, 9 * D]), ("ffn1_w_gu", [L, D, 2 * DFF]), ("ffn1_w_down", [L, DFF, D]),
    ("w_in", [L, D, 11584]), ("winr", [L, D, 1344]),
    ("wqb_n", [L, 512, 1024]), ("wqb_r", [L, 512, 512]), ("wqb_rr", [L, 512, 512]),
    ("wkvb_k", [L, 256, 1024]), ("wkvb_v", [L, 256, 1024]),
    ("w_branch_conv", [L, 1024, D]), ("w_branch_mla", [L, 1024, D]), ("w_branch_gqa", [L, 1024, D]),
    ("w_out", [L, D, D]), ("ffn2_w_gu", [L, D, 2 * DFF]), ("ffn2_w_down", [L, DFF, D]),
]
SCRATCH_SPECS = [
    ("HX", [D, TOK], F32), ("XM", [D, TOK], BF16),
    ("GKE", [2, 128, GKEYS], BF16), ("GVE", [128, 20, 256], BF16),
    ("UE", [1024, HALF + 2], F32), ("UC", [1024, CTX + 2], F32),
    ("CKN", [1024, CTX], BF16), ("CKR", [64, CTX], BF16), ("CV", [8, 128, 2, 128], BF16),
]
XCH_SPECS = [
    ("XU", 1024, 2, F32), ("XG", 256, 256, BF16), ("XGV", 256, 256, BF16), ("XKR", 64, HALF, BF16),
    ("XKN0", 256, HALF, BF16), ("XKN1", 256, HALF, BF16), ("XKN2", 256, HALF, BF16), ("XKN3", 256, HALF, BF16),
    ("XV0", 256, HALF, BF16), ("XV1", 256, HALF, BF16), ("XV2", 256, HALF, BF16), ("XV3", 256, HALF, BF16),
]
PAIRS = [[0, 1], [2, 3], [4, 5], [6, 7]]
def build(mode="FUSED"):
    nc = bass.Bass("TRN2", target_bir_lowering=False)
    P = Prog(nc)
    dr = {}

    def din(name, shape, dt=F32):
        dr[name] = nc.dram_tensor(name, shape, dt, kind="ExternalInput").ap()

    wshape = dict(WEIGHT_SPECS)
    wkeys = []
    din("smalls", [128, NS])
    din("cosg", [128, TOK]); din("sing", [128, TOK])
    din("cosm", [64, TOK]); din("sinm", [64, TOK])
    din("masks", [128, 4 * 512], BF16)
    din("xT", [D, HALF]); din("cT", [D, CTX])
    for n, sh, dt in SCRATCH_SPECS:
        dr[n] = nc.dram_tensor(n, sh, dt).ap()
    for l in range(L):
        for n, r, c, dt in XCH_SPECS:
            dr[f"{n}i{l}"] = nc.dram_tensor(f"{n}i{l}", [r, c], dt).ap()
            dr[f"{n}o{l}"] = nc.dram_tensor(f"{n}o{l}", [2 * r, c], dt).ap()
    dr["outT"] = nc.dram_tensor("outT", [D, HALF], F32, kind="ExternalOutput").ap()
    DT = {k: T(v, k) for k, v in dr.items()}
    WT = T(None, "weights")

    ARENA_BYTES = 204 * 1024
    arena_ap = nc.alloc_sbuf_tensor("arena", [128, ARENA_BYTES // 4], F32).ap()
    A = Arena(P, arena_ap, ARENA_BYTES)
    smalls = A.tile(NS, F32, "smalls")
    modL = [A.tile(288, F32, "mod0"), A.tile(288, F32, "mod1")]
    curl = [0]

    def MOD():
        return modL[curl[0]]
    CA = A.tile(96, F32, "CA")
    CG = A.tile(96, F32, "CG")
    misc = A.tile(64, F32, "misc")
    scb = A.tile(32, BF16, "scb")
    ones = A.tile(128, BF16, "ones")
    masks = A.tile(4 * 512, BF16, "masks")
    sinkrow = A.tile(2 * 512, F32, "sinkrow")
    ring = [A.tile(SLOT, BF16, f"ring{i}") for i in range(NSLOT)]
    ringi = [0]
    hx, hx_all = A.chunks(16, 512, F32, "hx", shared_ds=True)
    HA = Arena(P, hx_all, 16 * 512 * 4)
    n_rstd = A.tile(512, F32, "n_rstd")
    n_tmp = Rot([A.tile(512, F32, f"n_tmp{i}") for i in range(3)])
    psum = [T(nc.alloc_psum_tensor(f"ps{i}", [128, 512], F32).ap(), f"ps{i}") for i in range(8)]
    ps_o, ps_d = psum[0], psum[1]
    psr = Rot(psum[2:])

    def sm(c0, c1=None):
        return smalls.ap[:, c0:(c0 + 1 if c1 is None else c1)]

    def wload(name, l, kc, c0, ncols, k0=0):
        slot = ring[ringi[0] % NSLOT]
        ringi[0] += 1
        view = slot.ap[:, 0:kc * ncols].rearrange("p (k n) -> p k n", k=kc)
        key = f"{name}@{l}"
        if key not in dr:
            dr[key] = nc.dram_tensor(f"{name}_L{l}", wshape[name][1:], F32, kind="ExternalInput").ap()
            wkeys.append((f"{name}_L{l}", name, l))
        srcv = dr[key].rearrange("(c p) n -> p c n", p=128)[:, k0:k0 + kc, c0:c0 + ncols]
        P.dma("gpsimd", slot, WT, view, srcv)
        return slot, view

    P.dma("sync", smalls, DT["smalls"], smalls.ap, dr["smalls"])
    P.dma("sync", masks, DT["masks"], masks.ap, dr["masks"])
    P.op("vector", lambda e: e.memset(ones.ap, 1.0), writes=[ones])

    ada_next = [0, 0]
    silu_done = [False]

    def ada_slabs(l, count, banks):
        b = l * LS
        if not silu_done[0]:
            P.op("scalar", lambda e: e.activation(scb.ap, sm(S_CVEC, S_CVEC + 32), ACT.Silu),
                 reads=[smalls], writes=[scb])
            silu_done[0] = True
        for _ in range(count):
            s = ada_next[l]
            if s >= 72:
                return
            ada_next[l] += 1
            pm = banks[s % len(banks)]
            slot, v = wload("ada_w", l, 16, s * 256, 256)
            for q in range(2):
                n = s * 2 + q
                P.mm(pm, [(pm.ap[:, 2 * n:2 * n + 2], v[:, j, q * 128:(q + 1) * 128], scb.ap[:, 2 * j:2 * j + 2])
                          for j in range(16)], reads=[slot, scb], start=True, stop=True, fresh=(q == 0))
            c0 = 4 * s
            P.op("vector", lambda e, pm=pm, c0=c0, l=l, b=b: e.tensor_tensor(
                modL[l].ap[:, c0:c0 + 4], pm.ap[:, c0:c0 + 4], sm(b + S_ADAB + c0, b + S_ADAB + c0 + 4), op=ALU.add),
                reads=[pm, smalls, modL[l]], writes=[modL[l]])

    def layer_setup(l):
        b = l * LS
        ada_slabs(l, 72, [psum[6], psum[7]])
        curl[0] = l
        mod = MOD()
        for k, sn in enumerate((S_N1, S_NM, S_N2)):
            for w in range(2):
                m_scale = mod.ap.rearrange("p (m j w) -> p m j w", m=9, j=16)[:, 3 * k + 1, :, w]
                outv = CA.ap.rearrange("p (k j w) -> p k j w", k=3, j=16)[:, k, :, w]
                P.op("vector", lambda e, o=outv, i=m_scale, sn=sn: e.scalar_tensor_tensor(
                    o, i, 1.0, sm(b + sn, b + sn + 16), op0=ALU.add, op1=ALU.mult),
                    reads=[mod, smalls], writes=[CA])
                m_gate = mod.ap.rearrange("p (m j w) -> p m j w", m=9, j=16)[:, 3 * k + 2, :, w]
                outg = CG.ap.rearrange("p (k j w) -> p k j w", k=3, j=16)[:, k, :, w]
                gs = 1.0 if k == 1 else 0.5
                P.op("vector", lambda e, o=outg, i=m_gate, gs=gs: e.tensor_scalar(
                    o, i, gs, None, op0=ALU.mult), reads=[mod], writes=[CG])
        P.op("vector", lambda e: e.tensor_scalar(CA.ap, CA.ap, float(np.sqrt(D)), None, op0=ALU.mult),
             reads=[CA], writes=[CA])
        P.op("vector", lambda e: e.tensor_scalar(misc.ap[:, 0:4], sm(b + S_QN, b + S_QN + 4), float(np.sqrt(512)), None,
                                                 op0=ALU.mult), reads=[smalls], writes=[misc])
        P.op("vector", lambda e: e.tensor_scalar(misc.ap[:, 4:6], sm(b + S_KVN, b + S_KVN + 2), 16.0, None,
                                                 op0=ALU.mult), reads=[smalls, misc], writes=[misc])
        P.op("vector", lambda e: e.tensor_scalar(misc.ap[:, 8:24], sm(S_FIN, S_FIN + 16), float(np.sqrt(D)), None,
                                                 op0=ALU.mult), reads=[smalls, misc], writes=[misc])
        P.op("scalar", lambda e: e.activation(misc.ap[:, 24:32], sm(b + S_SINK, b + S_SINK + 8), ACT.Exp),
             reads=[smalls, misc], writes=[misc])
        for h in range(8):
            P.op("vector", lambda e, h=h: e.tensor_scalar(
                sinkrow.ap[:, h * 128:(h + 1) * 128], ones.ap[:, 0:128], misc.ap[:, 24 + h:25 + h], None, op0=ALU.mult),
                reads=[ones, misc, sinkrow], writes=[sinkrow])

    def Acol(k, j, w):
        i = (k * 16 + j) * 2 + w
        return CA.ap[:, i:i + 1]

    def Gcol(k, j, w):
        i = (k * 16 + j) * 2 + w
        return CG.ap[:, i:i + 1]

    def Bcol(m, j, w):
        i = (m * 16 + j) * 2 + w
        return MOD().ap[:, i:i + 1]

    def rmsnorm(xs, N, epscol, a_fn, b_fn, outs, consts, sq_tiles=None):
        sqs = outs if sq_tiles is None else sq_tiles
        pss = psr.next()
        nchunk = len(xs)
        for j, x in enumerate(xs):
            sq = sqs[j]
            if j % 2 == 0:
                P.op("scalar", lambda e, sq=sq, x=x: e.activation(sq.ap[:, :N], x.ap[:, :N], ACT.Square),
                     reads=[x], writes=[sq])
            else:
                P.op("vector", lambda e, sq=sq, x=x: e.tensor_tensor(sq.ap[:, :N], x.ap[:, :N], x.ap[:, :N], op=ALU.mult),
                     reads=[x], writes=[sq])
        for j in range(nchunk):
            sq = sqs[j]
            P.mm(pss, [(pss.ap[:, :N], ones.ap, sq.ap[:, :N])], reads=[ones, sq], start=(j == 0), stop=(j == nchunk - 1))
        rstd = n_rstd
        P.op("scalar", lambda e: e.activation(rstd.ap[:, :N], pss.ap[:, :N], ACT.Sqrt, bias=sm(epscol), scale=1.0),
             reads=[pss, smalls], writes=[rstd])
        P.op("vector", lambda e: e.reciprocal(rstd.ap[:, :N], rstd.ap[:, :N]), reads=[rstd], writes=[rstd])
        for j, x in enumerate(xs):
            o = outs[j]
            aj = a_fn(j)
            if b_fn is None:
                P.op("vector", lambda e, o=o, x=x, aj=aj: e.scalar_tensor_tensor(
                    o.ap[:, :N], x.ap[:, :N], aj, rstd.ap[:, :N], op0=ALU.mult, op1=ALU.mult),
                    reads=[x, rstd] + consts, writes=[o])
            else:
                bj = b_fn(j)
                t = n_tmp.next()
                P.op("vector", lambda e, t=t, x=x, aj=aj: e.scalar_tensor_tensor(
                    t.ap[:, :N], x.ap[:, :N], aj, rstd.ap[:, :N], op0=ALU.mult, op1=ALU.mult),
                    reads=[x, rstd] + consts, writes=[t])
                P.op("scalar", lambda e, t=t, o=o, bj=bj: e.activation(
                    o.ap[:, :N], t.ap[:, :N], ACT.Identity, bias=bj, scale=1.0),
                    reads=[t] + consts, writes=[o])

    def ffn(l, which, xn, N, w):
        m = A.mark()
        k = 0 if which == 1 else 2
        gu = "ffn1_w_gu" if which == 1 else "ffn2_w_gu"
        dn = "ffn1_w_down" if which == 1 else "ffn2_w_down"
        h, _ = A.chunks(FCH, 512, BF16, "h")
        sgs = Rot([A.tile(512, F32, "sg") for _ in range(2)])
        gb, ub = psum[0:4], psum[4:8]
        for s in range(11):
            for banks, c0 in ((gb, 0), (ub, DFF)):
                for kh in range(2):
                    slot, v = wload(gu, l, 8, c0 + s * 512, 512, k0=kh * 8)
                    for q in range(4):
                        p = banks[q]
                        P.mm(p, [(p.ap[:, :N], v[:, j, q * 128:(q + 1) * 128], xn[kh * 8 + j].ap[:, :N]) for j in range(8)],
                             reads=[slot] + xn[kh * 8:(kh + 1) * 8], start=(kh == 0), stop=(kh == 1))
            for q in range(4):
                fc = s * 4 + q
                pg, pu = gb[q], ub[q]
                sg = sgs.next()
                P.op("scalar", lambda e, sg=sg, pg=pg: e.activation(sg.ap[:, :N], pg.ap[:, :N], ACT.Silu),
                     reads=[pg], writes=[sg])
                P.op("vector", lambda e, sg=sg, pu=pu, fc=fc: e.tensor_tensor(
                    h[fc].ap[:, :N], sg.ap[:, :N], pu.ap[:, :N], op=ALU.mult), reads=[sg, pu], writes=[h[fc]])
        for dg in range(4):
            banks = psum[0:4] if dg % 2 == 0 else psum[4:8]
            f0 = 0
            sizes = [8, 8, 8, 8, 8, 4]
            for si, nf in enumerate(sizes):
                slot, v = wload(dn, l, nf, dg * 512, 512, k0=f0)
                for q in range(4):
                    p = banks[q]
                    P.mm(p, [(p.ap[:, :N], v[:, f, q * 128:(q + 1) * 128], h[f0 + f].ap[:, :N]) for f in range(nf)],
                         reads=[slot] + h[f0:f0 + nf], start=(si == 0), stop=(si == len(sizes) - 1))
                f0 += nf
            for q in range(4):
                dc = dg * 4 + q
                p = banks[q]
                P.op("vector", lambda e, p=p, dc=dc: e.scalar_tensor_tensor(
                    hx[dc].ap[:, :N], p.ap[:, :N], Gcol(k, dc, w), hx[dc].ap[:, :N], op0=ALU.mult, op1=ALU.add),
                    reads=[p, hx[dc], CG], writes=[hx[dc]])
        A.release(m)

    def load_hx(srcname, t0, N):
        v = dr[srcname].rearrange("(c p) t -> p c t", p=128)[:, :, t0:t0 + N]
        P.dma("sync", hx, DT[srcname], hx_all.rearrange("p (c t) -> p c t", c=16)[:, :, 0:N], v)

    def store_hx(t0, N):
        v = dr["HX"].rearrange("(c p) t -> p c t", p=128)[:, :, t0:t0 + N]
        P.dma("sync", DT["HX"], hx, v, hx_all.rearrange("p (c t) -> p c t", c=16)[:, :, 0:N])

    def rope_combine(p1, p2, cs, sn, out_ap, np_, N, tmps, out_t, eng_add="vector"):
        t1 = tmps.next()
        t2 = tmps.next()
        P.op("vector", lambda e: e.tensor_tensor(t1.ap[0:np_, :N], p1.ap[0:np_, :N], cs.ap[0:np_, :N], op=ALU.mult),
             reads=[p1, cs], writes=[t1])
        P.op("vector", lambda e: e.tensor_tensor(t2.ap[0:np_, :N], p2.ap[0:np_, :N], sn.ap[0:np_, :N], op=ALU.mult),
             reads=[p2, sn], writes=[t2])
        P.op(eng_add, lambda e: e.tensor_tensor(out_ap, t1.ap[0:np_, :N], t2.ap[0:np_, :N], op=ALU.add),
             reads=[t1, t2], writes=[out_t])

    accs = [(psum[0], psum[1]), (psum[2], psum[3])]
    ring4 = Rot(psum[4:7])

    def attn_stream(groups, PT):
        flat = []
        for gi, g in enumerate(groups):
            n = len(g["items"])
            for i, it in enumerate(g["items"]):
                flat.append((gi, i == 0, i == n - 1, g, it))

        def issue_S(k):
            it = flat[k][4]
            p = ring4.next()
            P.mm(p, it["s_pairs"](p), reads=it["s_reads"]())
            return p
        LOOK = 2
        pend = [issue_S(k) for k in range(min(LOOK, len(flat)))]
        for k in range(len(flat)):
            gi, fi, la, g, it = flat[k]
            if fi and g.get("prefetch") is not None:
                g["prefetch"]()
            p = pend.pop(0)
            if k + LOOK < len(flat):
                pend.append(issue_S(k + LOOK))
            N = g["N"]
            po, pd = accs[gi % 2]
            pt = PT.next()
            P.op("scalar", lambda e, pt=pt, p=p, N=N, sc=g["scale"]: e.activation(
                pt.ap[:, :N], p.ap[:, :N], ACT.Exp, scale=sc), reads=[p], writes=[pt])
            if it["mask"] is not None:
                mk = it["mask"]
                P.op("vector", lambda e, pt=pt, mk=mk, N=N: e.tensor_tensor(
                    pt.ap[:, :N], pt.ap[:, :N], masks.ap[:, mk * 512:mk * 512 + N], op=ALU.mult), reads=[pt, masks], writes=[pt])
            P.mm(po, [(po.ap[:, :N], it["v_lhsT"](), pt.ap[:, :N])], reads=it["v_reads"]() + [pt], start=fi, stop=la)
            P.mm(pd, [(pd.ap[:, :N], ones.ap, pt.ap[:, :N])], reads=[ones, pt], start=fi, stop=la)
            if la:
                g["finish"](po, pd)
                if g.get("post") is not None:
                    g["post"]()

    def phase1_tile(l, ti):
        t0, N = TILES[ti]
        w = 1 if ti == 4 else 0
        mT = A.mark()
        xn, xn_all = A.chunks(16, 512, BF16, "xn")
        if l == 0:
            if w == 0:
                v = dr["xT"].rearrange("(c p) t -> p c t", p=128)[:, :, t0:t0 + N]
                P.dma("sync", hx, DT["xT"], hx_all.rearrange("p (c t) -> p c t", c=16)[:, :, 0:N], v)
            else:
                v = dr["cT"].rearrange("(c p) t -> p c t", p=128)
                P.dma("sync", hx, DT["cT"], hx_all.rearrange("p (c t) -> p c t", c=16)[:, :, 0:N], v)
        else:
            load_hx("HX", t0, N)
        rmsnorm(hx, N, S_EPS, lambda j: Acol(0, j, w), lambda j: Bcol(0, j, w), xn, [CA, MOD()])
        ffn(l, 1, xn, N, w)
        store_hx(t0, N)
        xm = xn
        rmsnorm(hx, N, S_EPS, lambda j: Acol(1, j, w), lambda j: Bcol(3, j, w), xm, [CA, MOD()])
        xmv = dr["XM"].rearrange("(c p) t -> p c t", p=128)[:, :, t0:t0 + N]
        P.dma("sync", DT["XM"], xm, xmv, xn_all.rearrange("p (c t) -> p c t", c=16)[:, :, 0:N])
        mJ = A.mark()
        stg = Rot([A.tile(512, BF16, "stg") for _ in range(3)])
        ft = Rot([A.tile(512, F32, "ft") for _ in range(4)])
        ust = Rot([A.tile(512, F32, "ust") for _ in range(2)])
        cosg = A.tile(512, F32, "cosg"); sing = A.tile(512, F32, "sing")
        cosm = A.tile(512, F32, "cosm"); sinm = A.tile(512, F32, "sinm")
        P.dma("sync", cosg, DT["cosg"], cosg.ap[:, :N], dr["cosg"][:, t0:t0 + N])
        P.dma("sync", sing, DT["sing"], sing.ap[:, :N], dr["sing"][:, t0:t0 + N])
        P.dma("sync", cosm, DT["cosm"], cosm.ap[0:64, :N], dr["cosm"][:, t0:t0 + N])
        P.dma("sync", sinm, DT["sinm"], sinm.ap[0:64, :N], dr["sinm"][:, t0:t0 + N])
        for s in range(4):
            gs, gv = wload("w_in", l, 16, C_GC + s * 256, 256)
            vs, vv = wload("w_in", l, 16, C_V + s * 256, 256)
            for q in range(2):
                jc = s * 2 + q
                pg = psr.next()
                P.mm(pg, [(pg.ap[:, :N], gv[:, j, q * 128:(q + 1) * 128], xm[j].ap[:, :N]) for j in range(16)], reads=[gs] + xm)
                pv = psr.next()
                P.mm(pv, [(pv.ap[:, :N], vv[:, j, q * 128:(q + 1) * 128], xm[j].ap[:, :N]) for j in range(16)], reads=[vs] + xm)
                t = ft.next()
                P.op("scalar", lambda e, t=t, pg=pg: e.activation(t.ap[:, :N], pg.ap[:, :N], ACT.Copy), reads=[pg], writes=[t])
                u = ust.next()
                P.op("vector", lambda e, u=u, t=t, pv=pv: e.tensor_tensor(u.ap[:, :N], t.ap[:, :N], pv.ap[:, :N], op=ALU.mult),
                     reads=[t, pv], writes=[u])
                if w == 0:
                    P.dma("sync", DT["UE"], u, dr["UE"][jc * 128:(jc + 1) * 128, 1 + t0:1 + t0 + N], u.ap[:, :N])
                    if ti == 0:
                        P.dma("sync", DT[f"XUi{l}"], u, dr[f"XUi{l}"][jc * 128:(jc + 1) * 128, 0:1], u.ap[:, 0:1], slow=True)
                    if ti == 3:
                        P.dma("sync", DT[f"XUi{l}"], u, dr[f"XUi{l}"][jc * 128:(jc + 1) * 128, 1:2], u.ap[:, N - 1:N], slow=True)
                else:
                    P.dma("sync", DT["UC"], u, dr["UC"][jc * 128:(jc + 1) * 128, 1:1 + N], u.ap[:, :N])
        ks, kv_ = wload("w_in", l, 16, C_CKV, 256)
        krs, krv = wload("w_in", l, 16, C_KR, 64)
        rs, rv = wload("winr", l, 16, 1280, 64)
        ckv, _ = A.chunks(2, 512, F32, "ckv")
        ckvn, _ = A.chunks(2, 512, BF16, "ckvn")
        for j2 in range(2):
            p = psr.next()
            P.mm(p, [(p.ap[:, :N], kv_[:, j, j2 * 128:(j2 + 1) * 128], xm[j].ap[:, :N]) for j in range(16)], reads=[ks] + xm)
            P.op("scalar", lambda e, p=p, j2=j2: e.activation(ckv[j2].ap[:, :N], p.ap[:, :N], ACT.Copy), reads=[p], writes=[ckv[j2]])
        rmsnorm(ckv, N, S_EPS + 2, lambda j: misc.ap[:, 4 + j:5 + j], None, ckvn, [misc])
        key0 = t0 if w == 0 else 0
        p1 = psr.next()
        P.mm(p1, [(p1.ap[0:64, :N], krv[:, j, 0:64], xm[j].ap[:, :N]) for j in range(16)], reads=[krs] + xm)
        p2 = psr.next()
        P.mm(p2, [(p2.ap[0:64, :N], rv[:, j, 0:64], xm[j].ap[:, :N]) for j in range(16)], reads=[rs] + xm)
        st = stg.next()
        rope_combine(p1, p2, cosm, sinm, st.ap[0:64, :N], 64, N, ft, st)
        krn = f"XKRi{l}" if w == 0 else "CKR"
        P.dma("sync", DT[krn], st, dr[krn][:, key0:key0 + N], st.ap[0:64, :N])
        kks, kkv = wload("wkvb_k", l, 2, 0, 1024)
        for h in range(8):
            p = psr.next()
            P.mm(p, [(p.ap[:, :N], kkv[:, j2, h * 128:(h + 1) * 128], ckvn[j2].ap[:, :N]) for j2 in range(2)], reads=[kks] + ckvn)
            st = stg.next()
            P.op("scalar", lambda e, st=st, p=p: e.activation(st.ap[:, :N], p.ap[:, :N], ACT.Copy), reads=[p], writes=[st])
            if w == 0:
                knn = f"XKN{h // 2}i{l}"
                P.dma("sync", DT[knn], st, dr[knn][(h % 2) * 128:(h % 2 + 1) * 128, key0:key0 + N], st.ap[:, :N])
            else:
                P.dma("sync", DT["CKN"], st, dr["CKN"][h * 128:(h + 1) * 128, key0:key0 + N], st.ap[:, :N])
        vvs, vvv = wload("wkvb_v", l, 2, 0, 1024)
        for tb in range(N // 128):
            c = key0 // 128 + tb
            for hh in range(2):
                p = psr.next()
                P.mm(p, [(p.ap[:, 0:512], ckvn[j2].ap[:, tb * 128:(tb + 1) * 128], vvv[:, j2, hh * 512:(hh + 1) * 512])
                         for j2 in range(2)], reads=[vvs] + ckvn)
                st = stg.next()
                P.op("vector", lambda e, st=st, p=p: e.tensor_copy(st.ap[:, 0:512], p.ap[:, 0:512]), reads=[p], writes=[st])
                if w == 0:
                    for q2 in range(2):
                        vn = f"XV{hh * 2 + q2}i{l}"
                        vdst = dr[vn].rearrange("(h p) (c d) -> h p c d", h=2, d=128)[:, :, c, :]
                        P.dma("sync", DT[vn], st, vdst.rearrange("h p d -> p h d"),
                              st.ap[:, q2 * 256:(q2 + 1) * 256].rearrange("p (h d) -> p h d", h=2))
                else:
                    vdst = dr["CV"][hh * 4:(hh + 1) * 4, :, c, :]
                    P.dma("sync", DT["CV"], st, vdst.rearrange("h p d -> p h d"),
                          st.ap[:, 0:512].rearrange("p (h d) -> p h d", h=4))
        gks, gkv = wload("w_in", l, 16, C_GK, 256)
        gvs, gvv = wload("w_in", l, 16, C_GV, 256)
        rs, rv = wload("winr", l, 16, 1024, 256)
        gkey0 = 128 + t0 if w == 0 else 128 + HALF + 128
        for g in range(2):
            p1 = psr.next()
            P.mm(p1, [(p1.ap[:, :N], gkv[:, j, g * 128:(g + 1) * 128], xm[j].ap[:, :N]) for j in range(16)], reads=[gks] + xm)
            p2 = psr.next()
            P.mm(p2, [(p2.ap[:, :N], rv[:, j, g * 128:(g + 1) * 128], xm[j].ap[:, :N]) for j in range(16)], reads=[rs] + xm)
            st = stg.next()
            rope_combine(p1, p2, cosg, sing, st.ap[:, :N], 128, N, ft, st)
            P.dma("sync", DT["GKE"], st, dr["GKE"][g, :, gkey0:gkey0 + N], st.ap[:, :N])
            if w == 0 and ti == 0:
                P.dma("sync", DT[f"XGi{l}"], st, dr[f"XGi{l}"][g * 128:(g + 1) * 128, 0:128], st.ap[:, 0:128])
            if w == 0 and ti == 3:
                P.dma("sync", DT[f"XGi{l}"], st, dr[f"XGi{l}"][g * 128:(g + 1) * 128, 128:256], st.ap[:, N - 128:N])
        for tb in range(N // 128):
            c = gkey0 // 128 + tb
            p = psr.next()
            P.mm(p, [(p.ap[:, 0:256], xm[j].ap[:, tb * 128:(tb + 1) * 128], gvv[:, j, 0:256]) for j in range(16)], reads=[gvs] + xm)
            st = stg.next()
            P.op("vector", lambda e, st=st, p=p: e.tensor_copy(st.ap[:, 0:256], p.ap[:, 0:256]), reads=[p], writes=[st])
            P.dma("sync", DT["GVE"], st, dr["GVE"][:, c, :], st.ap[:, 0:256])
            if w == 0 and ti == 0 and tb == 0:
                P.dma("sync", DT[f"XGVi{l}"], st, dr[f"XGVi{l}"][0:128, :], st.ap[:, 0:256])
            if w == 0 and ti == 3 and tb == N // 128 - 1:
                P.dma("sync", DT[f"XGVi{l}"], st, dr[f"XGVi{l}"][128:256, :], st.ap[:, 0:256])
        A.release(mJ)
        A.release(mT)

    def phase2_tile(l, ti, final):
        t0, N = TILES[ti]
        w = 1 if ti == 4 else 0
        mT = A.mark()
        xm, xm_all = A.chunks(16, 512, BF16, "xm", shared_ds=True)
        yc, _ = A.chunks(8, 512, BF16, "yc")
        ymla, _ = A.chunks(8, 512, BF16, "ymla")
        ygqa, ygqa_all = A.chunks(8, 512, BF16, "ygqa")
        P.dma("sync", xm, DT["XM"], xm_all.rearrange("p (c t) -> p c t", c=16)[:, :, 0:N], dr["XM"].rearrange("(c p) t -> p c t", p=128)[:, :, t0:t0 + N])
        m = A.mark()
        U, U_all = A.chunks(8, 516, F32, "U", shared_ds=True)
        if w == 0:
            uv = dr["UE"].rearrange("(c p) t -> p c t", p=128)[:, :, t0:t0 + N + 2]
        else:
            uv = dr["UC"].rearrange("(c p) t -> p c t", p=128)
        Uv = U_all.rearrange("p (c t) -> p c t", c=8)[:, :, 0:N + 2]
        P.dma("sync", U, DT["UE" if w == 0 else "UC"], Uv, uv)
        U3 = U_all.rearrange("p (c t) -> p c t", c=8)
        if w == 1:
            P.op("vector", lambda e: e.memset(U3[:, :, 0:1], 0.0), writes=U)
            P.op("vector", lambda e: e.memset(U3[:, :, N + 1:N + 2], 0.0), writes=U)
        else:
            xuo = dr[f"XUo{l}"].rearrange("(r c p) t -> r p c t", r=2, p=128)
            if t0 == 0:
                P.dma("sync", U, DT[f"XUo{l}"], U3[:, :, 0:1], xuo[0][:, :, 1:2], slow=True)
                P.op("vector", lambda e: e.tensor_scalar(U3[:, :, 0:1], U3[:, :, 0:1], sm(S_FLAG), None, op0=ALU.mult),
                     reads=[smalls] + U, writes=U)
            if t0 + N == HALF:
                P.dma("sync", U, DT[f"XUo{l}"], U3[:, :, N + 1:N + 2], xuo[1][:, :, 0:1], slow=True)
                P.op("vector", lambda e: e.tensor_scalar(U3[:, :, N + 1:N + 2], U3[:, :, N + 1:N + 2], sm(S_FLAG + 1), None, op0=ALU.mult),
                     reads=[smalls] + U, writes=U)
        cy = Rot([A.tile(512, F32, "cy") for _ in range(2)])
        cb = l * LS + S_CONV
        for s in range(4):
            gs, gv = wload("w_in", l, 16, C_GB + s * 256, 256)
            for q in range(2):
                jc = s * 2 + q
                pg = psr.next()
                P.mm(pg, [(pg.ap[:, :N], gv[:, j, q * 128:(q + 1) * 128], xm[j].ap[:, :N]) for j in range(16)], reads=[gs] + xm)
                y = cy.next()
                u = U[jc]
                P.op("vector", lambda e, y=y, u=u, jc=jc: e.tensor_scalar(
                    y.ap[:, :N], u.ap[:, 0:N], sm(cb + jc), None, op0=ALU.mult), reads=[u, smalls], writes=[y])
                P.op("vector", lambda e, y=y, u=u, jc=jc: e.scalar_tensor_tensor(
                    y.ap[:, :N], u.ap[:, 1:N + 1], sm(cb + 8 + jc), y.ap[:, :N], op0=ALU.mult, op1=ALU.add),
                    reads=[u, smalls, y], writes=[y])
                P.op("vector", lambda e, y=y, u=u, jc=jc: e.scalar_tensor_tensor(
                    y.ap[:, :N], u.ap[:, 2:N + 2], sm(cb + 16 + jc), y.ap[:, :N], op0=ALU.mult, op1=ALU.add),
                    reads=[u, smalls, y], writes=[y])
                P.op("vector", lambda e, y=y, pg=pg, jc=jc: e.tensor_tensor(
                    yc[jc].ap[:, :N], y.ap[:, :N], pg.ap[:, :N], op=ALU.mult), reads=[y, pg], writes=[yc[jc]])
        A.release(m)
        m = A.mark()
        P.barrier(hx)
        mh = HA.mark()
        qn, _ = HA.chunks(8, 512, BF16, "qn")
        qr, _ = HA.chunks(8, 512, BF16, "qr")
        m2 = A.mark()
        cq, _ = A.chunks(4, 512, F32, "cq")
        cqn, _ = A.chunks(4, 512, BF16, "cqn")
        cosm = A.tile(512, F32, "cosm"); sinm = A.tile(512, F32, "sinm")
        ft = Rot([A.tile(512, F32, "ft") for _ in range(4)])
        P.dma("sync", cosm, DT["cosm"], cosm.ap[0:64, :N], dr["cosm"][:, t0:t0 + N])
        P.dma("sync", sinm, DT["sinm"], sinm.ap[0:64, :N], dr["sinm"][:, t0:t0 + N])
        for c4 in range(4):
            if c4 % 2 == 0:
                qs, qv = wload("w_in", l, 16, C_CQ + c4 * 128, 256)
            p = psr.next()
            P.mm(p, [(p.ap[:, :N], qv[:, j, (c4 % 2) * 128:(c4 % 2 + 1) * 128], xm[j].ap[:, :N]) for j in range(16)], reads=[qs] + xm)
            P.op("scalar", lambda e, p=p, c4=c4: e.activation(cq[c4].ap[:, :N], p.ap[:, :N], ACT.Copy), reads=[p], writes=[cq[c4]])
        rmsnorm(cq, N, S_EPS + 1, lambda j: misc.ap[:, j:j + 1], None, cqn, [misc])
        ns_, nv = wload("wqb_n", l, 4, 0, 1024)
        r1s, r1v = wload("wqb_r", l, 4, 0, 512)
        r2s, r2v = wload("wqb_rr", l, 4, 0, 512)
        for h in range(8):
            p = psr.next()
            P.mm(p, [(p.ap[:, :N], nv[:, c4, h * 128:(h + 1) * 128], cqn[c4].ap[:, :N]) for c4 in range(4)], reads=[ns_] + cqn)
            P.op("scalar", lambda e, p=p, h=h: e.activation(qn[h].ap[:, :N], p.ap[:, :N], ACT.Copy), reads=[p], writes=[qn[h]])
            p1 = psr.next()
            P.mm(p1, [(p1.ap[0:64, :N], r1v[:, c4, h * 64:(h + 1) * 64], cqn[c4].ap[:, :N]) for c4 in range(4)], reads=[r1s] + cqn)
            p2 = psr.next()
            P.mm(p2, [(p2.ap[0:64, :N], r2v[:, c4, h * 64:(h + 1) * 64], cqn[c4].ap[:, :N]) for c4 in range(4)], reads=[r2s] + cqn)
            rope_combine(p1, p2, cosm, sinm, qr[h].ap[0:64, :N], 64, N, ft, qr[h])
        A.release(m2)
        KN = Rot([A.tile(NKEY, BF16, f"KN{i}") for i in range(2)])
        VV = Rot([A.tile(NKEY, BF16, f"VV{i}") for i in range(2)])
        KR = HA.tile(NKEY, BF16, "KR")
        PT = Rot([A.tile(512, BF16, "PT") for _ in range(3)])
        rden = A.tile(512, F32, "rden")
        kchunks = list(range(NKC)) if w == 0 else [NKC - 2, NKC - 1]
        kc0 = kchunks[0] * 128
        nk = len(kchunks) * 128
        if w == 0:
            for r in range(2):
                P.dma("sync", KR, DT[f"XKRo{l}"], KR.ap[0:64, r * HALF:(r + 1) * HALF], dr[f"XKRo{l}"][r * 64:(r + 1) * 64, :])
        P.dma("sync", KR, DT["CKR"], KR.ap[0:64, 2 * HALF:2 * HALF + CTX], dr["CKR"])
        kns = [None] * 8
        vvs_ = [None] * 8

        def load_head(h):
            kn = KN.next()
            vv = VV.next()
            if w == 0:
                for r in range(2):
                    r0 = r * 256 + (h % 2) * 128
                    P.dma("sync", kn, DT[f"XKN{h // 2}o{l}"], kn.ap[:, r * HALF:(r + 1) * HALF],
                          dr[f"XKN{h // 2}o{l}"][r0:r0 + 128, :])
                    P.dma("sync", vv, DT[f"XV{h // 2}o{l}"], vv.ap[:, r * HALF:(r + 1) * HALF],
                          dr[f"XV{h // 2}o{l}"][r0:r0 + 128, :])
            P.dma("sync", kn, DT["CKN"], kn.ap[:, 2 * HALF:2 * HALF + CTX], dr["CKN"][h * 128:(h + 1) * 128, :])
            P.dma("sync", vv, DT["CV"], vv.ap[:, 2 * HALF:2 * HALF + CTX].rearrange("p (c d) -> p c d", d=128), dr["CV"][h])
            kns[h], vvs_[h] = kn, vv

        def mk_finish(h):
            def fin(po, pd):
                P.op("vector", lambda e: e.reciprocal(rden.ap[:, :N], pd.ap[:, :N]), reads=[pd], writes=[rden])
                P.op("vector", lambda e: e.tensor_tensor(ymla[h].ap[:, :N], po.ap[:, :N], rden.ap[:, :N], op=ALU.mult),
                     reads=[po, rden], writes=[ymla[h]])
            return fin

        def mk_item(h, c):
            def s_pairs(p):
                return [(p.ap[:, :N], kns[h].ap[:, c * 128:(c + 1) * 128], qn[h].ap[:, :N]),
                        (p.ap[:, :N], KR.ap[0:64, c * 128:(c + 1) * 128], qr[h].ap[0:64, :N])]
            return dict(s_pairs=s_pairs, s_reads=lambda: [kns[h], KR, qn[h], qr[h]],
                        v_lhsT=lambda: vvs_[h].ap[:, c * 128:(c + 1) * 128], v_reads=lambda: [vvs_[h]], mask=None)

        load_head(0)
        groups = []
        for h in range(8):
            groups.append(dict(N=N, scale=float(MLA_SCALE), items=[mk_item(h, c) for c in kchunks], finish=mk_finish(h),
                               prefetch=(lambda h=h: load_head(h + 1)) if h < 7 else None,
                               post=(lambda: ada_slabs(l + 1, 2, [psum[7]])) if l + 1 < L else None))
        attn_stream(groups, PT)
        HA.release(mh)
        A.release(m)
        m = A.mark()
        NB = N // 128
        qg = A.tile(8 * 512, BF16, "qg")
        cosg = A.tile(512, F32, "cosg"); sing = A.tile(512, F32, "sing")
        ft = Rot([A.tile(512, F32, "ft") for _ in range(4)])
        GK = A.tile(2 * GKEYS, BF16, "GK")
        GV = A.tile(20 * 256, BF16, "GV")
        PT = Rot([A.tile(512, BF16, "PTg") for _ in range(3)])
        dent = A.tile(512, F32, "dent")
        P.dma("sync", cosg, DT["cosg"], cosg.ap[:, :N], dr["cosg"][:, t0:t0 + N])
        P.dma("sync", sing, DT["sing"], sing.ap[:, :N], dr["sing"][:, t0:t0 + N])
        GK3 = GK.ap.rearrange("p (g k) -> p g k", g=2)
        GV3 = GV.ap.rearrange("p (c d) -> p c d", c=20)
        P.dma("sync", GK, DT["GKE"], GK3[:, :, 128:GKEYS], dr["GKE"].rearrange("g p k -> p g k")[:, :, 128:GKEYS])
        P.dma("sync", GV, DT["GVE"], GV3[:, 1:20, :], dr["GVE"][:, 1:20, :])
        if w == 0:
            xgo = dr[f"XGo{l}"].rearrange("(r g p) k -> r p g k", r=2, g=2)
            P.dma("sync", GK, DT[f"XGo{l}"], GK3[:, :, 0:128], xgo[0][:, :, 128:256])
            P.dma("sync", GK, DT[f"XGo{l}"], GK3[:, :, 128 + HALF:128 + HALF + 128], xgo[1][:, :, 0:128])
            P.dma("sync", GV, DT[f"XGVo{l}"], GV.ap[:, 0:256], dr[f"XGVo{l}"][128:256, :])
            P.dma("sync", GV, DT[f"XGVo{l}"], GV.ap[:, 17 * 256:18 * 256], dr[f"XGVo{l}"][256:384, :])
        for s in range(4):
            q1s, q1v = wload("w_in", l, 16, C_GQ + s * 256, 256)
            q2s, q2v = wload("winr", l, 16, s * 256, 256)
            for q in range(2):
                h = s * 2 + q
                p1 = psr.next()
                P.mm(p1, [(p1.ap[:, :N], q1v[:, j, q * 128:(q + 1) * 128], xm[j].ap[:, :N]) for j in range(16)], reads=[q1s] + xm)
                p2 = psr.next()
                P.mm(p2, [(p2.ap[:, :N], q2v[:, j, q * 128:(q + 1) * 128], xm[j].ap[:, :N]) for j in range(16)], reads=[q2s] + xm)
                outv = qg.ap.rearrange("p (b h i) -> p b h i", b=4, h=8)[:, 0:NB, h, :]
                t1 = ft.next(); t2 = ft.next()
                P.op("vector", lambda e, t1=t1, p1=p1: e.tensor_tensor(t1.ap[:, :N], p1.ap[:, :N], cosg.ap[:, :N], op=ALU.mult),
                     reads=[p1, cosg], writes=[t1])
                P.op("vector", lambda e, t2=t2, p2=p2: e.tensor_tensor(t2.ap[:, :N], p2.ap[:, :N], sing.ap[:, :N], op=ALU.mult),
                     reads=[p2, sing], writes=[t2])
                P.op("vector", lambda e, t1=t1, t2=t2, outv=outv: e.tensor_tensor(
                    outv, t1.ap[:, :N].rearrange("p (b i) -> p b i", i=128), t2.ap[:, :N].rearrange("p (b i) -> p b i", i=128),
                    op=ALU.add), reads=[t1, t2], writes=[qg])
        groups = []
        for qb in range(NB):
            nb = t0 // 128 + qb
            if w == 0:
                chunks = [(nb, 2 if nb == 0 else 0), (nb + 1, None), (nb + 2, 3 if nb == 15 else 1), (18, None), (19, None)]
            else:
                chunks = [(18, None), (19, None)]
            for g in range(2):
                rhs = qg.ap[:, qb * 1024 + g * 512: qb * 1024 + (g + 1) * 512]

                def mk_item(c, mk, g=g, rhs=rhs):
                    return dict(s_pairs=lambda p: [(p.ap, GK.ap[:, g * GKEYS + c * 128: g * GKEYS + (c + 1) * 128], rhs)],
                                s_reads=lambda: [GK, qg],
                                v_lhsT=lambda: GV.ap[:, c * 256 + g * 128: c * 256 + (g + 1) * 128],
                                v_reads=lambda: [GV], mask=mk)

                def fin(po, pd, g=g, qb=qb):
                    P.op("vector", lambda e: e.tensor_tensor(dent.ap, pd.ap, sinkrow.ap[:, g * 512:(g + 1) * 512], op=ALU.add),
                         reads=[pd, sinkrow], writes=[dent])
                    P.op("vector", lambda e: e.reciprocal(dent.ap, dent.ap), reads=[dent], writes=[dent])
                    outv = ygqa_all.rearrange("p (h t) -> p h t", h=8)[:, g * 4:(g + 1) * 4, qb * 128:(qb + 1) * 128]
                    P.op("vector", lambda e: e.tensor_tensor(
                        outv, po.ap.rearrange("p (h i) -> p h i", h=4), dent.ap.rearrange("p (h i) -> p h i", h=4), op=ALU.mult),
                        reads=[po, dent], writes=ygqa)
                groups.append(dict(N=512, scale=float(GQA_SCALE), items=[mk_item(c, mk) for c, mk in chunks], finish=fin, prefetch=None))
        attn_stream(groups, PT)
        A.release(m)
        m = A.mark()
        macc, _ = A.chunks(16, 512, F32, "macc")
        merged = yc + ymla
        sig = Rot([A.tile(512, F32, "sig") for _ in range(2)])
        tt = Rot([A.tile(512, F32, "tt") for _ in range(2)])
        for b, (bw, ys) in enumerate((("w_branch_conv", yc), ("w_branch_mla", ymla), ("w_branch_gqa", ygqa))):
            for dg in range(8):
                gs, gv = wload("w_in", l, 16, C_GATE + b * D + dg * 256, 256)
                bs, bv = wload(bw, l, 8, dg * 256, 256)
                for q in range(2):
                    dc = dg * 2 + q
                    pg = psr.next()
                    P.mm(pg, [(pg.ap[:, :N], gv[:, j, q * 128:(q + 1) * 128], xm[j].ap[:, :N]) for j in range(16)], reads=[gs] + xm)
                    pb = psr.next()
                    P.mm(pb, [(pb.ap[:, :N], bv[:, j, q * 128:(q + 1) * 128], ys[j].ap[:, :N]) for j in range(8)], reads=[bs] + ys)
                    sg = sig.next()
                    P.op("scalar", lambda e, sg=sg, pg=pg: e.activation(sg.ap[:, :N], pg.ap[:, :N], ACT.Sigmoid), reads=[pg], writes=[sg])
                    if b == 0:
                        P.op("vector", lambda e, sg=sg, pb=pb, dc=dc: e.tensor_tensor(
                            macc[dc].ap[:, :N], sg.ap[:, :N], pb.ap[:, :N], op=ALU.mult), reads=[sg, pb], writes=[macc[dc]])
                    else:
                        t = tt.next()
                        P.op("vector", lambda e, sg=sg, pb=pb, t=t: e.tensor_tensor(
                            t.ap[:, :N], sg.ap[:, :N], pb.ap[:, :N], op=ALU.mult), reads=[sg, pb], writes=[t])
                        o = macc[dc] if b == 1 else merged[dc]
                        P.op("vector", lambda e, t=t, o=o, dc=dc: e.tensor_tensor(
                            o.ap[:, :N], macc[dc].ap[:, :N], t.ap[:, :N], op=ALU.add), reads=[t, macc[dc]], writes=[o])
        load_hx_from(dr["HX"], DT["HX"], t0, N)
        for s in range(8):
            ws, wv = wload("w_out", l, 16, s * 256, 256)
            for q in range(2):
                dc = s * 2 + q
                p = psr.next()
                P.mm(p, [(p.ap[:, :N], wv[:, j, q * 128:(q + 1) * 128], merged[j].ap[:, :N]) for j in range(16)], reads=[ws] + merged)
                P.op("vector", lambda e, p=p, dc=dc: e.scalar_tensor_tensor(
                    hx[dc].ap[:, :N], p.ap[:, :N], Gcol(1, dc, w), hx[dc].ap[:, :N], op0=ALU.mult, op1=ALU.add),
                    reads=[p, hx[dc], CG], writes=[hx[dc]])
        A.release(m)
        A.release(mT)
        mT = A.mark()
        xn, _ = A.chunks(16, 512, BF16, "xn2")
        rmsnorm(hx, N, S_EPS, lambda j: Acol(2, j, w), lambda j: Bcol(6, j, w), xn, [CA, MOD()])
        ffn(l, 2, xn, N, w)
        if final:
            outs, outs_all = A.chunks(16, 512, F32, "fin")
            rmsnorm(hx, N, S_EPS, lambda j: misc.ap[:, 8 + j:9 + j], None, outs, [misc], sq_tiles=xn)
            ov = dr["outT"].rearrange("(c p) t -> p c t", p=128)[:, :, t0:t0 + N]
            P.dma("sync", DT["outT"], outs, ov, outs_all.rearrange("p (c t) -> p c t", c=16)[:, :, 0:N])
        else:
            store_hx(t0, N)
        A.release(mT)

    def DT_S(n):
        return DT[n + "_in"] if (n + "_in") in DT else DT[n]

    def load_hx_from(ap, t, t0, N):
        v = ap.rearrange("(c p) t -> p c t", p=128)[:, :, t0:t0 + N]
        P.dma("sync", hx, t, hx_all.rearrange("p (c t) -> p c t", c=16)[:, :, 0:N], v)

    for l in range(L):
        layer_setup(l)
        for ti in range(4):
            phase1_tile(l, ti)
        for n, r, c, dt in XCH_SPECS:
            P.cc(DT[f"{n}o{l}"], DT[f"{n}i{l}"], dr[f"{n}o{l}"], dr[f"{n}i{l}"], PAIRS)
        phase1_tile(l, 4)
        for ti in range(5 if l == 0 else 4):
            phase2_tile(l, ti, final=(l == L - 1))
    outs_t = [DT["outT"]]
    P.wait_all("sync", outs_t)
    for X in COMPUTE:
        P.wait_all(X, outs_t)
    P.emit()
    nc._wkeys = list(wkeys)
    build.last = (P.n_inst, len(P.dval), {e: len(P.waited[e]) for e in COMPUTE})
    return nc


def _rope_tables(half):
    def tab(dim_half, nfreq_dim):
        inv = (10000.0 ** (-np.arange(0, dim_half, 2, dtype=np.float32) / np.float32(dim_half))).astype(np.float32)
        return inv
    pos = np.arange(half * HALF, (half + 1) * HALF)
    row = (pos // 64).astype(np.float32)
    col = (pos % 64).astype(np.float32)

    def make(hd):
        inv = tab(hd, None)
        nf = hd // 2
        ar = row[None, :] * inv[:, None]
        ac = col[None, :] * inv[:, None]
        cr, sr, cc, sc = np.cos(ar), np.sin(ar), np.cos(ac), np.sin(ac)
        cos = np.concatenate([cr, cr, cc, cc], 0)
        sin = np.concatenate([-sr, sr, -sc, sc], 0)
        cosf = np.ones((2 * hd, TOK), np.float32)
        sinf = np.zeros((2 * hd, TOK), np.float32)
        cosf[:, :HALF] = cos
        sinf[:, :HALF] = sin
        return cosf, sinf
    cg, sg = make(64)
    cm, sm_ = make(32)
    return cg, sg, cm, sm_


def _fm(v):
    return np.ascontiguousarray(v.reshape(-1, 128).T)


def _prep_static(inp):
    f = lambda a: np.ascontiguousarray(np.asarray(a, dtype=np.float32))
    w_in = f(inp["w_in"])
    pg = np.concatenate([np.arange(32, 64), np.arange(0, 32), np.arange(96, 128), np.arange(64, 96)])
    pm = np.concatenate([np.arange(16, 32), np.arange(0, 16), np.arange(48, 64), np.arange(32, 48)])
    cols = []
    for h in range(8):
        cols.append(C_GQ + h * 128 + pg)
    for g in range(2):
        cols.append(C_GK + g * 128 + pg)
    cols.append(C_KR + pm)
    cols = np.concatenate(cols)
    wqb = f(inp["mla_w_qb"])
    wkvb = f(inp["mla_w_kvb"])
    cn = np.concatenate([h * 192 + np.arange(128) for h in range(8)])
    cr = np.concatenate([h * 192 + 128 + np.arange(64) for h in range(8)])
    crr = np.concatenate([h * 192 + 128 + pm for h in range(8)])
    ck = np.concatenate([h * 256 + np.arange(128) for h in range(8)])
    cv = np.concatenate([h * 256 + 128 + np.arange(128) for h in range(8)])
    W = {
        "ada_w": f(inp["ada_w"]), "ffn1_w_gu": f(inp["ffn1_w_gu"]), "ffn1_w_down": f(inp["ffn1_w_down"]),
        "w_in": w_in, "winr": np.ascontiguousarray(w_in[:, :, cols]),
        "wqb_n": np.ascontiguousarray(wqb[:, :, cn]), "wqb_r": np.ascontiguousarray(wqb[:, :, cr]),
        "wqb_rr": np.ascontiguousarray(wqb[:, :, crr]),
        "wkvb_k": np.ascontiguousarray(wkvb[:, :, ck]), "wkvb_v": np.ascontiguousarray(wkvb[:, :, cv]),
        "w_branch_conv": f(inp["w_branch_conv"]), "w_branch_mla": f(inp["w_branch_mla"]),
        "w_branch_gqa": f(inp["w_branch_gqa"]), "w_out": f(inp["w_out"]),
        "ffn2_w_gu": f(inp["ffn2_w_gu"]), "ffn2_w_down": f(inp["ffn2_w_down"]),
    }
    return W


def _smalls(inp, b, half):
    s = np.zeros((128, NS), np.float32)
    for l in range(L):
        o = l * LS
        ab = _fm(np.asarray(inp["ada_b"][l], np.float32))
        s[:, o + S_ADAB:o + S_ADAB + 288] = np.repeat(ab, 2, axis=1)
        s[:, o + S_N1:o + S_N1 + 16] = _fm(np.asarray(inp["ffn1_norm"][l], np.float32))
        s[:, o + S_NM:o + S_NM + 16] = _fm(np.asarray(inp["mix_norm"][l], np.float32))
        s[:, o + S_N2:o + S_N2 + 16] = _fm(np.asarray(inp["ffn2_norm"][l], np.float32))
        s[:, o + S_QN:o + S_QN + 4] = _fm(np.asarray(inp["mla_q_norm"][l], np.float32))
        s[:, o + S_KVN:o + S_KVN + 2] = _fm(np.asarray(inp["mla_kv_norm"][l], np.float32))
        cw = np.asarray(inp["conv_w"][l], np.float32)
        for k in range(3):
            s[:, o + S_CONV + k * 8:o + S_CONV + (k + 1) * 8] = _fm(cw[k])
        s[:, o + S_SINK:o + S_SINK + 8] = np.asarray(inp["gqa_sink"][l], np.float32)[None, :]
    s[:, S_FIN:S_FIN + 16] = _fm(np.asarray(inp["final_norm"], np.float32))
    cv = np.stack([_fm(np.asarray(inp["c"][b], np.float32)), _fm(np.asarray(inp["c_ctx"], np.float32))], axis=2)
    s[:, S_CVEC:S_CVEC + 32] = cv.reshape(128, 32)
    s[:, S_FLAG] = 0.0 if half == 0 else 1.0
    s[:, S_FLAG + 1] = 1.0 if half == 0 else 0.0
    s[:, S_EPS] = EPS * 2048
    s[:, S_EPS + 1] = EPS * 512
    s[:, S_EPS + 2] = EPS * 256
    return s


def _masks(half):
    j = np.arange(128)[:, None]
    i = np.arange(128)[None, :]
    prev = (j >= i).astype(np.float32)
    nxt = (j <= i).astype(np.float32)
    first = prev if half == 1 else np.zeros_like(prev)
    last = nxt if half == 0 else np.zeros_like(nxt)
    m = np.concatenate([np.tile(x, (1, 4)) for x in (prev, nxt, first, last)], axis=1)
    return m.astype(ml_dtypes.bfloat16)


_NC_CACHE = {}


def _get_nc():
    if "nc" not in _NC_CACHE:
        _NC_CACHE["nc"] = build()
    return _NC_CACHE["nc"]


def kernel(**inp):
    W = _prep_static(inp)
    x = np.asarray(inp["x"], np.float32)
    ctx = np.asarray(inp["ctx"], np.float32)
    nc = _get_nc()
    ws = {k: W[n][l] for k, n, l in nc._wkeys}
    ins = []
    for c in range(8):
        b, half = c // 2, c % 2
        cg, sg, cm, sm_ = _rope_tables(half)
        d = dict(ws)
        d["smalls"] = _smalls(inp, b, half)
        d["cosg"], d["sing"], d["cosm"], d["sinm"] = cg, sg, cm, sm_
        d["masks"] = _masks(half)
        d["xT"] = np.ascontiguousarray(x[b, half * HALF:(half + 1) * HALF, :].T)
        d["cT"] = np.ascontiguousarray(ctx[b].T)
        ins.append(d)
    r = run_bass_kernel_spmd(nc, ins, core_ids=list(range(8))).results
    out = np.zeros((4, SEQ, D), np.float32)
    for c in range(8):
        b, half = c // 2, c % 2
        out[b, half * HALF:(half + 1) * HALF, :] = np.asarray(r[c]["outT"]).T
    return out
```

```python
import numpy as np
import ml_dtypes
import concourse.bass as bass
import concourse.mybir as mybir
from concourse.bass_utils import run_bass_kernel_spmd

F32 = mybir.dt.float32
BF16 = mybir.dt.bfloat16
ACT = mybir.ActivationFunctionType
ALU = mybir.AluOpType

COMPUTE = ("tensor", "vector", "scalar", "gpsimd")
ALLENG = ("tensor", "vector", "scalar", "gpsimd", "sync")

D = 2048
NCH = 16
DFF = 5632
FCH = 44
L = 2
SEQ = 4096
HALF = 2048
CTX = 256
TOK = HALF + CTX
EPS = 1e-6
MLA_SCALE = 192 ** -0.5
GQA_SCALE = 128 ** -0.5
C_GB, C_GC, C_V = 0, 1024, 2048
C_CQ, C_CKV, C_KR = 3072, 3584, 3840
C_GQ, C_GK, C_GV = 3904, 4928, 5184
C_GATE = 5440
NKEY = 2 * HALF + CTX
NKC = NKEY // 128
GKEYS = 128 + HALF + 128 + CTX
TILES = [(0, 512), (512, 512), (1024, 512), (1536, 512), (2048, 256)]
SLOT = 4096
NSLOT = 8
LS = 374
S_ADAB, S_N1, S_NM, S_N2, S_QN, S_KVN, S_CONV, S_SINK = 0, 288, 304, 320, 336, 340, 342, 366
S_FIN = 2 * LS
S_CVEC = S_FIN + 16
S_FLAG = S_CVEC + 32
S_EPS = S_FLAG + 2
NS = S_EPS + 3


class T:
    __slots__ = ("ap", "w", "r", "ds", "name")

    def __init__(self, ap, name="", ds=None):
        self.ap = ap
        self.w = None
        self.r = {}
        self.ds = ds
        self.name = name


class Prog:
    def __init__(self, nc):
        self.nc = nc
        self.ops = {e: [] for e in ALLENG}
        self.cnt = {e: 0 for e in COMPUTE}
        self.known = {e: {} for e in ALLENG}
        self.waited = {e: set() for e in COMPUTE}
        self.dval = []
        self.free_ds = []
        self.n_inst = 0

    def new_ds(self):
        if self.free_ds:
            return self.free_ds.pop()
        self.dval.append(0)
        return len(self.dval) - 1

    def _emit_waits(self, X, evs):
        k = self.known[X]
        for ev in evs:
            if ev is None:
                continue
            if ev[0] == 'e':
                if ev[1] == X and X == "tensor":
                    continue
                key, val = ev[1], ev[2]
            else:
                key, val = ev[1] + 1000, self.dval[ev[1]]
            if k.get(key, 0) >= val:
                continue
            k[key] = val
            if ev[0] == 'e':
                self.waited[ev[1]].add(val)
                self.ops[X].append(('w', ev[1], val))
            else:
                self.ops[X].append(('wd', ev[1], val))

    @staticmethod
    def _deps(reads, writes):
        evs = []
        for t in reads:
            evs.append(t.w)
        for t in writes:
            evs.append(t.w)
            evs.extend(t.r.values())
        return evs

    @staticmethod
    def _mark(reads, writes, ev):
        key = ev[1] if ev[0] == 'e' else ev[1] + 1000
        for t in reads:
            t.r[key] = ev
        for t in writes:
            t.w = ev
            t.r = {}

    def op(self, X, fn, reads=(), writes=()):
        self._emit_waits(X, self._deps(reads, writes))
        self.cnt[X] += 1
        seq = self.cnt[X]
        ev = ('e', X, seq)
        self.ops[X].append(('i', fn, seq))
        self._mark(reads, writes, ev)
        self.n_inst += 1
        return ev

    def mm(self, ps, pairs, reads, start=True, stop=True, fresh=None):
        X = "tensor"
        if fresh is None:
            fresh = start
        self._emit_waits(X, self._deps(reads, [ps] if fresh else []))
        n = len(pairs)
        ev = None
        for i, (o, l, r) in enumerate(pairs):
            self.cnt[X] += 1
            seq = self.cnt[X]
            st = start and i == 0
            sp = stop and i == n - 1
            self.ops[X].append(('i', (lambda e, o=o, l=l, r=r, st=st, sp=sp:
                                      e.matmul(o, l, r, start=st, stop=sp)), seq))
            ev = ('e', X, seq)
        self.n_inst += n
        self._mark(reads, [ps], ev)
        return ev

    def dma(self, Q, dst, src, out_ap, in_ap, slow=False):
        dsts = dst if isinstance(dst, (list, tuple)) else [dst]
        srcs = src if isinstance(src, (list, tuple)) else [src]
        self._emit_waits(Q, self._deps(srcs, dsts))
        d0 = dsts[0]
        if d0.ds is None:
            d0.ds = self.new_ds()
        ds = d0.ds
        self.dval[ds] += 16
        ev = ('d', ds, self.dval[ds])
        self.ops[Q].append(('dma', out_ap, in_ap, ds, slow))
        self._mark(srcs, dsts, ev)
        self.n_inst += 1
        return ev

    def cc(self, dst, src, out_ap, in_ap, groups):
        Q = "gpsimd"
        self._emit_waits(Q, self._deps([src], [dst]))
        self.dval.append(1)
        ds = len(self.dval) - 1
        ev = ('d', ds, 1)
        self.ops[Q].append(('cc', out_ap, in_ap, ds, groups))
        self._mark([src], [dst], ev)
        self.n_inst += 1
        return ev

    def barrier(self, tiles):
        evs = []
        for t in tiles:
            evs.append(t.w)
            evs.extend(t.r.values())
        for X in ALLENG:
            if X == "gpsimd":
                continue
            self._emit_waits(X, evs)

    def wait_all(self, X, tiles):
        evs = []
        for t in tiles:
            evs.append(t.w)
            evs.extend(t.r.values())
        self._emit_waits(X, evs)

    def emit(self):
        nc = self.nc
        esem = {e: nc.alloc_semaphore(name=f"es_{e}") for e in COMPUTE}
        dsems = [nc.alloc_semaphore(name=f"ds_{i}") for i in range(len(self.dval))]
        rank = {}
        for e in COMPUTE:
            rank[e] = {q: i + 1 for i, q in enumerate(sorted(self.waited[e]))}

        def run(X, eng):
            myrank = rank.get(X, {})
            for o in self.ops[X]:
                k = o[0]
                if k == 'i':
                    ins = o[1](eng)
                    if o[2] in myrank:
                        ins.then_inc(esem[X], 1)
                elif k == 'w':
                    eng.wait_ge(esem[o[1]], rank[o[1]][o[2]])
                elif k == 'wd':
                    eng.wait_ge(dsems[o[1]], o[2])
                elif k == 'cc':
                    eng.collective_compute("AllGather", ALU.bypass, replica_groups=o[4],
                                           ins=[o[2]], outs=[o[1]]).then_inc(dsems[o[3]])
                elif o[4]:
                    eng.dma_start(out=o[1], in_=o[2], allow_slow_non_contiguous=True).then_inc(dsems[o[3]], 16)
                else:
                    eng.dma_start(out=o[1], in_=o[2]).then_inc(dsems[o[3]], 16)

        with nc.Block() as block:
            @block.tensor
            def _(e):
                run("tensor", e)

            @block.vector
            def _(e):
                run("vector", e)

            @block.scalar
            def _(e):
                run("scalar", e)

            @block.gpsimd
            def _(e):
                run("gpsimd", e)

            @block.sync
            def _(e):
                run("sync", e)


class Arena:
    def __init__(self, P, ap, nbytes):
        self.P = P
        self.ap = ap
        self.nbytes = nbytes
        self.top = 0
        self.live = []

    def _carve(self, nelem, dt):
        bpe = 4 if dt == F32 else 2
        nb = (nelem * bpe + 63) // 64 * 64
        assert self.top + nb <= self.nbytes, f"arena overflow {self.top}+{nb}>{self.nbytes}"
        a = self.ap[:, self.top // 4:(self.top + nb) // 4]
        if dt != F32:
            a = a.bitcast(dt)
        self.top += nb
        return a[:, 0:nelem]

    def tile(self, n, dt, name=""):
        t = T(self._carve(n, dt), name)
        self.live.append(t)
        return t

    def chunks(self, k, n, dt, name="", shared_ds=False):
        a = self._carve(k * n, dt)
        ds = self.P.new_ds() if shared_ds else None
        ts = [T(a[:, i * n:(i + 1) * n], f"{name}{i}", ds) for i in range(k)]
        self.live.extend(ts)
        return ts, a

    def mark(self):
        return (self.top, len(self.live))

    def release(self, m):
        dead = self.live[m[1]:]
        self.P.barrier(dead)
        freed = set()
        for t in dead:
            if t.ds is not None and t.ds not in freed:
                freed.add(t.ds)
                self.P.free_ds.append(t.ds)
            t.ds = None
        del self.live[m[1]:]
        self.top = m[0]


class Rot:
    def __init__(self, ts):
        self.ts = ts
        self.i = 0

    def next(self):
        t = self.ts[self.i % len(self.ts)]
        self.i += 1
        return t


WEIGHT_SPECS = [
    ("ada_w", [L, D, 9 * D]), ("ffn1_w_gu", [L, D, 2 * DFF]), ("ffn1_w_down", [L, DFF, D]),
    ("w_in", [L, D, 11584]), ("winr", [L, D, 1344]),
    ("wqb_n", [L, 512, 1024]), ("wqb_r", [L, 512, 512]), ("wqb_rr", [L, 512, 512]),
    ("wkvb_k", [L, 256, 1024]), ("wkvb_v", [L, 256, 1024]),
    ("w_branch_conv", [L, 1024, D]), ("w_branch_mla", [L, 1024, D]), ("w_branch_gqa", [L, 1024, D]),
    ("w_out", [L, D, D]), ("ffn2_w_gu", [L, D, 2 * DFF]), ("ffn2_w_down", [L, DFF, D]),
]
SCRATCH_SPECS = [
    ("HX", [D, TOK], F32), ("XM", [D, TOK], BF16),
    ("GKE", [2, 128, GKEYS], BF16), ("GVE", [128, 20, 256], BF16),
    ("UE", [1024, HALF + 2], F32), ("UC", [1024, CTX + 2], F32),
    ("CKN", [1024, CTX], BF16), ("CKR", [64, CTX], BF16), ("CV", [8, 128, 2, 128], BF16),
]
XCH_SPECS = [
    ("XU", 1024, 2, F32), ("XG", 256, 256, BF16), ("XGV", 256, 256, BF16), ("XKR", 64, HALF, BF16),
    ("XKN0", 256, HALF, BF16), ("XKN1", 256, HALF, BF16), ("XKN2", 256, HALF, BF16), ("XKN3", 256, HALF, BF16),
    ("XV0", 256, HALF, BF16), ("XV1", 256, HALF, BF16), ("XV2", 256, HALF, BF16), ("XV3", 256, HALF, BF16),
]
PAIRS = [[0, 1], [2, 3], [4, 5], [6, 7]]
def build(mode="FUSED"):
    nc = bass.Bass("TRN2", target_bir_lowering=False)
    P = Prog(nc)
    dr = {}

    def din(name, shape, dt=F32):
        dr[name] = nc.dram_tensor(name, shape, dt, kind="ExternalInput").ap()

    wshape = dict(WEIGHT_SPECS)
    wkeys = []
    din("smalls", [128, NS])
    din("cosg", [128, TOK]); din("sing", [128, TOK])
    din("cosm", [64, TOK]); din("sinm", [64, TOK])
    din("masks", [128, 4 * 512], BF16)
    din("xT", [D, HALF]); din("cT", [D, CTX])
    for n, sh, dt in SCRATCH_SPECS:
        dr[n] = nc.dram_tensor(n, sh, dt).ap()
    for l in range(L):
        for n, r, c, dt in XCH_SPECS:
            dr[f"{n}i{l}"] = nc.dram_tensor(f"{n}i{l}", [r, c], dt).ap()
            dr[f"{n}o{l}"] = nc.dram_tensor(f"{n}o{l}", [2 * r, c], dt).ap()
    dr["outT"] = nc.dram_tensor("outT", [D, HALF], F32, kind="ExternalOutput").ap()
    DT = {k: T(v, k) for k, v in dr.items()}
    WT = T(None, "weights")

    ARENA_BYTES = 204 * 1024
    arena_ap = nc.alloc_sbuf_tensor("arena", [128, ARENA_BYTES // 4], F32).ap()
    A = Arena(P, arena_ap, ARENA_BYTES)
    smalls = A.tile(NS, F32, "smalls")
    modL = [A.tile(288, F32, "mod0"), A.tile(288, F32, "mod1")]
    curl = [0]

    def MOD():
        return modL[curl[0]]
    CA = A.tile(96, F32, "CA")
    CG = A.tile(96, F32, "CG")
    misc = A.tile(64, F32, "misc")
    scb = A.tile(32, BF16, "scb")
    ones = A.tile(128, BF16, "ones")
    masks = A.tile(4 * 512, BF16, "masks")
    sinkrow = A.tile(2 * 512, F32, "sinkrow")
    ring = [A.tile(SLOT, BF16, f"ring{i}") for i in range(NSLOT)]
    ringi = [0]
    hx, hx_all = A.chunks(16, 512, F32, "hx", shared_ds=True)
    HA = Arena(P, hx_all, 16 * 512 * 4)
    n_rstd = A.tile(512, F32, "n_rstd")
    n_tmp = Rot([A.tile(512, F32, f"n_tmp{i}") for i in range(3)])
    psum = [T(nc.alloc_psum_tensor(f"ps{i}", [128, 512], F32).ap(), f"ps{i}") for i in range(8)]
    ps_o, ps_d = psum[0], psum[1]
    psr = Rot(psum[2:])

    def sm(c0, c1=None):
        return smalls.ap[:, c0:(c0 + 1 if c1 is None else c1)]

    def wload(name, l, kc, c0, ncols, k0=0):
        slot = ring[ringi[0] % NSLOT]
        ringi[0] += 1
        view = slot.ap[:, 0:kc * ncols].rearrange("p (k n) -> p k n", k=kc)
        key = f"{name}@{l}"
        if key not in dr:
            dr[key] = nc.dram_tensor(f"{name}_L{l}", wshape[name][1:], F32, kind="ExternalInput").ap()
            wkeys.append((f"{name}_L{l}", name, l))
        srcv = dr[key].rearrange("(c p) n -> p c n", p=128)[:, k0:k0 + kc, c0:c0 + ncols]
        P.dma("gpsimd", slot, WT, view, srcv)
        return slot, view

    P.dma("sync", smalls, DT["smalls"], smalls.ap, dr["smalls"])
    P.dma("sync", masks, DT["masks"], masks.ap, dr["masks"])
    P.op("vector", lambda e: e.memset(ones.ap, 1.0), writes=[ones])

    ada_next = [0, 0]
    silu_done = [False]

    def ada_slabs(l, count, banks):
        b = l * LS
        if not silu_done[0]:
            P.op("scalar", lambda e: e.activation(scb.ap, sm(S_CVEC, S_CVEC + 32), ACT.Silu),
                 reads=[smalls], writes=[scb])
            silu_done[0] = True
        for _ in range(count):
            s = ada_next[l]
            if s >= 72:
                return
            ada_next[l] += 1
            pm = banks[s % len(banks)]
            slot, v = wload("ada_w", l, 16, s * 256, 256)
            for q in range(2):
                n = s * 2 + q
                P.mm(pm, [(pm.ap[:, 2 * n:2 * n + 2], v[:, j, q * 128:(q + 1) * 128], scb.ap[:, 2 * j:2 * j + 2])
                          for j in range(16)], reads=[slot, scb], start=True, stop=True, fresh=(q == 0))
            c0 = 4 * s
            P.op("vector", lambda e, pm=pm, c0=c0, l=l, b=b: e.tensor_tensor(
                modL[l].ap[:, c0:c0 + 4], pm.ap[:, c0:c0 + 4], sm(b + S_ADAB + c0, b + S_ADAB + c0 + 4), op=ALU.add),
                reads=[pm, smalls, modL[l]], writes=[modL[l]])

    def layer_setup(l):
        b = l * LS
        ada_slabs(l, 72, [psum[6], psum[7]])
        curl[0] = l
        mod = MOD()
        for k, sn in enumerate((S_N1, S_NM, S_N2)):
            for w in range(2):
                m_scale = mod.ap.rearrange("p (m j w) -> p m j w", m=9, j=16)[:, 3 * k + 1, :, w]
                outv = CA.ap.rearrange("p (k j w) -> p k j w", k=3, j=16)[:, k, :, w]
                P.op("vector", lambda e, o=outv, i=m_scale, sn=sn: e.scalar_tensor_tensor(
                    o, i, 1.0, sm(b + sn, b + sn + 16), op0=ALU.add, op1=ALU.mult),
                    reads=[mod, smalls], writes=[CA])
                m_gate = mod.ap.rearrange("p (m j w) -> p m j w", m=9, j=16)[:, 3 * k + 2, :, w]
                outg = CG.ap.rearrange("p (k j w) -> p k j w", k=3, j=16)[:, k, :, w]
                gs = 1.0 if k == 1 else 0.5
                P.op("vector", lambda e, o=outg, i=m_gate, gs=gs: e.tensor_scalar(
                    o, i, gs, None, op0=ALU.mult), reads=[mod], writes=[CG])
        P.op("vector", lambda e: e.tensor_scalar(CA.ap, CA.ap, float(np.sqrt(D)), None, op0=ALU.mult),
             reads=[CA], writes=[CA])
        P.op("vector", lambda e: e.tensor_scalar(misc.ap[:, 0:4], sm(b + S_QN, b + S_QN + 4), float(np.sqrt(512)), None,
                                                 op0=ALU.mult), reads=[smalls], writes=[misc])
        P.op("vector", lambda e: e.tensor_scalar(misc.ap[:, 4:6], sm(b + S_KVN, b + S_KVN + 2), 16.0, None,
                                                 op0=ALU.mult), reads=[smalls, misc], writes=[misc])
        P.op("vector", lambda e: e.tensor_scalar(misc.ap[:, 8:24], sm(S_FIN, S_FIN + 16), float(np.sqrt(D)), None,
                                                 op0=ALU.mult), reads=[smalls, misc], writes=[misc])
        P.op("scalar", lambda e: e.activation(misc.ap[:, 24:32], sm(b + S_SINK, b + S_SINK + 8), ACT.Exp),
             reads=[smalls, misc], writes=[misc])
        for h in range(8):
            P.op("vector", lambda e, h=h: e.tensor_scalar(
                sinkrow.ap[:, h * 128:(h + 1) * 128], ones.ap[:, 0:128], misc.ap[:, 24 + h:25 + h], None, op0=ALU.mult),
                reads=[ones, misc, sinkrow], writes=[sinkrow])

    def Acol(k, j, w):
        i = (k * 16 + j) * 2 + w
        return CA.ap[:, i:i + 1]

    def Gcol(k, j, w):
        i = (k * 16 + j) * 2 + w
        return CG.ap[:, i:i + 1]

    def Bcol(m, j, w):
        i = (m * 16 + j) * 2 + w
        return MOD().ap[:, i:i + 1]

    def rmsnorm(xs, N, epscol, a_fn, b_fn, outs, consts, sq_tiles=None):
        sqs = outs if sq_tiles is None else sq_tiles
        pss = psr.next()
        nchunk = len(xs)
        for j, x in enumerate(xs):
            sq = sqs[j]
            if j % 2 == 0:
                P.op("scalar", lambda e, sq=sq, x=x: e.activation(sq.ap[:, :N], x.ap[:, :N], ACT.Square),
                     reads=[x], writes=[sq])
            else:
                P.op("vector", lambda e, sq=sq, x=x: e.tensor_tensor(sq.ap[:, :N], x.ap[:, :N], x.ap[:, :N], op=ALU.mult),
                     reads=[x], writes=[sq])
        for j in range(nchunk):
            sq = sqs[j]
            P.mm(pss, [(pss.ap[:, :N], ones.ap, sq.ap[:, :N])], reads=[ones, sq], start=(j == 0), stop=(j == nchunk - 1))
        rstd = n_rstd
        P.op("scalar", lambda e: e.activation(rstd.ap[:, :N], pss.ap[:, :N], ACT.Sqrt, bias=sm(epscol), scale=1.0),
             reads=[pss, smalls], writes=[rstd])
        P.op("vector", lambda e: e.reciprocal(rstd.ap[:, :N], rstd.ap[:, :N]), reads=[rstd], writes=[rstd])
        for j, x in enumerate(xs):
            o = outs[j]
            aj = a_fn(j)
            if b_fn is None:
                P.op("vector", lambda e, o=o, x=x, aj=aj: e.scalar_tensor_tensor(
                    o.ap[:, :N], x.ap[:, :N], aj, rstd.ap[:, :N], op0=ALU.mult, op1=ALU.mult),
                    reads=[x, rstd] + consts, writes=[o])
            else:
                bj = b_fn(j)
                t = n_tmp.next()
                P.op("vector", lambda e, t=t, x=x, aj=aj: e.scalar_tensor_tensor(
                    t.ap[:, :N], x.ap[:, :N], aj, rstd.ap[:, :N], op0=ALU.mult, op1=ALU.mult),
                    reads=[x, rstd] + consts, writes=[t])
                P.op("scalar", lambda e, t=t, o=o, bj=bj: e.activation(
                    o.ap[:, :N], t.ap[:, :N], ACT.Identity, bias=bj, scale=1.0),
                    reads=[t] + consts, writes=[o])

    def ffn(l, which, xn, N, w):
        m = A.mark()
        k = 0 if which == 1 else 2
        gu = "ffn1_w_gu" if which == 1 else "ffn2_w_gu"
        dn = "ffn1_w_down" if which == 1 else "ffn2_w_down"
        h, _ = A.chunks(FCH, 512, BF16, "h")
        sgs = Rot([A.tile(512, F32, "sg") for _ in range(2)])
        gb, ub = psum[0:4], psum[4:8]
        for s in range(11):
            for banks, c0 in ((gb, 0), (ub, DFF)):
                for kh in range(2):
                    slot, v = wload(gu, l, 8, c0 + s * 512, 512, k0=kh * 8)
                    for q in range(4):
                        p = banks[q]
                        P.mm(p, [(p.ap[:, :N], v[:, j, q * 128:(q + 1) * 128], xn[kh * 8 + j].ap[:, :N]) for j in range(8)],
                             reads=[slot] + xn[kh * 8:(kh + 1) * 8], start=(kh == 0), stop=(kh == 1))
            for q in range(4):
                fc = s * 4 + q
                pg, pu = gb[q], ub[q]
                sg = sgs.next()
                P.op("scalar", lambda e, sg=sg, pg=pg: e.activation(sg.ap[:, :N], pg.ap[:, :N], ACT.Silu),
                     reads=[pg], writes=[sg])
                P.op("vector", lambda e, sg=sg, pu=pu, fc=fc: e.tensor_tensor(
                    h[fc].ap[:, :N], sg.ap[:, :N], pu.ap[:, :N], op=ALU.mult), reads=[sg, pu], writes=[h[fc]])
        for dg in range(4):
            banks = psum[0:4] if dg % 2 == 0 else psum[4:8]
            f0 = 0
            sizes = [8, 8, 8, 8, 8, 4]
            for si, nf in enumerate(sizes):
                slot, v = wload(dn, l, nf, dg * 512, 512, k0=f0)
                for q in range(4):
                    p = banks[q]
                    P.mm(p, [(p.ap[:, :N], v[:, f, q * 128:(q + 1) * 128], h[f0 + f].ap[:, :N]) for f in range(nf)],
                         reads=[slot] + h[f0:f0 + nf], start=(si == 0), stop=(si == len(sizes) - 1))
                f0 += nf
            for q in range(4):
                dc = dg * 4 + q
                p = banks[q]
                P.op("vector", lambda e, p=p, dc=dc: e.scalar_tensor_tensor(
                    hx[dc].ap[:, :N], p.ap[:, :N], Gcol(k, dc, w), hx[dc].ap[:, :N], op0=ALU.mult, op1=ALU.add),
                    reads=[p, hx[dc], CG], writes=[hx[dc]])
        A.release(m)

    def load_hx(srcname, t0, N):
        v = dr[srcname].rearrange("(c p) t -> p c t", p=128)[:, :, t0:t0 + N]
        P.dma("sync", hx, DT[srcname], hx_all.rearrange("p (c t) -> p c t", c=16)[:, :, 0:N], v)

    def store_hx(t0, N):
        v = dr["HX"].rearrange("(c p) t -> p c t", p=128)[:, :, t0:t0 + N]
        P.dma("sync", DT["HX"], hx, v, hx_all.rearrange("p (c t) -> p c t", c=16)[:, :, 0:N])

    def rope_combine(p1, p2, cs, sn, out_ap, np_, N, tmps, out_t, eng_add="vector"):
        t1 = tmps.next()
        t2 = tmps.next()
        P.op("vector", lambda e: e.tensor_tensor(t1.ap[0:np_, :N], p1.ap[0:np_, :N], cs.ap[0:np_, :N], op=ALU.mult),
             reads=[p1, cs], writes=[t1])
        P.op("vector", lambda e: e.tensor_tensor(t2.ap[0:np_, :N], p2.ap[0:np_, :N], sn.ap[0:np_, :N], op=ALU.mult),
             reads=[p2, sn], writes=[t2])
        P.op(eng_add, lambda e: e.tensor_tensor(out_ap, t1.ap[0:np_, :N], t2.ap[0:np_, :N], op=ALU.add),
             reads=[t1, t2], writes=[out_t])

    accs = [(psum[0], psum[1]), (psum[2], psum[3])]
    ring4 = Rot(psum[4:7])

    def attn_stream(groups, PT):
        flat = []
        for gi, g in enumerate(groups):
            n = len(g["items"])
            for i, it in enumerate(g["items"]):
                flat.append((gi, i == 0, i == n - 1, g, it))

        def issue_S(k):
            it = flat[k][4]
            p = ring4.next()
            P.mm(p, it["s_pairs"](p), reads=it["s_reads"]())
            return p
        LOOK = 2
        pend = [issue_S(k) for k in range(min(LOOK, len(flat)))]
        for k in range(len(flat)):
            gi, fi, la, g, it = flat[k]
            if fi and g.get("prefetch") is not None:
                g["prefetch"]()
            p = pend.pop(0)
            if k + LOOK < len(flat):
                pend.append(issue_S(k + LOOK))
            N = g["N"]
            po, pd = accs[gi % 2]
            pt = PT.next()
            P.op("scalar", lambda e, pt=pt, p=p, N=N, sc=g["scale"]: e.activation(
                pt.ap[:, :N], p.ap[:, :N], ACT.Exp, scale=sc), reads=[p], writes=[pt])
            if it["mask"] is not None:
                mk = it["mask"]
                P.op("vector", lambda e, pt=pt, mk=mk, N=N: e.tensor_tensor(
                    pt.ap[:, :N], pt.ap[:, :N], masks.ap[:, mk * 512:mk * 512 + N], op=ALU.mult), reads=[pt, masks], writes=[pt])
            P.mm(po, [(po.ap[:, :N], it["v_lhsT"](), pt.ap[:, :N])], reads=it["v_reads"]() + [pt], start=fi, stop=la)
            P.mm(pd, [(pd.ap[:, :N], ones.ap, pt.ap[:, :N])], reads=[ones, pt], start=fi, stop=la)
            if la:
                g["finish"](po, pd)
                if g.get("post") is not None:
                    g["post"]()

    def phase1_tile(l, ti):
        t0, N = TILES[ti]
        w = 1 if ti == 4 else 0
        mT = A.mark()
        xn, xn_all = A.chunks(16, 512, BF16, "xn")
        if l == 0:
            if w == 0:
                v = dr["xT"].rearrange("(c p) t -> p c t", p=128)[:, :, t0:t0 + N]
                P.dma("sync", hx, DT["xT"], hx_all.rearrange("p (c t) -> p c t", c=16)[:, :, 0:N], v)
            else:
                v = dr["cT"].rearrange("(c p) t -> p c t", p=128)
                P.dma("sync", hx, DT["cT"], hx_all.rearrange("p (c t) -> p c t", c=16)[:, :, 0:N], v)
        else:
            load_hx("HX", t0, N)
        rmsnorm(hx, N, S_EPS, lambda j: Acol(0, j, w), lambda j: Bcol(0, j, w), xn, [CA, MOD()])
        ffn(l, 1, xn, N, w)
        store_hx(t0, N)
        xm = xn
        rmsnorm(hx, N, S_EPS, lambda j: Acol(1, j, w), lambda j: Bcol(3, j, w), xm, [CA, MOD()])
        xmv = dr["XM"].rearrange("(c p) t -> p c t", p=128)[:, :, t0:t0 + N]
        P.dma("sync", DT["XM"], xm, xmv, xn_all.rearrange("p (c t) -> p c t", c=16)[:, :, 0:N])
        mJ = A.mark()
        stg = Rot([A.tile(512, BF16, "stg") for _ in range(3)])
        ft = Rot([A.tile(512, F32, "ft") for _ in range(4)])
        ust = Rot([A.tile(512, F32, "ust") for _ in range(2)])
        cosg = A.tile(512, F32, "cosg"); sing = A.tile(512, F32, "sing")
        cosm = A.tile(512, F32, "cosm"); sinm = A.tile(512, F32, "sinm")
        P.dma("sync", cosg, DT["cosg"], cosg.ap[:, :N], dr["cosg"][:, t0:t0 + N])
        P.dma("sync", sing, DT["sing"], sing.ap[:, :N], dr["sing"][:, t0:t0 + N])
        P.dma("sync", cosm, DT["cosm"], cosm.ap[0:64, :N], dr["cosm"][:, t0:t0 + N])
        P.dma("sync", sinm, DT["sinm"], sinm.ap[0:64, :N], dr["sinm"][:, t0:t0 + N])
        for s in range(4):
            gs, gv = wload("w_in", l, 16, C_GC + s * 256, 256)
            vs, vv = wload("w_in", l, 16, C_V + s * 256, 256)
            for q in range(2):
                jc = s * 2 + q
                pg = psr.next()
                P.mm(pg, [(pg.ap[:, :N], gv[:, j, q * 128:(q + 1) * 128], xm[j].ap[:, :N]) for j in range(16)], reads=[gs] + xm)
                pv = psr.next()
                P.mm(pv, [(pv.ap[:, :N], vv[:, j, q * 128:(q + 1) * 128], xm[j].ap[:, :N]) for j in range(16)], reads=[vs] + xm)
                t = ft.next()
                P.op("scalar", lambda e, t=t, pg=pg: e.activation(t.ap[:, :N], pg.ap[:, :N], ACT.Copy), reads=[pg], writes=[t])
                u = ust.next()
                P.op("vector", lambda e, u=u, t=t, pv=pv: e.tensor_tensor(u.ap[:, :N], t.ap[:, :N], pv.ap[:, :N], op=ALU.mult),
                     reads=[t, pv], writes=[u])
                if w == 0:
                    P.dma("sync", DT["UE"], u, dr["UE"][jc * 128:(jc + 1) * 128, 1 + t0:1 + t0 + N], u.ap[:, :N])
                    if ti == 0:
                        P.dma("sync", DT[f"XUi{l}"], u, dr[f"XUi{l}"][jc * 128:(jc + 1) * 128, 0:1], u.ap[:, 0:1], slow=True)
                    if ti == 3:
                        P.dma("sync", DT[f"XUi{l}"], u, dr[f"XUi{l}"][jc * 128:(jc + 1) * 128, 1:2], u.ap[:, N - 1:N], slow=True)
                else:
                    P.dma("sync", DT["UC"], u, dr["UC"][jc * 128:(jc + 1) * 128, 1:1 + N], u.ap[:, :N])
        ks, kv_ = wload("w_in", l, 16, C_CKV, 256)
        krs, krv = wload("w_in", l, 16, C_KR, 64)
        rs, rv = wload("winr", l, 16, 1280, 64)
        ckv, _ = A.chunks(2, 512, F32, "ckv")
        ckvn, _ = A.chunks(2, 512, BF16, "ckvn")
        for j2 in range(2):
            p = psr.next()
            P.mm(p, [(p.ap[:, :N], kv_[:, j, j2 * 128:(j2 + 1) * 128], xm[j].ap[:, :N]) for j in range(16)], reads=[ks] + xm)
            P.op("scalar", lambda e, p=p, j2=j2: e.activation(ckv[j2].ap[:, :N], p.ap[:, :N], ACT.Copy), reads=[p], writes=[ckv[j2]])
        rmsnorm(ckv, N, S_EPS + 2, lambda j: misc.ap[:, 4 + j:5 + j], None, ckvn, [misc])
        key0 = t0 if w == 0 else 0
        p1 = psr.next()
        P.mm(p1, [(p1.ap[0:64, :N], krv[:, j, 0:64], xm[j].ap[:, :N]) for j in range(16)], reads=[krs] + xm)
        p2 = psr.next()
        P.mm(p2, [(p2.ap[0:64, :N], rv[:, j, 0:64], xm[j].ap[:, :N]) for j in range(16)], reads=[rs] + xm)
        st = stg.next()
        rope_combine(p1, p2, cosm, sinm, st.ap[0:64, :N], 64, N, ft, st)
        krn = f"XKRi{l}" if w == 0 else "CKR"
        P.dma("sync", DT[krn], st, dr[krn][:, key0:key0 + N], st.ap[0:64, :N])
        kks, kkv = wload("wkvb_k", l, 2, 0, 1024)
        for h in range(8):
            p = psr.next()
            P.mm(p, [(p.ap[:, :N], kkv[:, j2, h * 128:(h + 1) * 128], ckvn[j2].ap[:, :N]) for j2 in range(2)], reads=[kks] + ckvn)
            st = stg.next()
            P.op("scalar", lambda e, st=st, p=p: e.activation(st.ap[:, :N], p.ap[:, :N], ACT.Copy), reads=[p], writes=[st])
            if w == 0:
                knn = f"XKN{h // 2}i{l}"
                P.dma("sync", DT[knn], st, dr[knn][(h % 2) * 128:(h % 2 + 1) * 128, key0:key0 + N], st.ap[:, :N])
            else:
                P.dma("sync", DT["CKN"], st, dr["CKN"][h * 128:(h + 1) * 128, key0:key0 + N], st.ap[:, :N])
        vvs, vvv = wload("wkvb_v", l, 2, 0, 1024)
        for tb in range(N // 128):
            c = key0 // 128 + tb
            for hh in range(2):
                p = psr.next()
                P.mm(p, [(p.ap[:, 0:512], ckvn[j2].ap[:, tb * 128:(tb + 1) * 128], vvv[:, j2, hh * 512:(hh + 1) * 512])
                         for j2 in range(2)], reads=[vvs] + ckvn)
                st = stg.next()
                P.op("vector", lambda e, st=st, p=p: e.tensor_copy(st.ap[:, 0:512], p.ap[:, 0:512]), reads=[p], writes=[st])
                if w == 0:
                    for q2 in range(2):
                        vn = f"XV{hh * 2 + q2}i{l}"
                        vdst = dr[vn].rearrange("(h p) (c d) -> h p c d", h=2, d=128)[:, :, c, :]
                        P.dma("sync", DT[vn], st, vdst.rearrange("h p d -> p h d"),
                              st.ap[:, q2 * 256:(q2 + 1) * 256].rearrange("p (h d) -> p h d", h=2))
                else:
                    vdst = dr["CV"][hh * 4:(hh + 1) * 4, :, c, :]
                    P.dma("sync", DT["CV"], st, vdst.rearrange("h p d -> p h d"),
                          st.ap[:, 0:512].rearrange("p (h d) -> p h d", h=4))
        gks, gkv = wload("w_in", l, 16, C_GK, 256)
        gvs, gvv = wload("w_in", l, 16, C_GV, 256)
        rs, rv = wload("winr", l, 16, 1024, 256)
        gkey0 = 128 + t0 if w == 0 else 128 + HALF + 128
        for g in range(2):
            p1 = psr.next()
            P.mm(p1, [(p1.ap[:, :N], gkv[:, j, g * 128:(g + 1) * 128], xm[j].ap[:, :N]) for j in range(16)], reads=[gks] + xm)
            p2 = psr.next()
            P.mm(p2, [(p2.ap[:, :N], rv[:, j, g * 128:(g + 1) * 128], xm[j].ap[:, :N]) for j in range(16)], reads=[rs] + xm)
            st = stg.next()
            rope_combine(p1, p2, cosg, sing, st.ap[:, :N], 128, N, ft, st)
            P.dma("sync", DT["GKE"], st, dr["GKE"][g, :, gkey0:gkey0 + N], st.ap[:, :N])
            if w == 0 and ti == 0:
                P.dma("sync", DT[f"XGi{l}"], st, dr[f"XGi{l}"][g * 128:(g + 1) * 128, 0:128], st.ap[:, 0:128])
            if w == 0 and ti == 3:
                P.dma("sync", DT[f"XGi{l}"], st, dr[f"XGi{l}"][g * 128:(g + 1) * 128, 128:256], st.ap[:, N - 128:N])
        for tb in range(N // 128):
            c = gkey0 // 128 + tb
            p = psr.next()
            P.mm(p, [(p.ap[:, 0:256], xm[j].ap[:, tb * 128:(tb + 1) * 128], gvv[:, j, 0:256]) for j in range(16)], reads=[gvs] + xm)
            st = stg.next()
            P.op("vector", lambda e, st=st, p=p: e.tensor_copy(st.ap[:, 0:256], p.ap[:, 0:256]), reads=[p], writes=[st])
            P.dma("sync", DT["GVE"], st, dr["GVE"][:, c, :], st.ap[:, 0:256])
            if w == 0 and ti == 0 and tb == 0:
                P.dma("sync", DT[f"XGVi{l}"], st, dr[f"XGVi{l}"][0:128, :], st.ap[:, 0:256])
            if w == 0 and ti == 3 and tb == N // 128 - 1:
                P.dma("sync", DT[f"XGVi{l}"], st, dr[f"XGVi{l}"][128:256, :], st.ap[:, 0:256])
        A.release(mJ)
        A.release(mT)

    def phase2_tile(l, ti, final):
        t0, N = TILES[ti]
        w = 1 if ti == 4 else 0
        mT = A.mark()
        xm, xm_all = A.chunks(16, 512, BF16, "xm", shared_ds=True)
        yc, _ = A.chunks(8, 512, BF16, "yc")
        ymla, _ = A.chunks(8, 512, BF16, "ymla")
        ygqa, ygqa_all = A.chunks(8, 512, BF16, "ygqa")
        P.dma("sync", xm, DT["XM"], xm_all.rearrange("p (c t) -> p c t", c=16)[:, :, 0:N], dr["XM"].rearrange("(c p) t -> p c t", p=128)[:, :, t0:t0 + N])
        m = A.mark()
        U, U_all = A.chunks(8, 516, F32, "U", shared_ds=True)
        if w == 0:
            uv = dr["UE"].rearrange("(c p) t -> p c t", p=128)[:, :, t0:t0 + N + 2]
        else:
            uv = dr["UC"].rearrange("(c p) t -> p c t", p=128)
        Uv = U_all.rearrange("p (c t) -> p c t", c=8)[:, :, 0:N + 2]
        P.dma("sync", U, DT["UE" if w == 0 else "UC"], Uv, uv)
        U3 = U_all.rearrange("p (c t) -> p c t", c=8)
        if w == 1:
            P.op("vector", lambda e: e.memset(U3[:, :, 0:1], 0.0), writes=U)
            P.op("vector", lambda e: e.memset(U3[:, :, N + 1:N + 2], 0.0), writes=U)
        else:
            xuo = dr[f"XUo{l}"].rearrange("(r c p) t -> r p c t", r=2, p=128)
            if t0 == 0:
                P.dma("sync", U, DT[f"XUo{l}"], U3[:, :, 0:1], xuo[0][:, :, 1:2], slow=True)
                P.op("vector", lambda e: e.tensor_scalar(U3[:, :, 0:1], U3[:, :, 0:1], sm(S_FLAG), None, op0=ALU.mult),
                     reads=[smalls] + U, writes=U)
            if t0 + N == HALF:
                P.dma("sync", U, DT[f"XUo{l}"], U3[:, :, N + 1:N + 2], xuo[1][:, :, 0:1], slow=True)
                P.op("vector", lambda e: e.tensor_scalar(U3[:, :, N + 1:N + 2], U3[:, :, N + 1:N + 2], sm(S_FLAG + 1), None, op0=ALU.mult),
                     reads=[smalls] + U, writes=U)
        cy = Rot([A.tile(512, F32, "cy") for _ in range(2)])
        cb = l * LS + S_CONV
        for s in range(4):
            gs, gv = wload("w_in", l, 16, C_GB + s * 256, 256)
            for q in range(2):
                jc = s * 2 + q
                pg = psr.next()
                P.mm(pg, [(pg.ap[:, :N], gv[:, j, q * 128:(q + 1) * 128], xm[j].ap[:, :N]) for j in range(16)], reads=[gs] + xm)
                y = cy.next()
                u = U[jc]
                P.op("vector", lambda e, y=y, u=u, jc=jc: e.tensor_scalar(
                    y.ap[:, :N], u.ap[:, 0:N], sm(cb + jc), None, op0=ALU.mult), reads=[u, smalls], writes=[y])
                P.op("vector", lambda e, y=y, u=u, jc=jc: e.scalar_tensor_tensor(
                    y.ap[:, :N], u.ap[:, 1:N + 1], sm(cb + 8 + jc), y.ap[:, :N], op0=ALU.mult, op1=ALU.add),
                    reads=[u, smalls, y], writes=[y])
                P.op("vector", lambda e, y=y, u=u, jc=jc: e.scalar_tensor_tensor(
                    y.ap[:, :N], u.ap[:, 2:N + 2], sm(cb + 16 + jc), y.ap[:, :N], op0=ALU.mult, op1=ALU.add),
                    reads=[u, smalls, y], writes=[y])
                P.op("vector", lambda e, y=y, pg=pg, jc=jc: e.tensor_tensor(
                    yc[jc].ap[:, :N], y.ap[:, :N], pg.ap[:, :N], op=ALU.mult), reads=[y, pg], writes=[yc[jc]])
        A.release(m)
        m = A.mark()
        P.barrier(hx)
        mh = HA.mark()
        qn, _ = HA.chunks(8, 512, BF16, "qn")
        qr, _ = HA.chunks(8, 512, BF16, "qr")
        m2 = A.mark()
        cq, _ = A.chunks(4, 512, F32, "cq")
        cqn, _ = A.chunks(4, 512, BF16, "cqn")
        cosm = A.tile(512, F32, "cosm"); sinm = A.tile(512, F32, "sinm")
        ft = Rot([A.tile(512, F32, "ft") for _ in range(4)])
        P.dma("sync", cosm, DT["cosm"], cosm.ap[0:64, :N], dr["cosm"][:, t0:t0 + N])
        P.dma("sync", sinm, DT["sinm"], sinm.ap[0:64, :N], dr["sinm"][:, t0:t0 + N])
        for c4 in range(4):
            if c4 % 2 == 0:
                qs, qv = wload("w_in", l, 16, C_CQ + c4 * 128, 256)
            p = psr.next()
            P.mm(p, [(p.ap[:, :N], qv[:, j, (c4 % 2) * 128:(c4 % 2 + 1) * 128], xm[j].ap[:, :N]) for j in range(16)], reads=[qs] + xm)
            P.op("scalar", lambda e, p=p, c4=c4: e.activation(cq[c4].ap[:, :N], p.ap[:, :N], ACT.Copy), reads=[p], writes=[cq[c4]])
        rmsnorm(cq, N, S_EPS + 1, lambda j: misc.ap[:, j:j + 1], None, cqn, [misc])
        ns_, nv = wload("wqb_n", l, 4, 0, 1024)
        r1s, r1v = wload("wqb_r", l, 4, 0, 512)
        r2s, r2v = wload("wqb_rr", l, 4, 0, 512)
        for h in range(8):
            p = psr.next()
            P.mm(p, [(p.ap[:, :N], nv[:, c4, h * 128:(h + 1) * 128], cqn[c4].ap[:, :N]) for c4 in range(4)], reads=[ns_] + cqn)
            P.op("scalar", lambda e, p=p, h=h: e.activation(qn[h].ap[:, :N], p.ap[:, :N], ACT.Copy), reads=[p], writes=[qn[h]])
            p1 = psr.next()
            P.mm(p1, [(p1.ap[0:64, :N], r1v[:, c4, h * 64:(h + 1) * 64], cqn[c4].ap[:, :N]) for c4 in range(4)], reads=[r1s] + cqn)
            p2 = psr.next()
            P.mm(p2, [(p2.ap[0:64, :N], r2v[:, c4, h * 64:(h + 1) * 64], cqn[c4].ap[:, :N]) for c4 in range(4)], reads=[r2s] + cqn)
            P.op("vector", lambda e, h=h: e.memset(qr[h].ap[64:128, :N], 0.0), writes=[qr[h]])
            rope_combine(p1, p2, cosm, sinm, qr[h].ap[0:64, :N], 64, N, ft, qr[h])
        A.release(m2)
        KN = Rot([A.tile(NKEY, BF16, f"KN{i}") for i in range(2)])
        VV = Rot([A.tile(NKEY, BF16, f"VV{i}") for i in range(2)])
        KR = HA.tile(NKEY, BF16, "KR")
        PT = Rot([A.tile(512, BF16, "PT") for _ in range(3)])
        rden = A.tile(512, F32, "rden")
        kchunks = list(range(NKC)) if w == 0 else [NKC - 2, NKC - 1]
        kc0 = kchunks[0] * 128
        nk = len(kchunks) * 128
        P.op("vector", lambda e: e.memset(KR.ap[64:128, :], 0.0), writes=[KR])
        if w == 0:
            for r in range(2):
                P.dma("sync", KR, DT[f"XKRo{l}"], KR.ap[0:64, r * HALF:(r + 1) * HALF], dr[f"XKRo{l}"][r * 64:(r + 1) * 64, :])
        P.dma("sync", KR, DT["CKR"], KR.ap[0:64, 2 * HALF:2 * HALF + CTX], dr["CKR"])
        kns = [None] * 8
        vvs_ = [None] * 8

        def load_head(h):
            kn = KN.next()
            vv = VV.next()
            if w == 0:
                for r in range(2):
                    r0 = r * 256 + (h % 2) * 128
                    P.dma("sync", kn, DT[f"XKN{h // 2}o{l}"], kn.ap[:, r * HALF:(r + 1) * HALF],
                          dr[f"XKN{h // 2}o{l}"][r0:r0 + 128, :])
                    P.dma("sync", vv, DT[f"XV{h // 2}o{l}"], vv.ap[:, r * HALF:(r + 1) * HALF],
                          dr[f"XV{h // 2}o{l}"][r0:r0 + 128, :])
            P.dma("sync", kn, DT["CKN"], kn.ap[:, 2 * HALF:2 * HALF + CTX], dr["CKN"][h * 128:(h + 1) * 128, :])
            P.dma("sync", vv, DT["CV"], vv.ap[:, 2 * HALF:2 * HALF + CTX].rearrange("p (c d) -> p c d", d=128), dr["CV"][h])
            kns[h], vvs_[h] = kn, vv

        def mk_finish(h):
            def fin(po, pd):
                P.op("vector", lambda e: e.reciprocal(rden.ap[:, :N], pd.ap[:, :N]), reads=[pd], writes=[rden])
                P.op("vector", lambda e: e.tensor_tensor(ymla[h].ap[:, :N], po.ap[:, :N], rden.ap[:, :N], op=ALU.mult),
                     reads=[po, rden], writes=[ymla[h]])
            return fin

        def mk_item(h, c):
            def s_pairs(p):
                return [(p.ap[:, :N], kns[h].ap[:, c * 128:(c + 1) * 128], qn[h].ap[:, :N]),
                        (p.ap[:, :N], KR.ap[:, c * 128:(c + 1) * 128], qr[h].ap[:, :N])]
            return dict(s_pairs=s_pairs, s_reads=lambda: [kns[h], KR, qn[h], qr[h]],
                        v_lhsT=lambda: vvs_[h].ap[:, c * 128:(c + 1) * 128], v_reads=lambda: [vvs_[h]], mask=None)

        load_head(0)
        groups = []
        for h in range(8):
            groups.append(dict(N=N, scale=float(MLA_SCALE), items=[mk_item(h, c) for c in kchunks], finish=mk_finish(h),
                               prefetch=(lambda h=h: load_head(h + 1)) if h < 7 else None,
                               post=(lambda: ada_slabs(l + 1, 2, [psum[7]])) if l + 1 < L else None))
        attn_stream(groups, PT)
        HA.release(mh)
        A.release(m)
        m = A.mark()
        NB = N // 128
        qg = A.tile(8 * 512, BF16, "qg")
        cosg = A.tile(512, F32, "cosg"); sing = A.tile(512, F32, "sing")
        ft = Rot([A.tile(512, F32, "ft") for _ in range(4)])
        GK = A.tile(2 * GKEYS, BF16, "GK")
        GV = A.tile(20 * 256, BF16, "GV")
        PT = Rot([A.tile(512, BF16, "PTg") for _ in range(3)])
        dent = A.tile(512, F32, "dent")
        P.dma("sync", cosg, DT["cosg"], cosg.ap[:, :N], dr["cosg"][:, t0:t0 + N])
        P.dma("sync", sing, DT["sing"], sing.ap[:, :N], dr["sing"][:, t0:t0 + N])
        GK3 = GK.ap.rearrange("p (g k) -> p g k", g=2)
        GV3 = GV.ap.rearrange("p (c d) -> p c d", c=20)
        P.dma("sync", GK, DT["GKE"], GK3[:, :, 128:GKEYS], dr["GKE"].rearrange("g p k -> p g k")[:, :, 128:GKEYS])
        P.dma("sync", GV, DT["GVE"], GV3[:, 1:20, :], dr["GVE"][:, 1:20, :])
        if w == 0:
            xgo = dr[f"XGo{l}"].rearrange("(r g p) k -> r p g k", r=2, g=2)
            P.dma("sync", GK, DT[f"XGo{l}"], GK3[:, :, 0:128], xgo[0][:, :, 128:256])
            P.dma("sync", GK, DT[f"XGo{l}"], GK3[:, :, 128 + HALF:128 + HALF + 128], xgo[1][:, :, 0:128])
            P.dma("sync", GV, DT[f"XGVo{l}"], GV.ap[:, 0:256], dr[f"XGVo{l}"][128:256, :])
            P.dma("sync", GV, DT[f"XGVo{l}"], GV.ap[:, 17 * 256:18 * 256], dr[f"XGVo{l}"][256:384, :])
        for s in range(4):
            q1s, q1v = wload("w_in", l, 16, C_GQ + s * 256, 256)
            q2s, q2v = wload("winr", l, 16, s * 256, 256)
            for q in range(2):
                h = s * 2 + q
                p1 = psr.next()
                P.mm(p1, [(p1.ap[:, :N], q1v[:, j, q * 128:(q + 1) * 128], xm[j].ap[:, :N]) for j in range(16)], reads=[q1s] + xm)
                p2 = psr.next()
                P.mm(p2, [(p2.ap[:, :N], q2v[:, j, q * 128:(q + 1) * 128], xm[j].ap[:, :N]) for j in range(16)], reads=[q2s] + xm)
                outv = qg.ap.rearrange("p (b h i) -> p b h i", b=4, h=8)[:, 0:NB, h, :]
                t1 = ft.next(); t2 = ft.next()
                P.op("vector", lambda e, t1=t1, p1=p1: e.tensor_tensor(t1.ap[:, :N], p1.ap[:, :N], cosg.ap[:, :N], op=ALU.mult),
                     reads=[p1, cosg], writes=[t1])
                P.op("vector", lambda e, t2=t2, p2=p2: e.tensor_tensor(t2.ap[:, :N], p2.ap[:, :N], sing.ap[:, :N], op=ALU.mult),
                     reads=[p2, sing], writes=[t2])
                P.op("vector", lambda e, t1=t1, t2=t2, outv=outv: e.tensor_tensor(
                    outv, t1.ap[:, :N].rearrange("p (b i) -> p b i", i=128), t2.ap[:, :N].rearrange("p (b i) -> p b i", i=128),
                    op=ALU.add), reads=[t1, t2], writes=[qg])
        groups = []
        for qb in range(NB):
            nb = t0 // 128 + qb
            if w == 0:
                chunks = [(nb, 2 if nb == 0 else 0), (nb + 1, None), (nb + 2, 3 if nb == 15 else 1), (18, None), (19, None)]
            else:
                chunks = [(18, None), (19, None)]
            for g in range(2):
                rhs = qg.ap[:, qb * 1024 + g * 512: qb * 1024 + (g + 1) * 512]

                def mk_item(c, mk, g=g, rhs=rhs):
                    return dict(s_pairs=lambda p: [(p.ap, GK.ap[:, g * GKEYS + c * 128: g * GKEYS + (c + 1) * 128], rhs)],
                                s_reads=lambda: [GK, qg],
                                v_lhsT=lambda: GV.ap[:, c * 256 + g * 128: c * 256 + (g + 1) * 128],
                                v_reads=lambda: [GV], mask=mk)

                def fin(po, pd, g=g, qb=qb):
                    P.op("vector", lambda e: e.tensor_tensor(dent.ap, pd.ap, sinkrow.ap[:, g * 512:(g + 1) * 512], op=ALU.add),
                         reads=[pd, sinkrow], writes=[dent])
                    P.op("vector", lambda e: e.reciprocal(dent.ap, dent.ap), reads=[dent], writes=[dent])
                    outv = ygqa_all.rearrange("p (h t) -> p h t", h=8)[:, g * 4:(g + 1) * 4, qb * 128:(qb + 1) * 128]
                    P.op("vector", lambda e: e.tensor_tensor(
                        outv, po.ap.rearrange("p (h i) -> p h i", h=4), dent.ap.rearrange("p (h i) -> p h i", h=4), op=ALU.mult),
                        reads=[po, dent], writes=ygqa)
                groups.append(dict(N=512, scale=float(GQA_SCALE), items=[mk_item(c, mk) for c, mk in chunks], finish=fin, prefetch=None))
        attn_stream(groups, PT)
        A.release(m)
        m = A.mark()
        macc, _ = A.chunks(16, 512, F32, "macc")
        merged = yc + ymla
        sig = Rot([A.tile(512, F32, "sig") for _ in range(2)])
        tt = Rot([A.tile(512, F32, "tt") for _ in range(2)])
        for b, (bw, ys) in enumerate((("w_branch_conv", yc), ("w_branch_mla", ymla), ("w_branch_gqa", ygqa))):
            for dg in range(8):
                gs, gv = wload("w_in", l, 16, C_GATE + b * D + dg * 256, 256)
                bs, bv = wload(bw, l, 8, dg * 256, 256)
                for q in range(2):
                    dc = dg * 2 + q
                    pg = psr.next()
                    P.mm(pg, [(pg.ap[:, :N], gv[:, j, q * 128:(q + 1) * 128], xm[j].ap[:, :N]) for j in range(16)], reads=[gs] + xm)
                    pb = psr.next()
                    P.mm(pb, [(pb.ap[:, :N], bv[:, j, q * 128:(q + 1) * 128], ys[j].ap[:, :N]) for j in range(8)], reads=[bs] + ys)
                    sg = sig.next()
                    P.op("scalar", lambda e, sg=sg, pg=pg: e.activation(sg.ap[:, :N], pg.ap[:, :N], ACT.Sigmoid), reads=[pg], writes=[sg])
                    if b == 0:
                        P.op("vector", lambda e, sg=sg, pb=pb, dc=dc: e.tensor_tensor(
                            macc[dc].ap[:, :N], sg.ap[:, :N], pb.ap[:, :N], op=ALU.mult), reads=[sg, pb], writes=[macc[dc]])
                    else:
                        t = tt.next()
                        P.op("vector", lambda e, sg=sg, pb=pb, t=t: e.tensor_tensor(
                            t.ap[:, :N], sg.ap[:, :N], pb.ap[:, :N], op=ALU.mult), reads=[sg, pb], writes=[t])
                        o = macc[dc] if b == 1 else merged[dc]
                        P.op("vector", lambda e, t=t, o=o, dc=dc: e.tensor_tensor(
                            o.ap[:, :N], macc[dc].ap[:, :N], t.ap[:, :N], op=ALU.add), reads=[t, macc[dc]], writes=[o])
        load_hx_from(dr["HX"], DT["HX"], t0, N)
        for s in range(8):
            ws, wv = wload("w_out", l, 16, s * 256, 256)
            for q in range(2):
                dc = s * 2 + q
                p = psr.next()
                P.mm(p, [(p.ap[:, :N], wv[:, j, q * 128:(q + 1) * 128], merged[j].ap[:, :N]) for j in range(16)], reads=[ws] + merged)
                P.op("vector", lambda e, p=p, dc=dc: e.scalar_tensor_tensor(
                    hx[dc].ap[:, :N], p.ap[:, :N], Gcol(1, dc, w), hx[dc].ap[:, :N], op0=ALU.mult, op1=ALU.add),
                    reads=[p, hx[dc], CG], writes=[hx[dc]])
        A.release(m)
        A.release(mT)
        mT = A.mark()
        xn, _ = A.chunks(16, 512, BF16, "xn2")
        rmsnorm(hx, N, S_EPS, lambda j: Acol(2, j, w), lambda j: Bcol(6, j, w), xn, [CA, MOD()])
        ffn(l, 2, xn, N, w)
        if final:
            outs, outs_all = A.chunks(16, 512, F32, "fin")
            rmsnorm(hx, N, S_EPS, lambda j: misc.ap[:, 8 + j:9 + j], None, outs, [misc], sq_tiles=xn)
            ov = dr["outT"].rearrange("(c p) t -> p c t", p=128)[:, :, t0:t0 + N]
            P.dma("sync", DT["outT"], outs, ov, outs_all.rearrange("p (c t) -> p c t", c=16)[:, :, 0:N])
        else:
            store_hx(t0, N)
        A.release(mT)

    def DT_S(n):
        return DT[n + "_in"] if (n + "_in") in DT else DT[n]

    def load_hx_from(ap, t, t0, N):
        v = ap.rearrange("(c p) t -> p c t", p=128)[:, :, t0:t0 + N]
        P.dma("sync", hx, t, hx_all.rearrange("p (c t) -> p c t", c=16)[:, :, 0:N], v)

    for l in range(L):
        layer_setup(l)
        for ti in range(4):
            phase1_tile(l, ti)
        for n, r, c, dt in XCH_SPECS:
            P.cc(DT[f"{n}o{l}"], DT[f"{n}i{l}"], dr[f"{n}o{l}"], dr[f"{n}i{l}"], PAIRS)
        phase1_tile(l, 4)
        for ti in range(5 if l == 0 else 4):
            phase2_tile(l, ti, final=(l == L - 1))
    outs_t = [DT["outT"]]
    P.wait_all("sync", outs_t)
    for X in COMPUTE:
        P.wait_all(X, outs_t)
    P.emit()
    nc._wkeys = list(wkeys)
    build.last = (P.n_inst, len(P.dval), {e: len(P.waited[e]) for e in COMPUTE})
    return nc


def _rope_tables(half):
    def tab(dim_half, nfreq_dim):
        inv = (10000.0 ** (-np.arange(0, dim_half, 2, dtype=np.float32) / np.float32(dim_half))).astype(np.float32)
        return inv
    pos = np.arange(half * HALF, (half + 1) * HALF)
    row = (pos // 64).astype(np.float32)
    col = (pos % 64).astype(np.float32)

    def make(hd):
        inv = tab(hd, None)
        nf = hd // 2
        ar = row[None, :] * inv[:, None]
        ac = col[None, :] * inv[:, None]
        cr, sr, cc, sc = np.cos(ar), np.sin(ar), np.cos(ac), np.sin(ac)
        cos = np.concatenate([cr, cr, cc, cc], 0)
        sin = np.concatenate([-sr, sr, -sc, sc], 0)
        cosf = np.ones((2 * hd, TOK), np.float32)
        sinf = np.zeros((2 * hd, TOK), np.float32)
        cosf[:, :HALF] = cos
        sinf[:, :HALF] = sin
        return cosf, sinf
    cg, sg = make(64)
    cm, sm_ = make(32)
    return cg, sg, cm, sm_


def _fm(v):
    return np.ascontiguousarray(v.reshape(-1, 128).T)


def _prep_static(inp):
    f = lambda a: np.ascontiguousarray(np.asarray(a, dtype=np.float32))
    w_in = f(inp["w_in"])
    pg = np.concatenate([np.arange(32, 64), np.arange(0, 32), np.arange(96, 128), np.arange(64, 96)])
    pm = np.concatenate([np.arange(16, 32), np.arange(0, 16), np.arange(48, 64), np.arange(32, 48)])
    cols = []
    for h in range(8):
        cols.append(C_GQ + h * 128 + pg)
    for g in range(2):
        cols.append(C_GK + g * 128 + pg)
    cols.append(C_KR + pm)
    cols = np.concatenate(cols)
    wqb = f(inp["mla_w_qb"])
    wkvb = f(inp["mla_w_kvb"])
    cn = np.concatenate([h * 192 + np.arange(128) for h in range(8)])
    cr = np.concatenate([h * 192 + 128 + np.arange(64) for h in range(8)])
    crr = np.concatenate([h * 192 + 128 + pm for h in range(8)])
    ck = np.concatenate([h * 256 + np.arange(128) for h in range(8)])
    cv = np.concatenate([h * 256 + 128 + np.arange(128) for h in range(8)])
    W = {
        "ada_w": f(inp["ada_w"]), "ffn1_w_gu": f(inp["ffn1_w_gu"]), "ffn1_w_down": f(inp["ffn1_w_down"]),
        "w_in": w_in, "winr": np.ascontiguousarray(w_in[:, :, cols]),
        "wqb_n": np.ascontiguousarray(wqb[:, :, cn]), "wqb_r": np.ascontiguousarray(wqb[:, :, cr]),
        "wqb_rr": np.ascontiguousarray(wqb[:, :, crr]),
        "wkvb_k": np.ascontiguousarray(wkvb[:, :, ck]), "wkvb_v": np.ascontiguousarray(wkvb[:, :, cv]),
        "w_branch_conv": f(inp["w_branch_conv"]), "w_branch_mla": f(inp["w_branch_mla"]),
        "w_branch_gqa": f(inp["w_branch_gqa"]), "w_out": f(inp["w_out"]),
        "ffn2_w_gu": f(inp["ffn2_w_gu"]), "ffn2_w_down": f(inp["ffn2_w_down"]),
    }
    return W


def _smalls(inp, b, half):
    s = np.zeros((128, NS), np.float32)
    for l in range(L):
        o = l * LS
        ab = _fm(np.asarray(inp["ada_b"][l], np.float32))
        s[:, o + S_ADAB:o + S_ADAB + 288] = np.repeat(ab, 2, axis=1)
        s[:, o + S_N1:o + S_N1 + 16] = _fm(np.asarray(inp["ffn1_norm"][l], np.float32))
        s[:, o + S_NM:o + S_NM + 16] = _fm(np.asarray(inp["mix_norm"][l], np.float32))
        s[:, o + S_N2:o + S_N2 + 16] = _fm(np.asarray(inp["ffn2_norm"][l], np.float32))
        s[:, o + S_QN:o + S_QN + 4] = _fm(np.asarray(inp["mla_q_norm"][l], np.float32))
        s[:, o + S_KVN:o + S_KVN + 2] = _fm(np.asarray(inp["mla_kv_norm"][l], np.float32))
        cw = np.asarray(inp["conv_w"][l], np.float32)
        for k in range(3):
            s[:, o + S_CONV + k * 8:o + S_CONV + (k + 1) * 8] = _fm(cw[k])
        s[:, o + S_SINK:o + S_SINK + 8] = np.asarray(inp["gqa_sink"][l], np.float32)[None, :]
    s[:, S_FIN:S_FIN + 16] = _fm(np.asarray(inp["final_norm"], np.float32))
    cv = np.stack([_fm(np.asarray(inp["c"][b], np.float32)), _fm(np.asarray(inp["c_ctx"], np.float32))], axis=2)
    s[:, S_CVEC:S_CVEC + 32] = cv.reshape(128, 32)
    s[:, S_FLAG] = 0.0 if half == 0 else 1.0
    s[:, S_FLAG + 1] = 1.0 if half == 0 else 0.0
    s[:, S_EPS] = EPS * 2048
    s[:, S_EPS + 1] = EPS * 512
    s[:, S_EPS + 2] = EPS * 256
    return s


def _masks(half):
    j = np.arange(128)[:, None]
    i = np.arange(128)[None, :]
    prev = (j >= i).astype(np.float32)
    nxt = (j <= i).astype(np.float32)
    first = prev if half == 1 else np.zeros_like(prev)
    last = nxt if half == 0 else np.zeros_like(nxt)
    m = np.concatenate([np.tile(x, (1, 4)) for x in (prev, nxt, first, last)], axis=1)
    return m.astype(ml_dtypes.bfloat16)


_NC_CACHE = {}


def _get_nc():
    if "nc" not in _NC_CACHE:
        _NC_CACHE["nc"] = build()
    return _NC_CACHE["nc"]


def kernel(**inp):
    W = _prep_static(inp)
    x = np.asarray(inp["x"], np.float32)
    ctx = np.asarray(inp["ctx"], np.float32)
    nc = _get_nc()
    ws = {k: W[n][l] for k, n, l in nc._wkeys}
    ins = []
    for c in range(8):
        b, half = c // 2, c % 2
        cg, sg, cm, sm_ = _rope_tables(half)
        d = dict(ws)
        d["smalls"] = _smalls(inp, b, half)
        d["cosg"], d["sing"], d["cosm"], d["sinm"] = cg, sg, cm, sm_
        d["masks"] = _masks(half)
        d["xT"] = np.ascontiguousarray(x[b, half * HALF:(half + 1) * HALF, :].T)
        d["cT"] = np.ascontiguousarray(ctx[b].T)
        ins.append(d)
    r = run_bass_kernel_spmd(nc, ins, core_ids=list(range(8))).results
    out = np.zeros((4, SEQ, D), np.float32)
    for c in range(8):
        b, half = c // 2, c % 2
        out[b, half * HALF:(half + 1) * HALF, :] = np.asarray(r[c]["outT"]).T
    return out
```

```python
import numpy as np
import ml_dtypes
import concourse.bass as bass
import concourse.mybir as mybir
from concourse.bass_utils import run_bass_kernel_spmd

F32 = mybir.dt.float32
BF16 = mybir.dt.bfloat16
ACT = mybir.ActivationFunctionType
ALU = mybir.AluOpType

COMPUTE = ("tensor", "vector", "scalar", "gpsimd")
ALLENG = ("tensor", "vector", "scalar", "gpsimd", "sync")

D = 2048
NCH = 16
DFF = 5632
FCH = 44
L = 2
SEQ = 4096
HALF = 2048
CTX = 256
TOK = HALF + CTX
EPS = 1e-6
MLA_SCALE = 192 ** -0.5
GQA_SCALE = 128 ** -0.5
C_GB, C_GC, C_V = 0, 1024, 2048
C_CQ, C_CKV, C_KR = 3072, 3584, 3840
C_GQ, C_GK, C_GV = 3904, 4928, 5184
C_GATE = 5440
NKEY = 2 * HALF + CTX
NKC = NKEY // 128
GKEYS = 128 + HALF + 128 + CTX
TILES = [(0, 512), (512, 512), (1024, 512), (1536, 512), (2048, 256)]
SLOT = 4096
NSLOT = 8
LS = 374
S_ADAB, S_N1, S_NM, S_N2, S_QN, S_KVN, S_CONV, S_SINK = 0, 288, 304, 320, 336, 340, 342, 366
S_FIN = 2 * LS
S_CVEC = S_FIN + 16
S_FLAG = S_CVEC + 32
S_EPS = S_FLAG + 2
NS = S_EPS + 3


class T:
    __slots__ = ("ap", "w", "r", "ds", "name")

    def __init__(self, ap, name="", ds=None):
        self.ap = ap
        self.w = None
        self.r = {}
        self.ds = ds
        self.name = name


class Prog:
    def __init__(self, nc):
        self.nc = nc
        self.ops = {e: [] for e in ALLENG}
        self.cnt = {e: 0 for e in COMPUTE}
        self.known = {e: {} for e in ALLENG}
        self.waited = {e: set() for e in COMPUTE}
        self.dval = []
        self.free_ds = []
        self.n_inst = 0

    def new_ds(self):
        if self.free_ds:
            return self.free_ds.pop()
        self.dval.append(0)
        return len(self.dval) - 1

    def _emit_waits(self, X, evs):
        k = self.known[X]
        for ev in evs:
            if ev is None:
                continue
            if ev[0] == 'e':
                if ev[1] == X and X == "tensor":
                    continue
                key, val = ev[1], ev[2]
            else:
                key, val = ev[1] + 1000, self.dval[ev[1]]
            if k.get(key, 0) >= val:
                continue
            k[key] = val
            if ev[0] == 'e':
                self.waited[ev[1]].add(val)
                self.ops[X].append(('w', ev[1], val))
            else:
                self.ops[X].append(('wd', ev[1], val))

    @staticmethod
    def _deps(reads, writes):
        evs = []
        for t in reads:
            evs.append(t.w)
        for t in writes:
            evs.append(t.w)
            evs.extend(t.r.values())
        return evs

    @staticmethod
    def _mark(reads, writes, ev):
        key = ev[1] if ev[0] == 'e' else ev[1] + 1000
        for t in reads:
            t.r[key] = ev
        for t in writes:
            t.w = ev
            t.r = {}

    def op(self, X, fn, reads=(), writes=()):
        self._emit_waits(X, self._deps(reads, writes))
        self.cnt[X] += 1
        seq = self.cnt[X]
        ev = ('e', X, seq)
        self.ops[X].append(('i', fn, seq))
        self._mark(reads, writes, ev)
        self.n_inst += 1
        return ev

    def mm(self, ps, pairs, reads, start=True, stop=True, fresh=None):
        X = "tensor"
        if fresh is None:
            fresh = start
        self._emit_waits(X, self._deps(reads, [ps] if fresh else []))
        n = len(pairs)
        ev = None
        for i, (o, l, r) in enumerate(pairs):
            self.cnt[X] += 1
            seq = self.cnt[X]
            st = start and i == 0
            sp = stop and i == n - 1
            self.ops[X].append(('i', (lambda e, o=o, l=l, r=r, st=st, sp=sp:
                                      e.matmul(o, l, r, start=st, stop=sp)), seq))
            ev = ('e', X, seq)
        self.n_inst += n
        self._mark(reads, [ps], ev)
        return ev

    def dma(self, Q, dst, src, out_ap, in_ap, slow=False):
        dsts = dst if isinstance(dst, (list, tuple)) else [dst]
        srcs = src if isinstance(src, (list, tuple)) else [src]
        self._emit_waits(Q, self._deps(srcs, dsts))
        d0 = dsts[0]
        if d0.ds is None:
            d0.ds = self.new_ds()
        ds = d0.ds
        self.dval[ds] += 16
        ev = ('d', ds, self.dval[ds])
        self.ops[Q].append(('dma', out_ap, in_ap, ds, slow))
        self._mark(srcs, dsts, ev)
        self.n_inst += 1
        return ev

    def cc(self, dst, src, out_ap, in_ap, groups):
        Q = "gpsimd"
        self._emit_waits(Q, self._deps([src], [dst]))
        self.dval.append(1)
        ds = len(self.dval) - 1
        ev = ('d', ds, 1)
        self.ops[Q].append(('cc', out_ap, in_ap, ds, groups))
        self._mark([src], [dst], ev)
        self.n_inst += 1
        return ev

    def barrier(self, tiles):
        evs = []
        for t in tiles:
            evs.append(t.w)
            evs.extend(t.r.values())
        for X in ALLENG:
            if X == "gpsimd":
                continue
            self._emit_waits(X, evs)

    def wait_all(self, X, tiles):
        evs = []
        for t in tiles:
            evs.append(t.w)
            evs.extend(t.r.values())
        self._emit_waits(X, evs)

    def emit(self):
        nc = self.nc
        esem = {e: nc.alloc_semaphore(name=f"es_{e}") for e in COMPUTE}
        dsems = [nc.alloc_semaphore(name=f"ds_{i}") for i in range(len(self.dval))]
        rank = {}
        for e in COMPUTE:
            rank[e] = {q: i + 1 for i, q in enumerate(sorted(self.waited[e]))}

        def run(X, eng):
            myrank = rank.get(X, {})
            for o in self.ops[X]:
                k = o[0]
                if k == 'i':
                    ins = o[1](eng)
                    if o[2] in myrank:
                        ins.then_inc(esem[X], 1)
                elif k == 'w':
                    eng.wait_ge(esem[o[1]], rank[o[1]][o[2]])
                elif k == 'wd':
                    eng.wait_ge(dsems[o[1]], o[2])
                elif k == 'cc':
                    eng.collective_compute("AllGather", ALU.bypass, replica_groups=o[4],
                                           ins=[o[2]], outs=[o[1]]).then_inc(dsems[o[3]])
                elif o[4]:
                    eng.dma_start(out=o[1], in_=o[2], allow_slow_non_contiguous=True).then_inc(dsems[o[3]], 16)
                else:
                    eng.dma_start(out=o[1], in_=o[2]).then_inc(dsems[o[3]], 16)

        with nc.Block() as block:
            @block.tensor
            def _(e):
                run("tensor", e)

            @block.vector
            def _(e):
                run("vector", e)

            @block.scalar
            def _(e):
                run("scalar", e)

            @block.gpsimd
            def _(e):
                run("gpsimd", e)

            @block.sync
            def _(e):
                run("sync", e)


class Arena:
    def __init__(self, P, ap, nbytes):
        self.P = P
        self.ap = ap
        self.nbytes = nbytes
        self.top = 0
        self.live = []

    def _carve(self, nelem, dt):
        bpe = 4 if dt == F32 else 2
        nb = (nelem * bpe + 63) // 64 * 64
        assert self.top + nb <= self.nbytes, f"arena overflow {self.top}+{nb}>{self.nbytes}"
        a = self.ap[:, self.top // 4:(self.top + nb) // 4]
        if dt != F32:
            a = a.bitcast(dt)
        self.top += nb
        return a[:, 0:nelem]

    def tile(self, n, dt, name=""):
        t = T(self._carve(n, dt), name)
        self.live.append(t)
        return t

    def chunks(self, k, n, dt, name="", shared_ds=False):
        a = self._carve(k * n, dt)
        ds = self.P.new_ds() if shared_ds else None
        ts = [T(a[:, i * n:(i + 1) * n], f"{name}{i}", ds) for i in range(k)]
        self.live.extend(ts)
        return ts, a

    def mark(self):
        return (self.top, len(self.live))

    def release(self, m):
        dead = self.live[m[1]:]
        self.P.barrier(dead)
        freed = set()
        for t in dead:
            if t.ds is not None and t.ds not in freed:
                freed.add(t.ds)
                self.P.free_ds.append(t.ds)
            t.ds = None
        del self.live[m[1]:]
        self.top = m[0]


class Rot:
    def __init__(self, ts):
        self.ts = ts
        self.i = 0

    def next(self):
        t = self.ts[self.i % len(self.ts)]
        self.i += 1
        return t


WEIGHT_SPECS = [
    ("ada_w", [L, D, 9 * D]), ("ffn1_w_gu", [L, D, 2 * DFF]), ("ffn1_w_down", [L, DFF, D]),
    ("w_in", [L, D, 11584]), ("winr", [L, D, 1344]),
    ("wqb_n", [L, 512, 1024]), ("wqb_r", [L, 512, 512]), ("wqb_rr", [L, 512, 512]),
    ("wkvb_k", [L, 256, 1024]), ("wkvb_v", [L, 256, 1024]),
    ("w_branch_conv", [L, 1024, D]), ("w_branch_mla", [L, 1024, D]), ("w_branch_gqa", [L, 1024, D]),
    ("w_out", [L, D, D]), ("ffn2_w_gu", [L, D, 2 * DFF]), ("ffn2_w_down", [L, DFF, D]),
]
SCRATCH_SPECS = [
    ("HX", [D, TOK], F32), ("XM", [D, TOK], BF16),
    ("GKE", [2, 128, GKEYS], BF16), ("GVE", [128, 20, 256], BF16),
    ("UE", [1024, HALF + 2], F32), ("UC", [1024, CTX + 2], F32),
    ("CKN", [1024, CTX], BF16), ("CKR", [64, CTX], BF16), ("CV", [8, 128, 2, 128], BF16),
]
XCH_SPECS = [
    ("XU", 1024, 2, F32), ("XG", 256, 256, BF16), ("XGV", 256, 256, BF16), ("XKR", 64, HALF, BF16),
    ("XKN0", 256, HALF, BF16), ("XKN1", 256, HALF, BF16), ("XKN2", 256, HALF, BF16), ("XKN3", 256, HALF, BF16),
    ("XV0", 256, HALF, BF16), ("XV1", 256, HALF, BF16), ("XV2", 256, HALF, BF16), ("XV3", 256, HALF, BF16),
]
PAIRS = [[0, 1], [2, 3], [4, 5], [6, 7]]
def build(mode="FUSED"):
    nc = bass.Bass("TRN2", target_bir_lowering=False)
    P = Prog(nc)
    dr = {}

    def din(name, shape, dt=F32):
        dr[name] = nc.dram_tensor(name, shape, dt, kind="ExternalInput").ap()

    wshape = dict(WEIGHT_SPECS)
    wkeys = []
    din("smalls", [128, NS])
    din("cosg", [128, TOK]); din("sing", [128, TOK])
    din("cosm", [64, TOK]); din("sinm", [64, TOK])
    din("masks", [128, 4 * 512], BF16)
    din("xT", [D, HALF]); din("cT", [D, CTX])
    for n, sh, dt in SCRATCH_SPECS:
        dr[n] = nc.dram_tensor(n, sh, dt).ap()
    for l in range(L):
        for n, r, c, dt in XCH_SPECS:
            dr[f"{n}i{l}"] = nc.dram_tensor(f"{n}i{l}", [r, c], dt).ap()
            dr[f"{n}o{l}"] = nc.dram_tensor(f"{n}o{l}", [2 * r, c], dt).ap()
    dr["outT"] = nc.dram_tensor("outT", [D, HALF], F32, kind="ExternalOutput").ap()
    DT = {k: T(v, k) for k, v in dr.items()}
    WT = T(None, "weights")

    ARENA_BYTES = 204 * 1024
    arena_ap = nc.alloc_sbuf_tensor("arena", [128, ARENA_BYTES // 4], F32).ap()
    A = Arena(P, arena_ap, ARENA_BYTES)
    smalls = A.tile(NS, F32, "smalls")
    modL = [A.tile(288, F32, "mod0"), A.tile(288, F32, "mod1")]
    curl = [0]

    def MOD():
        return modL[curl[0]]
    CA = A.tile(96, F32, "CA")
    CG = A.tile(96, F32, "CG")
    misc = A.tile(64, F32, "misc")
    scb = A.tile(32, BF16, "scb")
    ones = A.tile(128, BF16, "ones")
    masks = A.tile(4 * 512, BF16, "masks")
    sinkrow = A.tile(2 * 512, F32, "sinkrow")
    ring = [A.tile(SLOT, BF16, f"ring{i}") for i in range(NSLOT)]
    ringi = [0]
    hx, hx_all = A.chunks(16, 512, F32, "hx", shared_ds=True)
    HA = Arena(P, hx_all, 16 * 512 * 4)
    n_rstd = A.tile(512, F32, "n_rstd")
    n_tmp = Rot([A.tile(512, F32, f"n_tmp{i}") for i in range(3)])
    psum = [T(nc.alloc_psum_tensor(f"ps{i}", [128, 512], F32).ap(), f"ps{i}") for i in range(8)]
    ps_o, ps_d = psum[0], psum[1]
    psr = Rot(psum[2:])

    def sm(c0, c1=None):
        return smalls.ap[:, c0:(c0 + 1 if c1 is None else c1)]

    def wload(name, l, kc, c0, ncols, k0=0):
        slot = ring[ringi[0] % NSLOT]
        ringi[0] += 1
        view = slot.ap[:, 0:kc * ncols].rearrange("p (k n) -> p k n", k=kc)
        key = f"{name}@{l}"
        if key not in dr:
            dr[key] = nc.dram_tensor(f"{name}_L{l}", wshape[name][1:], F32, kind="ExternalInput").ap()
            wkeys.append((f"{name}_L{l}", name, l))
        srcv = dr[key].rearrange("(c p) n -> p c n", p=128)[:, k0:k0 + kc, c0:c0 + ncols]
        P.dma("gpsimd", slot, WT, view, srcv)
        return slot, view

    P.dma("sync", smalls, DT["smalls"], smalls.ap, dr["smalls"])
    P.dma("sync", masks, DT["masks"], masks.ap, dr["masks"])
    P.op("vector", lambda e: e.memset(ones.ap, 1.0), writes=[ones])

    ada_next = [0, 0]
    silu_done = [False]

    def ada_slabs(l, count, banks):
        b = l * LS
        if not silu_done[0]:
            P.op("scalar", lambda e: e.activation(scb.ap, sm(S_CVEC, S_CVEC + 32), ACT.Silu),
                 reads=[smalls], writes=[scb])
            silu_done[0] = True
        for _ in range(count):
            s = ada_next[l]
            if s >= 72:
                return
            ada_next[l] += 1
            pm = banks[s % len(banks)]
            slot, v = wload("ada_w", l, 16, s * 256, 256)
            for q in range(2):
                n = s * 2 + q
                P.mm(pm, [(pm.ap[:, 2 * n:2 * n + 2], v[:, j, q * 128:(q + 1) * 128], scb.ap[:, 2 * j:2 * j + 2])
                          for j in range(16)], reads=[slot, scb], start=True, stop=True, fresh=(q == 0))
            c0 = 4 * s
            P.op("vector", lambda e, pm=pm, c0=c0, l=l, b=b: e.tensor_tensor(
                modL[l].ap[:, c0:c0 + 4], pm.ap[:, c0:c0 + 4], sm(b + S_ADAB + c0, b + S_ADAB + c0 + 4), op=ALU.add),
                reads=[pm, smalls, modL[l]], writes=[modL[l]])

    def layer_setup(l):
        b = l * LS
        ada_slabs(l, 72, [psum[6], psum[7]])
        curl[0] = l
        mod = MOD()
        for k, sn in enumerate((S_N1, S_NM, S_N2)):
            for w in range(2):
                m_scale = mod.ap.rearrange("p (m j w) -> p m j w", m=9, j=16)[:, 3 * k + 1, :, w]
                outv = CA.ap.rearrange("p (k j w) -> p k j w", k=3, j=16)[:, k, :, w]
                P.op("vector", lambda e, o=outv, i=m_scale, sn=sn: e.scalar_tensor_tensor(
                    o, i, 1.0, sm(b + sn, b + sn + 16), op0=ALU.add, op1=ALU.mult),
                    reads=[mod, smalls], writes=[CA])
                m_gate = mod.ap.rearrange("p (m j w) -> p m j w", m=9, j=16)[:, 3 * k + 2, :, w]
                outg = CG.ap.rearrange("p (k j w) -> p k j w", k=3, j=16)[:, k, :, w]
                gs = 1.0 if k == 1 else 0.5
                P.op("vector", lambda e, o=outg, i=m_gate, gs=gs: e.tensor_scalar(
                    o, i, gs, None, op0=ALU.mult), reads=[mod], writes=[CG])
        P.op("vector", lambda e: e.tensor_scalar(CA.ap, CA.ap, float(np.sqrt(D)), None, op0=ALU.mult),
             reads=[CA], writes=[CA])
        P.op("vector", lambda e: e.tensor_scalar(misc.ap[:, 0:4], sm(b + S_QN, b + S_QN + 4), float(np.sqrt(512)), None,
                                                 op0=ALU.mult), reads=[smalls], writes=[misc])
        P.op("vector", lambda e: e.tensor_scalar(misc.ap[:, 4:6], sm(b + S_KVN, b + S_KVN + 2), 16.0, None,
                                                 op0=ALU.mult), reads=[smalls, misc], writes=[misc])
        P.op("vector", lambda e: e.tensor_scalar(misc.ap[:, 8:24], sm(S_FIN, S_FIN + 16), float(np.sqrt(D)), None,
                                                 op0=ALU.mult), reads=[smalls, misc], writes=[misc])
        P.op("scalar", lambda e: e.activation(misc.ap[:, 24:32], sm(b + S_SINK, b + S_SINK + 8), ACT.Exp),
             reads=[smalls, misc], writes=[misc])
        for h in range(8):
            P.op("vector", lambda e, h=h: e.tensor_scalar(
                sinkrow.ap[:, h * 128:(h + 1) * 128], ones.ap[:, 0:128], misc.ap[:, 24 + h:25 + h], None, op0=ALU.mult),
                reads=[ones, misc, sinkrow], writes=[sinkrow])

    def Acol(k, j, w):
        i = (k * 16 + j) * 2 + w
        return CA.ap[:, i:i + 1]

    def Gcol(k, j, w):
        i = (k * 16 + j) * 2 + w
        return CG.ap[:, i:i + 1]

    def Bcol(m, j, w):
        i = (m * 16 + j) * 2 + w
        return MOD().ap[:, i:i + 1]

    def rmsnorm(xs, N, epscol, a_fn, b_fn, outs, consts, sq_tiles=None):
        sqs = outs if sq_tiles is None else sq_tiles
        pss = psr.next()
        nchunk = len(xs)
        for j, x in enumerate(xs):
            sq = sqs[j]
            if j % 2 == 0:
                P.op("scalar", lambda e, sq=sq, x=x: e.activation(sq.ap[:, :N], x.ap[:, :N], ACT.Square),
                     reads=[x], writes=[sq])
            else:
                P.op("vector", lambda e, sq=sq, x=x: e.tensor_tensor(sq.ap[:, :N], x.ap[:, :N], x.ap[:, :N], op=ALU.mult),
                     reads=[x], writes=[sq])
        for j in range(nchunk):
            sq = sqs[j]
            P.mm(pss, [(pss.ap[:, :N], ones.ap, sq.ap[:, :N])], reads=[ones, sq], start=(j == 0), stop=(j == nchunk - 1))
        rstd = n_rstd
        P.op("scalar", lambda e: e.activation(rstd.ap[:, :N], pss.ap[:, :N], ACT.Sqrt, bias=sm(epscol), scale=1.0),
             reads=[pss, smalls], writes=[rstd])
        P.op("vector", lambda e: e.reciprocal(rstd.ap[:, :N], rstd.ap[:, :N]), reads=[rstd], writes=[rstd])
        for j, x in enumerate(xs):
            o = outs[j]
            aj = a_fn(j)
            if b_fn is None:
                P.op("vector", lambda e, o=o, x=x, aj=aj: e.scalar_tensor_tensor(
                    o.ap[:, :N], x.ap[:, :N], aj, rstd.ap[:, :N], op0=ALU.mult, op1=ALU.mult),
                    reads=[x, rstd] + consts, writes=[o])
            else:
                bj = b_fn(j)
                t = n_tmp.next()
                P.op("vector", lambda e, t=t, x=x, aj=aj: e.scalar_tensor_tensor(
                    t.ap[:, :N], x.ap[:, :N], aj, rstd.ap[:, :N], op0=ALU.mult, op1=ALU.mult),
                    reads=[x, rstd] + consts, writes=[t])
                P.op("scalar", lambda e, t=t, o=o, bj=bj: e.activation(
                    o.ap[:, :N], t.ap[:, :N], ACT.Identity, bias=bj, scale=1.0),
                    reads=[t] + consts, writes=[o])

    def ffn(l, which, xn, N, w):
        m = A.mark()
        k = 0 if which == 1 else 2
        gu = "ffn1_w_gu" if which == 1 else "ffn2_w_gu"
        dn = "ffn1_w_down" if which == 1 else "ffn2_w_down"
        h, _ = A.chunks(FCH, 512, BF16, "h")
        sgs = Rot([A.tile(512, F32, "sg") for _ in range(2)])
        gb, ub = psum[0:4], psum[4:8]
        for s in range(11):
            for banks, c0 in ((gb, 0), (ub, DFF)):
                for kh in range(2):
                    slot, v = wload(gu, l, 8, c0 + s * 512, 512, k0=kh * 8)
                    for q in range(4):
                        p = banks[q]
                        P.mm(p, [(p.ap[:, :N], v[:, j, q * 128:(q + 1) * 128], xn[kh * 8 + j].ap[:, :N]) for j in range(8)],
                             reads=[slot] + xn[kh * 8:(kh + 1) * 8], start=(kh == 0), stop=(kh == 1))
            for q in range(4):
                fc = s * 4 + q
                pg, pu = gb[q], ub[q]
                sg = sgs.next()
                P.op("scalar", lambda e, sg=sg, pg=pg: e.activation(sg.ap[:, :N], pg.ap[:, :N], ACT.Silu),
                     reads=[pg], writes=[sg])
                P.op("vector", lambda e, sg=sg, pu=pu, fc=fc: e.tensor_tensor(
                    h[fc].ap[:, :N], sg.ap[:, :N], pu.ap[:, :N], op=ALU.mult), reads=[sg, pu], writes=[h[fc]])
        for dg in range(4):
            banks = psum[0:4] if dg % 2 == 0 else psum[4:8]
            f0 = 0
            sizes = [8, 8, 8, 8, 8, 4]
            for si, nf in enumerate(sizes):
                slot, v = wload(dn, l, nf, dg * 512, 512, k0=f0)
                for q in range(4):
                    p = banks[q]
                    P.mm(p, [(p.ap[:, :N], v[:, f, q * 128:(q + 1) * 128], h[f0 + f].ap[:, :N]) for f in range(nf)],
                         reads=[slot] + h[f0:f0 + nf], start=(si == 0), stop=(si == len(sizes) - 1))
                f0 += nf
            for q in range(4):
                dc = dg * 4 + q
                p = banks[q]
                P.op("vector", lambda e, p=p, dc=dc: e.scalar_tensor_tensor(
                    hx[dc].ap[:, :N], p.ap[:, :N], Gcol(k, dc, w), hx[dc].ap[:, :N], op0=ALU.mult, op1=ALU.add),
                    reads=[p, hx[dc], CG], writes=[hx[dc]])
        A.release(m)

    def load_hx(srcname, t0, N):
        v = dr[srcname].rearrange("(c p) t -> p c t", p=128)[:, :, t0:t0 + N]
        P.dma("sync", hx, DT[srcname], hx_all.rearrange("p (c t) -> p c t", c=16)[:, :, 0:N], v)

    def store_hx(t0, N):
        v = dr["HX"].rearrange("(c p) t -> p c t", p=128)[:, :, t0:t0 + N]
        P.dma("sync", DT["HX"], hx, v, hx_all.rearrange("p (c t) -> p c t", c=16)[:, :, 0:N])

    def rope_combine(p1, p2, cs, sn, out_ap, np_, N, tmps, out_t, eng_add="vector"):
        t1 = tmps.next()
        t2 = tmps.next()
        P.op("vector", lambda e: e.tensor_tensor(t1.ap[0:np_, :N], p1.ap[0:np_, :N], cs.ap[0:np_, :N], op=ALU.mult),
             reads=[p1, cs], writes=[t1])
        P.op("vector", lambda e: e.tensor_tensor(t2.ap[0:np_, :N], p2.ap[0:np_, :N], sn.ap[0:np_, :N], op=ALU.mult),
             reads=[p2, sn], writes=[t2])
        P.op(eng_add, lambda e: e.tensor_tensor(out_ap, t1.ap[0:np_, :N], t2.ap[0:np_, :N], op=ALU.add),
             reads=[t1, t2], writes=[out_t])

    accs = [(psum[0], psum[1]), (psum[2], psum[3])]
    ring4 = Rot(psum[4:7])

    def attn_stream(groups, PT):
        flat = []
        for gi, g in enumerate(groups):
            n = len(g["items"])
            for i, it in enumerate(g["items"]):
                flat.append((gi, i == 0, i == n - 1, g, it))

        def issue_S(k):
            it = flat[k][4]
            p = ring4.next()
            P.mm(p, it["s_pairs"](p), reads=it["s_reads"]())
            return p
        LOOK = 2
        pend = [issue_S(k) for k in range(min(LOOK, len(flat)))]
        for k in range(len(flat)):
            gi, fi, la, g, it = flat[k]
            if fi and g.get("prefetch") is not None:
                g["prefetch"]()
            p = pend.pop(0)
            if k + LOOK < len(flat):
                pend.append(issue_S(k + LOOK))
            N = g["N"]
            po, pd = accs[gi % 2]
            pt = PT.next()
            P.op("scalar", lambda e, pt=pt, p=p, N=N, sc=g["scale"]: e.activation(
                pt.ap[:, :N], p.ap[:, :N], ACT.Exp, scale=sc), reads=[p], writes=[pt])
            if it["mask"] is not None:
                mk = it["mask"]
                P.op("vector", lambda e, pt=pt, mk=mk, N=N: e.tensor_tensor(
                    pt.ap[:, :N], pt.ap[:, :N], masks.ap[:, mk * 512:mk * 512 + N], op=ALU.mult), reads=[pt, masks], writes=[pt])
            P.mm(po, [(po.ap[:, :N], it["v_lhsT"](), pt.ap[:, :N])], reads=it["v_reads"]() + [pt], start=fi, stop=la)
            pair = g.get("pair_den")
            if pair is None:
                P.mm(pd, [(pd.ap[:, :N], ones.ap, pt.ap[:, :N])], reads=[ones, pt], start=fi, stop=la)
            else:
                ci = it["ci"]
                if pair.get("pending") is not None:
                    d2p, stp = pair["pending"]
                    pair["pending"] = None
                    P.mm(pd, [(pd.ap[:, :N], ones.ap, d2p.ap[:, :N])], reads=[ones, d2p], start=stp, stop=False)
                if ci % 2 == 0 and not la:
                    pair["held"] = pt
                else:
                    if ci % 2 == 1:
                        p0 = pair["held"]
                        d2 = pair["buf"].next()
                        P.op("vector", lambda e, d2=d2, p0=p0, pt=pt, N=N: e.tensor_tensor(
                            d2.ap[:, :N], p0.ap[:, :N], pt.ap[:, :N], op=ALU.add), reads=[p0, pt], writes=[d2])
                    else:
                        d2 = pt
                    if la:
                        P.mm(pd, [(pd.ap[:, :N], ones.ap, d2.ap[:, :N])], reads=[ones, d2], start=(ci <= 1), stop=True)
                    else:
                        pair["pending"] = (d2, ci <= 1)
            if la:
                g["finish"](po, pd)
                if g.get("post") is not None:
                    g["post"]()

    def phase1_tile(l, ti):
        t0, N = TILES[ti]
        w = 1 if ti == 4 else 0
        mT = A.mark()
        xn, xn_all = A.chunks(16, 512, BF16, "xn")
        if l == 0:
            if w == 0:
                v = dr["xT"].rearrange("(c p) t -> p c t", p=128)[:, :, t0:t0 + N]
                P.dma("sync", hx, DT["xT"], hx_all.rearrange("p (c t) -> p c t", c=16)[:, :, 0:N], v)
            else:
                v = dr["cT"].rearrange("(c p) t -> p c t", p=128)
                P.dma("sync", hx, DT["cT"], hx_all.rearrange("p (c t) -> p c t", c=16)[:, :, 0:N], v)
        else:
            load_hx("HX", t0, N)
        rmsnorm(hx, N, S_EPS, lambda j: Acol(0, j, w), lambda j: Bcol(0, j, w), xn, [CA, MOD()])
        ffn(l, 1, xn, N, w)
        store_hx(t0, N)
        xm = xn
        rmsnorm(hx, N, S_EPS, lambda j: Acol(1, j, w), lambda j: Bcol(3, j, w), xm, [CA, MOD()])
        xmv = dr["XM"].rearrange("(c p) t -> p c t", p=128)[:, :, t0:t0 + N]
        P.dma("sync", DT["XM"], xm, xmv, xn_all.rearrange("p (c t) -> p c t", c=16)[:, :, 0:N])
        mJ = A.mark()
        stg = Rot([A.tile(512, BF16, "stg") for _ in range(3)])
        ft = Rot([A.tile(512, F32, "ft") for _ in range(4)])
        ust = Rot([A.tile(512, F32, "ust") for _ in range(2)])
        cosg = A.tile(512, F32, "cosg"); sing = A.tile(512, F32, "sing")
        cosm = A.tile(512, F32, "cosm"); sinm = A.tile(512, F32, "sinm")
        P.dma("sync", cosg, DT["cosg"], cosg.ap[:, :N], dr["cosg"][:, t0:t0 + N])
        P.dma("sync", sing, DT["sing"], sing.ap[:, :N], dr["sing"][:, t0:t0 + N])
        P.dma("sync", cosm, DT["cosm"], cosm.ap[0:64, :N], dr["cosm"][:, t0:t0 + N])
        P.dma("sync", sinm, DT["sinm"], sinm.ap[0:64, :N], dr["sinm"][:, t0:t0 + N])
        for s in range(4):
            gs, gv = wload("w_in", l, 16, C_GC + s * 256, 256)
            vs, vv = wload("w_in", l, 16, C_V + s * 256, 256)
            for q in range(2):
                jc = s * 2 + q
                pg = psr.next()
                P.mm(pg, [(pg.ap[:, :N], gv[:, j, q * 128:(q + 1) * 128], xm[j].ap[:, :N]) for j in range(16)], reads=[gs] + xm)
                pv = psr.next()
                P.mm(pv, [(pv.ap[:, :N], vv[:, j, q * 128:(q + 1) * 128], xm[j].ap[:, :N]) for j in range(16)], reads=[vs] + xm)
                t = ft.next()
                P.op("scalar", lambda e, t=t, pg=pg: e.activation(t.ap[:, :N], pg.ap[:, :N], ACT.Copy), reads=[pg], writes=[t])
                u = ust.next()
                P.op("vector", lambda e, u=u, t=t, pv=pv: e.tensor_tensor(u.ap[:, :N], t.ap[:, :N], pv.ap[:, :N], op=ALU.mult),
                     reads=[t, pv], writes=[u])
                if w == 0:
                    P.dma("sync", DT["UE"], u, dr["UE"][jc * 128:(jc + 1) * 128, 1 + t0:1 + t0 + N], u.ap[:, :N])
                    if ti == 0:
                        P.dma("sync", DT[f"XUi{l}"], u, dr[f"XUi{l}"][jc * 128:(jc + 1) * 128, 0:1], u.ap[:, 0:1], slow=True)
                    if ti == 3:
                        P.dma("sync", DT[f"XUi{l}"], u, dr[f"XUi{l}"][jc * 128:(jc + 1) * 128, 1:2], u.ap[:, N - 1:N], slow=True)
                else:
                    P.dma("sync", DT["UC"], u, dr["UC"][jc * 128:(jc + 1) * 128, 1:1 + N], u.ap[:, :N])
        ks, kv_ = wload("w_in", l, 16, C_CKV, 256)
        krs, krv = wload("w_in", l, 16, C_KR, 64)
        rs, rv = wload("winr", l, 16, 1280, 64)
        ckv, _ = A.chunks(2, 512, F32, "ckv")
        ckvn, _ = A.chunks(2, 512, BF16, "ckvn")
        for j2 in range(2):
            p = psr.next()
            P.mm(p, [(p.ap[:, :N], kv_[:, j, j2 * 128:(j2 + 1) * 128], xm[j].ap[:, :N]) for j in range(16)], reads=[ks] + xm)
            P.op("scalar", lambda e, p=p, j2=j2: e.activation(ckv[j2].ap[:, :N], p.ap[:, :N], ACT.Copy), reads=[p], writes=[ckv[j2]])
        rmsnorm(ckv, N, S_EPS + 2, lambda j: misc.ap[:, 4 + j:5 + j], None, ckvn, [misc])
        key0 = t0 if w == 0 else 0
        p1 = psr.next()
        P.mm(p1, [(p1.ap[0:64, :N], krv[:, j, 0:64], xm[j].ap[:, :N]) for j in range(16)], reads=[krs] + xm)
        p2 = psr.next()
        P.mm(p2, [(p2.ap[0:64, :N], rv[:, j, 0:64], xm[j].ap[:, :N]) for j in range(16)], reads=[rs] + xm)
        st = stg.next()
        rope_combine(p1, p2, cosm, sinm, st.ap[0:64, :N], 64, N, ft, st)
        krn = f"XKRi{l}" if w == 0 else "CKR"
        P.dma("sync", DT[krn], st, dr[krn][:, key0:key0 + N], st.ap[0:64, :N])
        kks, kkv = wload("wkvb_k", l, 2, 0, 1024)
        for h in range(8):
            p = psr.next()
            P.mm(p, [(p.ap[:, :N], kkv[:, j2, h * 128:(h + 1) * 128], ckvn[j2].ap[:, :N]) for j2 in range(2)], reads=[kks] + ckvn)
            st = stg.next()
            P.op("scalar", lambda e, st=st, p=p: e.activation(st.ap[:, :N], p.ap[:, :N], ACT.Copy), reads=[p], writes=[st])
            if w == 0:
                knn = f"XKN{h // 2}i{l}"
                P.dma("sync", DT[knn], st, dr[knn][(h % 2) * 128:(h % 2 + 1) * 128, key0:key0 + N], st.ap[:, :N])
            else:
                P.dma("sync", DT["CKN"], st, dr["CKN"][h * 128:(h + 1) * 128, key0:key0 + N], st.ap[:, :N])
        vvs, vvv = wload("wkvb_v", l, 2, 0, 1024)
        for tb in range(N // 128):
            c = key0 // 128 + tb
            for hh in range(2):
                p = psr.next()
                P.mm(p, [(p.ap[:, 0:512], ckvn[j2].ap[:, tb * 128:(tb + 1) * 128], vvv[:, j2, hh * 512:(hh + 1) * 512])
                         for j2 in range(2)], reads=[vvs] + ckvn)
                st = stg.next()
                P.op("vector", lambda e, st=st, p=p: e.tensor_copy(st.ap[:, 0:512], p.ap[:, 0:512]), reads=[p], writes=[st])
                if w == 0:
                    for q2 in range(2):
                        vn = f"XV{hh * 2 + q2}i{l}"
                        vdst = dr[vn].rearrange("(h p) (c d) -> h p c d", h=2, d=128)[:, :, c, :]
                        P.dma("sync", DT[vn], st, vdst.rearrange("h p d -> p h d"),
                              st.ap[:, q2 * 256:(q2 + 1) * 256].rearrange("p (h d) -> p h d", h=2))
                else:
                    vdst = dr["CV"][hh * 4:(hh + 1) * 4, :, c, :]
                    P.dma("sync", DT["CV"], st, vdst.rearrange("h p d -> p h d"),
                          st.ap[:, 0:512].rearrange("p (h d) -> p h d", h=4))
        gks, gkv = wload("w_in", l, 16, C_GK, 256)
        gvs, gvv = wload("w_in", l, 16, C_GV, 256)
        rs, rv = wload("winr", l, 16, 1024, 256)
        gkey0 = 128 + t0 if w == 0 else 128 + HALF + 128
        for g in range(2):
            p1 = psr.next()
            P.mm(p1, [(p1.ap[:, :N], gkv[:, j, g * 128:(g + 1) * 128], xm[j].ap[:, :N]) for j in range(16)], reads=[gks] + xm)
            p2 = psr.next()
            P.mm(p2, [(p2.ap[:, :N], rv[:, j, g * 128:(g + 1) * 128], xm[j].ap[:, :N]) for j in range(16)], reads=[rs] + xm)
            st = stg.next()
            rope_combine(p1, p2, cosg, sing, st.ap[:, :N], 128, N, ft, st)
            P.dma("sync", DT["GKE"], st, dr["GKE"][g, :, gkey0:gkey0 + N], st.ap[:, :N])
            if w == 0 and ti == 0:
                P.dma("sync", DT[f"XGi{l}"], st, dr[f"XGi{l}"][g * 128:(g + 1) * 128, 0:128], st.ap[:, 0:128])
            if w == 0 and ti == 3:
                P.dma("sync", DT[f"XGi{l}"], st, dr[f"XGi{l}"][g * 128:(g + 1) * 128, 128:256], st.ap[:, N - 128:N])
        for tb in range(N // 128):
            c = gkey0 // 128 + tb
            p = psr.next()
            P.mm(p, [(p.ap[:, 0:256], xm[j].ap[:, tb * 128:(tb + 1) * 128], gvv[:, j, 0:256]) for j in range(16)], reads=[gvs] + xm)
            st = stg.next()
            P.op("vector", lambda e, st=st, p=p: e.tensor_copy(st.ap[:, 0:256], p.ap[:, 0:256]), reads=[p], writes=[st])
            P.dma("sync", DT["GVE"], st, dr["GVE"][:, c, :], st.ap[:, 0:256])
            if w == 0 and ti == 0 and tb == 0:
                P.dma("sync", DT[f"XGVi{l}"], st, dr[f"XGVi{l}"][0:128, :], st.ap[:, 0:256])
            if w == 0 and ti == 3 and tb == N // 128 - 1:
                P.dma("sync", DT[f"XGVi{l}"], st, dr[f"XGVi{l}"][128:256, :], st.ap[:, 0:256])
        A.release(mJ)
        A.release(mT)

    def phase2_tile(l, ti, final):
        t0, N = TILES[ti]
        w = 1 if ti == 4 else 0
        mT = A.mark()
        xm, xm_all = A.chunks(16, 512, BF16, "xm", shared_ds=True)
        yc, _ = A.chunks(8, 512, BF16, "yc")
        ymla, _ = A.chunks(8, 512, BF16, "ymla")
        ygqa, ygqa_all = A.chunks(8, 512, BF16, "ygqa")
        P.dma("sync", xm, DT["XM"], xm_all.rearrange("p (c t) -> p c t", c=16)[:, :, 0:N], dr["XM"].rearrange("(c p) t -> p c t", p=128)[:, :, t0:t0 + N])
        m = A.mark()
        U, U_all = A.chunks(8, 516, F32, "U", shared_ds=True)
        if w == 0:
            uv = dr["UE"].rearrange("(c p) t -> p c t", p=128)[:, :, t0:t0 + N + 2]
        else:
            uv = dr["UC"].rearrange("(c p) t -> p c t", p=128)
        Uv = U_all.rearrange("p (c t) -> p c t", c=8)[:, :, 0:N + 2]
        P.dma("sync", U, DT["UE" if w == 0 else "UC"], Uv, uv)
        U3 = U_all.rearrange("p (c t) -> p c t", c=8)
        if w == 1:
            P.op("vector", lambda e: e.memset(U3[:, :, 0:1], 0.0), writes=U)
            P.op("vector", lambda e: e.memset(U3[:, :, N + 1:N + 2], 0.0), writes=U)
        else:
            xuo = dr[f"XUo{l}"].rearrange("(r c p) t -> r p c t", r=2, p=128)
            if t0 == 0:
                P.dma("sync", U, DT[f"XUo{l}"], U3[:, :, 0:1], xuo[0][:, :, 1:2], slow=True)
                P.op("vector", lambda e: e.tensor_scalar(U3[:, :, 0:1], U3[:, :, 0:1], sm(S_FLAG), None, op0=ALU.mult),
                     reads=[smalls] + U, writes=U)
            if t0 + N == HALF:
                P.dma("sync", U, DT[f"XUo{l}"], U3[:, :, N + 1:N + 2], xuo[1][:, :, 0:1], slow=True)
                P.op("vector", lambda e: e.tensor_scalar(U3[:, :, N + 1:N + 2], U3[:, :, N + 1:N + 2], sm(S_FLAG + 1), None, op0=ALU.mult),
                     reads=[smalls] + U, writes=U)
        cy = Rot([A.tile(512, F32, "cy") for _ in range(2)])
        cb = l * LS + S_CONV
        for s in range(4):
            gs, gv = wload("w_in", l, 16, C_GB + s * 256, 256)
            for q in range(2):
                jc = s * 2 + q
                pg = psr.next()
                P.mm(pg, [(pg.ap[:, :N], gv[:, j, q * 128:(q + 1) * 128], xm[j].ap[:, :N]) for j in range(16)], reads=[gs] + xm)
                y = cy.next()
                u = U[jc]
                P.op("vector", lambda e, y=y, u=u, jc=jc: e.tensor_scalar(
                    y.ap[:, :N], u.ap[:, 0:N], sm(cb + jc), None, op0=ALU.mult), reads=[u, smalls], writes=[y])
                P.op("vector", lambda e, y=y, u=u, jc=jc: e.scalar_tensor_tensor(
                    y.ap[:, :N], u.ap[:, 1:N + 1], sm(cb + 8 + jc), y.ap[:, :N], op0=ALU.mult, op1=ALU.add),
                    reads=[u, smalls, y], writes=[y])
                P.op("vector", lambda e, y=y, u=u, jc=jc: e.scalar_tensor_tensor(
                    y.ap[:, :N], u.ap[:, 2:N + 2], sm(cb + 16 + jc), y.ap[:, :N], op0=ALU.mult, op1=ALU.add),
                    reads=[u, smalls, y], writes=[y])
                P.op("vector", lambda e, y=y, pg=pg, jc=jc: e.tensor_tensor(
                    yc[jc].ap[:, :N], y.ap[:, :N], pg.ap[:, :N], op=ALU.mult), reads=[y, pg], writes=[yc[jc]])
        A.release(m)
        m = A.mark()
        P.barrier(hx)
        mh = HA.mark()
        qn, _ = HA.chunks(8, 512, BF16, "qn")
        qr, _ = HA.chunks(8, 512, BF16, "qr")
        m2 = A.mark()
        cq, _ = A.chunks(4, 512, F32, "cq")
        cqn, _ = A.chunks(4, 512, BF16, "cqn")
        cosm = A.tile(512, F32, "cosm"); sinm = A.tile(512, F32, "sinm")
        ft = Rot([A.tile(512, F32, "ft") for _ in range(4)])
        P.dma("sync", cosm, DT["cosm"], cosm.ap[0:64, :N], dr["cosm"][:, t0:t0 + N])
        P.dma("sync", sinm, DT["sinm"], sinm.ap[0:64, :N], dr["sinm"][:, t0:t0 + N])
        for c4 in range(4):
            if c4 % 2 == 0:
                qs, qv = wload("w_in", l, 16, C_CQ + c4 * 128, 256)
            p = psr.next()
            P.mm(p, [(p.ap[:, :N], qv[:, j, (c4 % 2) * 128:(c4 % 2 + 1) * 128], xm[j].ap[:, :N]) for j in range(16)], reads=[qs] + xm)
            P.op("scalar", lambda e, p=p, c4=c4: e.activation(cq[c4].ap[:, :N], p.ap[:, :N], ACT.Copy), reads=[p], writes=[cq[c4]])
        rmsnorm(cq, N, S_EPS + 1, lambda j: misc.ap[:, j:j + 1], None, cqn, [misc])
        ns_, nv = wload("wqb_n", l, 4, 0, 1024)
        r1s, r1v = wload("wqb_r", l, 4, 0, 512)
        r2s, r2v = wload("wqb_rr", l, 4, 0, 512)
        for h in range(8):
            p = psr.next()
            P.mm(p, [(p.ap[:, :N], nv[:, c4, h * 128:(h + 1) * 128], cqn[c4].ap[:, :N]) for c4 in range(4)], reads=[ns_] + cqn)
            P.op("scalar", lambda e, p=p, h=h: e.activation(qn[h].ap[:, :N], p.ap[:, :N], ACT.Copy), reads=[p], writes=[qn[h]])
            p1 = psr.next()
            P.mm(p1, [(p1.ap[0:64, :N], r1v[:, c4, h * 64:(h + 1) * 64], cqn[c4].ap[:, :N]) for c4 in range(4)], reads=[r1s] + cqn)
            p2 = psr.next()
            P.mm(p2, [(p2.ap[0:64, :N], r2v[:, c4, h * 64:(h + 1) * 64], cqn[c4].ap[:, :N]) for c4 in range(4)], reads=[r2s] + cqn)
            P.op("vector", lambda e, h=h: e.memset(qr[h].ap[64:128, :N], 0.0), writes=[qr[h]])
            rope_combine(p1, p2, cosm, sinm, qr[h].ap[0:64, :N], 64, N, ft, qr[h])
        A.release(m2)
        KN = Rot([A.tile(NKEY, BF16, f"KN{i}") for i in range(2)])
        VV = Rot([A.tile(NKEY, BF16, f"VV{i}") for i in range(2)])
        KR = HA.tile(NKEY, BF16, "KR")
        PT = Rot([A.tile(512, BF16, "PT") for _ in range(4)])
        pair_den = dict(held=None, pending=None, buf=Rot([A.tile(512, BF16, f"pd2{i}") for i in range(2)]))
        rden = A.tile(512, F32, "rden")
        kchunks = list(range(NKC)) if w == 0 else [NKC - 2, NKC - 1]
        kc0 = kchunks[0] * 128
        nk = len(kchunks) * 128
        P.op("vector", lambda e: e.memset(KR.ap[64:128, :], 0.0), writes=[KR])
        if w == 0:
            for r in range(2):
                P.dma("sync", KR, DT[f"XKRo{l}"], KR.ap[0:64, r * HALF:(r + 1) * HALF], dr[f"XKRo{l}"][r * 64:(r + 1) * 64, :])
        P.dma("sync", KR, DT["CKR"], KR.ap[0:64, 2 * HALF:2 * HALF + CTX], dr["CKR"])
        kns = [None] * 8
        vvs_ = [None] * 8

        def load_head(h):
            kn = KN.next()
            vv = VV.next()
            if w == 0:
                for r in range(2):
                    r0 = r * 256 + (h % 2) * 128
                    P.dma("sync", kn, DT[f"XKN{h // 2}o{l}"], kn.ap[:, r * HALF:(r + 1) * HALF],
                          dr[f"XKN{h // 2}o{l}"][r0:r0 + 128, :])
                    P.dma("sync", vv, DT[f"XV{h // 2}o{l}"], vv.ap[:, r * HALF:(r + 1) * HALF],
                          dr[f"XV{h // 2}o{l}"][r0:r0 + 128, :])
            P.dma("sync", kn, DT["CKN"], kn.ap[:, 2 * HALF:2 * HALF + CTX], dr["CKN"][h * 128:(h + 1) * 128, :])
            P.dma("sync", vv, DT["CV"], vv.ap[:, 2 * HALF:2 * HALF + CTX].rearrange("p (c d) -> p c d", d=128), dr["CV"][h])
            kns[h], vvs_[h] = kn, vv

        def mk_finish(h):
            def fin(po, pd):
                P.op("vector", lambda e: e.reciprocal(rden.ap[:, :N], pd.ap[:, :N]), reads=[pd], writes=[rden])
                P.op("vector", lambda e: e.tensor_tensor(ymla[h].ap[:, :N], po.ap[:, :N], rden.ap[:, :N], op=ALU.mult),
                     reads=[po, rden], writes=[ymla[h]])
            return fin

        def mk_item(h, c):
            def s_pairs(p):
                return [(p.ap[:, :N], kns[h].ap[:, c * 128:(c + 1) * 128], qn[h].ap[:, :N]),
                        (p.ap[:, :N], KR.ap[:, c * 128:(c + 1) * 128], qr[h].ap[:, :N])]
            return dict(s_pairs=s_pairs, s_reads=lambda: [kns[h], KR, qn[h], qr[h]], ci=c - kchunks[0],
                        v_lhsT=lambda: vvs_[h].ap[:, c * 128:(c + 1) * 128], v_reads=lambda: [vvs_[h]], mask=None)

        load_head(0)
        groups = []
        for h in range(8):
            groups.append(dict(N=N, scale=float(MLA_SCALE), items=[mk_item(h, c) for c in kchunks], finish=mk_finish(h), pair_den=pair_den,
                               prefetch=(lambda h=h: load_head(h + 1)) if h < 7 else None,
                               post=(lambda: ada_slabs(l + 1, 2, [psum[7]])) if l + 1 < L else None))
        attn_stream(groups, PT)
        HA.release(mh)
        A.release(m)
        m = A.mark()
        NB = N // 128
        qg = A.tile(8 * 512, BF16, "qg")
        cosg = A.tile(512, F32, "cosg"); sing = A.tile(512, F32, "sing")
        ft = Rot([A.tile(512, F32, "ft") for _ in range(4)])
        GK = A.tile(2 * GKEYS, BF16, "GK")
        GV = A.tile(20 * 256, BF16, "GV")
        PT = Rot([A.tile(512, BF16, "PTg") for _ in range(3)])
        dent = A.tile(512, F32, "dent")
        P.dma("sync", cosg, DT["cosg"], cosg.ap[:, :N], dr["cosg"][:, t0:t0 + N])
        P.dma("sync", sing, DT["sing"], sing.ap[:, :N], dr["sing"][:, t0:t0 + N])
        GK3 = GK.ap.rearrange("p (g k) -> p g k", g=2)
        GV3 = GV.ap.rearrange("p (c d) -> p c d", c=20)
        P.dma("sync", GK, DT["GKE"], GK3[:, :, 128:GKEYS], dr["GKE"].rearrange("g p k -> p g k")[:, :, 128:GKEYS])
        P.dma("sync", GV, DT["GVE"], GV3[:, 1:20, :], dr["GVE"][:, 1:20, :])
        if w == 0:
            xgo = dr[f"XGo{l}"].rearrange("(r g p) k -> r p g k", r=2, g=2)
            P.dma("sync", GK, DT[f"XGo{l}"], GK3[:, :, 0:128], xgo[0][:, :, 128:256])
            P.dma("sync", GK, DT[f"XGo{l}"], GK3[:, :, 128 + HALF:128 + HALF + 128], xgo[1][:, :, 0:128])
            P.dma("sync", GV, DT[f"XGVo{l}"], GV.ap[:, 0:256], dr[f"XGVo{l}"][128:256, :])
            P.dma("sync", GV, DT[f"XGVo{l}"], GV.ap[:, 17 * 256:18 * 256], dr[f"XGVo{l}"][256:384, :])
        for s in range(4):
            q1s, q1v = wload("w_in", l, 16, C_GQ + s * 256, 256)
            q2s, q2v = wload("winr", l, 16, s * 256, 256)
            for q in range(2):
                h = s * 2 + q
                p1 = psr.next()
                P.mm(p1, [(p1.ap[:, :N], q1v[:, j, q * 128:(q + 1) * 128], xm[j].ap[:, :N]) for j in range(16)], reads=[q1s] + xm)
                p2 = psr.next()
                P.mm(p2, [(p2.ap[:, :N], q2v[:, j, q * 128:(q + 1) * 128], xm[j].ap[:, :N]) for j in range(16)], reads=[q2s] + xm)
                outv = qg.ap.rearrange("p (b h i) -> p b h i", b=4, h=8)[:, 0:NB, h, :]
                t1 = ft.next(); t2 = ft.next()
                P.op("vector", lambda e, t1=t1, p1=p1: e.tensor_tensor(t1.ap[:, :N], p1.ap[:, :N], cosg.ap[:, :N], op=ALU.mult),
                     reads=[p1, cosg], writes=[t1])
                P.op("vector", lambda e, t2=t2, p2=p2: e.tensor_tensor(t2.ap[:, :N], p2.ap[:, :N], sing.ap[:, :N], op=ALU.mult),
                     reads=[p2, sing], writes=[t2])
                P.op("vector", lambda e, t1=t1, t2=t2, outv=outv: e.tensor_tensor(
                    outv, t1.ap[:, :N].rearrange("p (b i) -> p b i", i=128), t2.ap[:, :N].rearrange("p (b i) -> p b i", i=128),
                    op=ALU.add), reads=[t1, t2], writes=[qg])
        groups = []
        for qb in range(NB):
            nb = t0 // 128 + qb
            if w == 0:
                chunks = [(nb, 2 if nb == 0 else 0), (nb + 1, None), (nb + 2, 3 if nb == 15 else 1), (18, None), (19, None)]
            else:
                chunks = [(18, None), (19, None)]
            for g in range(2):
                rhs = qg.ap[:, qb * 1024 + g * 512: qb * 1024 + (g + 1) * 512]

                def mk_item(c, mk, g=g, rhs=rhs):
                    return dict(s_pairs=lambda p: [(p.ap, GK.ap[:, g * GKEYS + c * 128: g * GKEYS + (c + 1) * 128], rhs)],
                                s_reads=lambda: [GK, qg],
                                v_lhsT=lambda: GV.ap[:, c * 256 + g * 128: c * 256 + (g + 1) * 128],
                                v_reads=lambda: [GV], mask=mk)

                def fin(po, pd, g=g, qb=qb):
                    P.op("vector", lambda e: e.tensor_tensor(dent.ap, pd.ap, sinkrow.ap[:, g * 512:(g + 1) * 512], op=ALU.add),
                         reads=[pd, sinkrow], writes=[dent])
                    P.op("vector", lambda e: e.reciprocal(dent.ap, dent.ap), reads=[dent], writes=[dent])
                    outv = ygqa_all.rearrange("p (h t) -> p h t", h=8)[:, g * 4:(g + 1) * 4, qb * 128:(qb + 1) * 128]
                    P.op("vector", lambda e: e.tensor_tensor(
                        outv, po.ap.rearrange("p (h i) -> p h i", h=4), dent.ap.rearrange("p (h i) -> p h i", h=4), op=ALU.mult),
                        reads=[po, dent], writes=ygqa)
                groups.append(dict(N=512, scale=float(GQA_SCALE), items=[mk_item(c, mk) for c, mk in chunks], finish=fin, prefetch=None))
        attn_stream(groups, PT)
        A.release(m)
        m = A.mark()
        macc, _ = A.chunks(16, 512, F32, "macc")
        merged = yc + ymla
        sig = Rot([A.tile(512, F32, "sig") for _ in range(2)])
        tt = Rot([A.tile(512, F32, "tt") for _ in range(2)])
        for b, (bw, ys) in enumerate((("w_branch_conv", yc), ("w_branch_mla", ymla), ("w_branch_gqa", ygqa))):
            for dg in range(8):
                gs, gv = wload("w_in", l, 16, C_GATE + b * D + dg * 256, 256)
                bs, bv = wload(bw, l, 8, dg * 256, 256)
                for q in range(2):
                    dc = dg * 2 + q
                    pg = psr.next()
                    P.mm(pg, [(pg.ap[:, :N], gv[:, j, q * 128:(q + 1) * 128], xm[j].ap[:, :N]) for j in range(16)], reads=[gs] + xm)
                    pb = psr.next()
                    P.mm(pb, [(pb.ap[:, :N], bv[:, j, q * 128:(q + 1) * 128], ys[j].ap[:, :N]) for j in range(8)], reads=[bs] + ys)
                    sg = sig.next()
                    P.op("scalar", lambda e, sg=sg, pg=pg: e.activation(sg.ap[:, :N], pg.ap[:, :N], ACT.Sigmoid), reads=[pg], writes=[sg])
                    if b == 0:
                        P.op("vector", lambda e, sg=sg, pb=pb, dc=dc: e.tensor_tensor(
                            macc[dc].ap[:, :N], sg.ap[:, :N], pb.ap[:, :N], op=ALU.mult), reads=[sg, pb], writes=[macc[dc]])
                    else:
                        t = tt.next()
                        P.op("vector", lambda e, sg=sg, pb=pb, t=t: e.tensor_tensor(
                            t.ap[:, :N], sg.ap[:, :N], pb.ap[:, :N], op=ALU.mult), reads=[sg, pb], writes=[t])
                        o = macc[dc] if b == 1 else merged[dc]
                        P.op("vector", lambda e, t=t, o=o, dc=dc: e.tensor_tensor(
                            o.ap[:, :N], macc[dc].ap[:, :N], t.ap[:, :N], op=ALU.add), reads=[t, macc[dc]], writes=[o])
        load_hx_from(dr["HX"], DT["HX"], t0, N)
        for s in range(8):
            ws, wv = wload("w_out", l, 16, s * 256, 256)
            for q in range(2):
                dc = s * 2 + q
                p = psr.next()
                P.mm(p, [(p.ap[:, :N], wv[:, j, q * 128:(q + 1) * 128], merged[j].ap[:, :N]) for j in range(16)], reads=[ws] + merged)
                P.op("vector", lambda e, p=p, dc=dc: e.scalar_tensor_tensor(
                    hx[dc].ap[:, :N], p.ap[:, :N], Gcol(1, dc, w), hx[dc].ap[:, :N], op0=ALU.mult, op1=ALU.add),
                    reads=[p, hx[dc], CG], writes=[hx[dc]])
        A.release(m)
        A.release(mT)
        mT = A.mark()
        xn, _ = A.chunks(16, 512, BF16, "xn2")
        rmsnorm(hx, N, S_EPS, lambda j: Acol(2, j, w), lambda j: Bcol(6, j, w), xn, [CA, MOD()])
        ffn(l, 2, xn, N, w)
        if final:
            outs, outs_all = A.chunks(16, 512, F32, "fin")
            rmsnorm(hx, N, S_EPS, lambda j: misc.ap[:, 8 + j:9 + j], None, outs, [misc], sq_tiles=xn)
            ov = dr["outT"].rearrange("(c p) t -> p c t", p=128)[:, :, t0:t0 + N]
            P.dma("sync", DT["outT"], outs, ov, outs_all.rearrange("p (c t) -> p c t", c=16)[:, :, 0:N])
        else:
            store_hx(t0, N)
        A.release(mT)

    def DT_S(n):
        return DT[n + "_in"] if (n + "_in") in DT else DT[n]

    def load_hx_from(ap, t, t0, N):
        v = ap.rearrange("(c p) t -> p c t", p=128)[:, :, t0:t0 + N]
        P.dma("sync", hx, t, hx_all.rearrange("p (c t) -> p c t", c=16)[:, :, 0:N], v)

    for l in range(L):
        layer_setup(l)
        for ti in range(4):
            phase1_tile(l, ti)
        for n, r, c, dt in XCH_SPECS:
            P.cc(DT[f"{n}o{l}"], DT[f"{n}i{l}"], dr[f"{n}o{l}"], dr[f"{n}i{l}"], PAIRS)
        phase1_tile(l, 4)
        for ti in range(5 if l == 0 else 4):
            phase2_tile(l, ti, final=(l == L - 1))
    outs_t = [DT["outT"]]
    P.wait_all("sync", outs_t)
    for X in COMPUTE:
        P.wait_all(X, outs_t)
    P.emit()
    nc._wkeys = list(wkeys)
    build.last = (P.n_inst, len(P.dval), {e: len(P.waited[e]) for e in COMPUTE})
    return nc


def _rope_tables(half):
    def tab(dim_half, nfreq_dim):
        inv = (10000.0 ** (-np.arange(0, dim_half, 2, dtype=np.float32) / np.float32(dim_half))).astype(np.float32)
        return inv
    pos = np.arange(half * HALF, (half + 1) * HALF)
    row = (pos // 64).astype(np.float32)
    col = (pos % 64).astype(np.float32)

    def make(hd):
        inv = tab(hd, None)
        nf = hd // 2
        ar = row[None, :] * inv[:, None]
        ac = col[None, :] * inv[:, None]
        cr, sr, cc, sc = np.cos(ar), np.sin(ar), np.cos(ac), np.sin(ac)
        cos = np.concatenate([cr, cr, cc, cc], 0)
        sin = np.concatenate([-sr, sr, -sc, sc], 0)
        cosf = np.ones((2 * hd, TOK), np.float32)
        sinf = np.zeros((2 * hd, TOK), np.float32)
        cosf[:, :HALF] = cos
        sinf[:, :HALF] = sin
        return cosf, sinf
    cg, sg = make(64)
    cm, sm_ = make(32)
    return cg, sg, cm, sm_


def _fm(v):
    return np.ascontiguousarray(v.reshape(-1, 128).T)


def _prep_static(inp):
    f = lambda a: np.ascontiguousarray(np.asarray(a, dtype=np.float32))
    w_in = f(inp["w_in"])
    pg = np.concatenate([np.arange(32, 64), np.arange(0, 32), np.arange(96, 128), np.arange(64, 96)])
    pm = np.concatenate([np.arange(16, 32), np.arange(0, 16), np.arange(48, 64), np.arange(32, 48)])
    cols = []
    for h in range(8):
        cols.append(C_GQ + h * 128 + pg)
    for g in range(2):
        cols.append(C_GK + g * 128 + pg)
    cols.append(C_KR + pm)
    cols = np.concatenate(cols)
    wqb = f(inp["mla_w_qb"])
    wkvb = f(inp["mla_w_kvb"])
    cn = np.concatenate([h * 192 + np.arange(128) for h in range(8)])
    cr = np.concatenate([h * 192 + 128 + np.arange(64) for h in range(8)])
    crr = np.concatenate([h * 192 + 128 + pm for h in range(8)])
    ck = np.concatenate([h * 256 + np.arange(128) for h in range(8)])
    cv = np.concatenate([h * 256 + 128 + np.arange(128) for h in range(8)])
    W = {
        "ada_w": f(inp["ada_w"]), "ffn1_w_gu": f(inp["ffn1_w_gu"]), "ffn1_w_down": f(inp["ffn1_w_down"]),
        "w_in": w_in, "winr": np.ascontiguousarray(w_in[:, :, cols]),
        "wqb_n": np.ascontiguousarray(wqb[:, :, cn]), "wqb_r": np.ascontiguousarray(wqb[:, :, cr]),
        "wqb_rr": np.ascontiguousarray(wqb[:, :, crr]),
        "wkvb_k": np.ascontiguousarray(wkvb[:, :, ck]), "wkvb_v": np.ascontiguousarray(wkvb[:, :, cv]),
        "w_branch_conv": f(inp["w_branch_conv"]), "w_branch_mla": f(inp["w_branch_mla"]),
        "w_branch_gqa": f(inp["w_branch_gqa"]), "w_out": f(inp["w_out"]),
        "ffn2_w_gu": f(inp["ffn2_w_gu"]), "ffn2_w_down": f(inp["ffn2_w_down"]),
    }
    return W


def _smalls(inp, b, half):
    s = np.zeros((128, NS), np.float32)
    for l in range(L):
        o = l * LS
        ab = _fm(np.asarray(inp["ada_b"][l], np.float32))
        s[:, o + S_ADAB:o + S_ADAB + 288] = np.repeat(ab, 2, axis=1)
        s[:, o + S_N1:o + S_N1 + 16] = _fm(np.asarray(inp["ffn1_norm"][l], np.float32))
        s[:, o + S_NM:o + S_NM + 16] = _fm(np.asarray(inp["mix_norm"][l], np.float32))
        s[:, o + S_N2:o + S_N2 + 16] = _fm(np.asarray(inp["ffn2_norm"][l], np.float32))
        s[:, o + S_QN:o + S_QN + 4] = _fm(np.asarray(inp["mla_q_norm"][l], np.float32))
        s[:, o + S_KVN:o + S_KVN + 2] = _fm(np.asarray(inp["mla_kv_norm"][l], np.float32))
        cw = np.asarray(inp["conv_w"][l], np.float32)
        for k in range(3):
            s[:, o + S_CONV + k * 8:o + S_CONV + (k + 1) * 8] = _fm(cw[k])
        s[:, o + S_SINK:o + S_SINK + 8] = np.asarray(inp["gqa_sink"][l], np.float32)[None, :]
    s[:, S_FIN:S_FIN + 16] = _fm(np.asarray(inp["final_norm"], np.float32))
    cv = np.stack([_fm(np.asarray(inp["c"][b], np.float32)), _fm(np.asarray(inp["c_ctx"], np.float32))], axis=2)
    s[:, S_CVEC:S_CVEC + 32] = cv.reshape(128, 32)
    s[:, S_FLAG] = 0.0 if half == 0 else 1.0
    s[:, S_FLAG + 1] = 1.0 if half == 0 else 0.0
    s[:, S_EPS] = EPS * 2048
    s[:, S_EPS + 1] = EPS * 512
    s[:, S_EPS + 2] = EPS * 256
    return s


def _masks(half):
    j = np.arange(128)[:, None]
    i = np.arange(128)[None, :]
    prev = (j >= i).astype(np.float32)
    nxt = (j <= i).astype(np.float32)
    first = prev if half == 1 else np.zeros_like(prev)
    last = nxt if half == 0 else np.zeros_like(nxt)
    m = np.concatenate([np.tile(x, (1, 4)) for x in (prev, nxt, first, last)], axis=1)
    return m.astype(ml_dtypes.bfloat16)


_NC_CACHE = {}


def _get_nc():
    if "nc" not in _NC_CACHE:
        _NC_CACHE["nc"] = build()
    return _NC_CACHE["nc"]


def kernel(**inp):
    W = _prep_static(inp)
    x = np.asarray(inp["x"], np.float32)
    ctx = np.asarray(inp["ctx"], np.float32)
    nc = _get_nc()
    ws = {k: W[n][l] for k, n, l in nc._wkeys}
    ins = []
    for c in range(8):
        b, half = c // 2, c % 2
        cg, sg, cm, sm_ = _rope_tables(half)
        d = dict(ws)
        d["smalls"] = _smalls(inp, b, half)
        d["cosg"], d["sing"], d["cosm"], d["sinm"] = cg, sg, cm, sm_
        d["masks"] = _masks(half)
        d["xT"] = np.ascontiguousarray(x[b, half * HALF:(half + 1) * HALF, :].T)
        d["cT"] = np.ascontiguousarray(ctx[b].T)
        ins.append(d)
    r = run_bass_kernel_spmd(nc, ins, core_ids=list(range(8))).results
    out = np.zeros((4, SEQ, D), np.float32)
    for c in range(8):
        b, half = c // 2, c % 2
        out[b, half * HALF:(half + 1) * HALF, :] = np.asarray(r[c]["outT"]).T
    return out
```

```python
import numpy as np
import ml_dtypes
import concourse.bass as bass
import concourse.mybir as mybir
from concourse.bass_utils import run_bass_kernel_spmd

F32 = mybir.dt.float32
BF16 = mybir.dt.bfloat16
ACT = mybir.ActivationFunctionType
ALU = mybir.AluOpType

COMPUTE = ("tensor", "vector", "scalar", "gpsimd")
ALLENG = ("tensor", "vector", "scalar", "gpsimd", "sync")

D = 2048
NCH = 16
DFF = 5632
FCH = 44
L = 2
SEQ = 4096
HALF = 2048
CTX = 256
TOK = HALF + CTX
EPS = 1e-6
MLA_SCALE = 192 ** -0.5
GQA_SCALE = 128 ** -0.5
C_GB, C_GC, C_V = 0, 1024, 2048
C_CQ, C_CKV, C_KR = 3072, 3584, 3840
C_GQ, C_GK, C_GV = 3904, 4928, 5184
C_GATE = 5440
NKEY = 2 * HALF + CTX
NKC = NKEY // 128
GKEYS = 128 + HALF + 128 + CTX
TILES = [(0, 512), (512, 512), (1024, 512), (1536, 512), (2048, 256)]
SLOT = 4096
NSLOT = 8
LS = 374
S_ADAB, S_N1, S_NM, S_N2, S_QN, S_KVN, S_CONV, S_SINK = 0, 288, 304, 320, 336, 340, 342, 366
S_FIN = 2 * LS
S_CVEC = S_FIN + 16
S_FLAG = S_CVEC + 32
S_EPS = S_FLAG + 2
NS = S_EPS + 3


class T:
    __slots__ = ("ap", "w", "r", "ds", "name")

    def __init__(self, ap, name="", ds=None):
        self.ap = ap
        self.w = None
        self.r = {}
        self.ds = ds
        self.name = name


class Prog:
    def __init__(self, nc):
        self.nc = nc
        self.ops = {e: [] for e in ALLENG}
        self.cnt = {e: 0 for e in COMPUTE}
        self.known = {e: {} for e in ALLENG}
        self.waited = {e: set() for e in COMPUTE}
        self.dval = []
        self.free_ds = []
        self.n_inst = 0

    def new_ds(self):
        if self.free_ds:
            return self.free_ds.pop()
        self.dval.append(0)
        return len(self.dval) - 1

    def _emit_waits(self, X, evs):
        k = self.known[X]
        for ev in evs:
            if ev is None:
                continue
            if ev[0] == 'e':
                if ev[1] == X and X == "tensor":
                    continue
                key, val = ev[1], ev[2]
            else:
                key, val = ev[1] + 1000, self.dval[ev[1]]
            if k.get(key, 0) >= val:
                continue
            k[key] = val
            if ev[0] == 'e':
                self.waited[ev[1]].add(val)
                self.ops[X].append(('w', ev[1], val))
            else:
                self.ops[X].append(('wd', ev[1], val))

    @staticmethod
    def _deps(reads, writes):
        evs = []
        for t in reads:
            evs.append(t.w)
        for t in writes:
            evs.append(t.w)
            evs.extend(t.r.values())
        return evs

    @staticmethod
    def _mark(reads, writes, ev):
        key = ev[1] if ev[0] == 'e' else ev[1] + 1000
        for t in reads:
            t.r[key] = ev
        for t in writes:
            t.w = ev
            t.r = {}

    def op(self, X, fn, reads=(), writes=()):
        self._emit_waits(X, self._deps(reads, writes))
        self.cnt[X] += 1
        seq = self.cnt[X]
        ev = ('e', X, seq)
        self.ops[X].append(('i', fn, seq))
        self._mark(reads, writes, ev)
        self.n_inst += 1
        return ev

    def mm(self, ps, pairs, reads, start=True, stop=True, fresh=None):
        X = "tensor"
        if fresh is None:
            fresh = start
        self._emit_waits(X, self._deps(reads, [ps] if fresh else []))
        n = len(pairs)
        ev = None
        for i, (o, l, r) in enumerate(pairs):
            self.cnt[X] += 1
            seq = self.cnt[X]
            st = start and i == 0
            sp = stop and i == n - 1
            self.ops[X].append(('i', (lambda e, o=o, l=l, r=r, st=st, sp=sp:
                                      e.matmul(o, l, r, start=st, stop=sp)), seq))
            ev = ('e', X, seq)
        self.n_inst += n
        self._mark(reads, [ps], ev)
        return ev

    def dma(self, Q, dst, src, out_ap, in_ap, slow=False):
        dsts = dst if isinstance(dst, (list, tuple)) else [dst]
        srcs = src if isinstance(src, (list, tuple)) else [src]
        self._emit_waits(Q, self._deps(srcs, dsts))
        d0 = dsts[0]
        if d0.ds is None:
            d0.ds = self.new_ds()
        ds = d0.ds
        self.dval[ds] += 16
        ev = ('d', ds, self.dval[ds])
        self.ops[Q].append(('dma', out_ap, in_ap, ds, slow))
        self._mark(srcs, dsts, ev)
        self.n_inst += 1
        return ev

    def cc(self, dst, src, out_ap, in_ap, groups):
        Q = "gpsimd"
        self._emit_waits(Q, self._deps([src], [dst]))
        self.dval.append(1)
        ds = len(self.dval) - 1
        ev = ('d', ds, 1)
        self.ops[Q].append(('cc', out_ap, in_ap, ds, groups))
        self._mark([src], [dst], ev)
        self.n_inst += 1
        return ev

    def barrier(self, tiles):
        evs = []
        for t in tiles:
            evs.append(t.w)
            evs.extend(t.r.values())
        for X in ALLENG:
            if X == "gpsimd":
                continue
            self._emit_waits(X, evs)

    def wait_all(self, X, tiles):
        evs = []
        for t in tiles:
            evs.append(t.w)
            evs.extend(t.r.values())
        self._emit_waits(X, evs)

    def emit(self):
        nc = self.nc
        esem = {e: nc.alloc_semaphore(name=f"es_{e}") for e in COMPUTE}
        dsems = [nc.alloc_semaphore(name=f"ds_{i}") for i in range(len(self.dval))]
        rank = {}
        for e in COMPUTE:
            rank[e] = {q: i + 1 for i, q in enumerate(sorted(self.waited[e]))}

        def run(X, eng):
            myrank = rank.get(X, {})
            for o in self.ops[X]:
                k = o[0]
                if k == 'i':
                    ins = o[1](eng)
                    if o[2] in myrank:
                        ins.then_inc(esem[X], 1)
                elif k == 'w':
                    eng.wait_ge(esem[o[1]], rank[o[1]][o[2]])
                elif k == 'wd':
                    eng.wait_ge(dsems[o[1]], o[2])
                elif k == 'cc':
                    eng.collective_compute("AllGather", ALU.bypass, replica_groups=o[4],
                                           ins=[o[2]], outs=[o[1]]).then_inc(dsems[o[3]])
                elif o[4]:
                    eng.dma_start(out=o[1], in_=o[2], allow_slow_non_contiguous=True).then_inc(dsems[o[3]], 16)
                else:
                    eng.dma_start(out=o[1], in_=o[2]).then_inc(dsems[o[3]], 16)

        with nc.Block() as block:
            @block.tensor
            def _(e):
                run("tensor", e)

            @block.vector
            def _(e):
                run("vector", e)

            @block.scalar
            def _(e):
                run("scalar", e)

            @block.gpsimd
            def _(e):
                run("gpsimd", e)

            @block.sync
            def _(e):
                run("sync", e)


class Arena:
    def __init__(self, P, ap, nbytes):
        self.P = P
        self.ap = ap
        self.nbytes = nbytes
        self.top = 0
        self.live = []

    def _carve(self, nelem, dt):
        bpe = 4 if dt == F32 else 2
        nb = (nelem * bpe + 63) // 64 * 64
        assert self.top + nb <= self.nbytes, f"arena overflow {self.top}+{nb}>{self.nbytes}"
        a = self.ap[:, self.top // 4:(self.top + nb) // 4]
        if dt != F32:
            a = a.bitcast(dt)
        self.top += nb
        return a[:, 0:nelem]

    def tile(self, n, dt, name=""):
        t = T(self._carve(n, dt), name)
        self.live.append(t)
        return t

    def chunks(self, k, n, dt, name="", shared_ds=False):
        a = self._carve(k * n, dt)
        ds = self.P.new_ds() if shared_ds else None
        ts = [T(a[:, i * n:(i + 1) * n], f"{name}{i}", ds) for i in range(k)]
        self.live.extend(ts)
        return ts, a

    def mark(self):
        return (self.top, len(self.live))

    def release(self, m):
        dead = self.live[m[1]:]
        self.P.barrier(dead)
        freed = set()
        for t in dead:
            if t.ds is not None and t.ds not in freed:
                freed.add(t.ds)
                self.P.free_ds.append(t.ds)
            t.ds = None
        del self.live[m[1]:]
        self.top = m[0]


class Rot:
    def __init__(self, ts):
        self.ts = ts
        self.i = 0

    def next(self):
        t = self.ts[self.i % len(self.ts)]
        self.i += 1
        return t


WEIGHT_SPECS = [
    ("ada_w", [L, D, 9 * D]), ("ffn1_w_gu", [L, D, 2 * DFF]), ("ffn1_w_down", [L, DFF, D]),
    ("w_in", [L, D, 11584]), ("winr", [L, D, 1344]),
    ("wqb_n", [L, 512, 1024]), ("wqb_r", [L, 512, 512]), ("wqb_rr", [L, 512, 512]),
    ("wkvb_k", [L, 256, 1024]), ("wkvb_v", [L, 256, 1024]),
    ("w_branch_conv", [L, 1024, D]), ("w_branch_mla", [L, 1024, D]), ("w_branch_gqa", [L, 1024, D]),
    ("w_out", [L, D, D]), ("ffn2_w_gu", [L, D, 2 * DFF]), ("ffn2_w_down", [L, DFF, D]),
]
SCRATCH_SPECS = [
    ("HX", [D, TOK], F32), ("XM", [D, TOK], BF16),
    ("GKE", [2, 128, GKEYS], BF16), ("GVE", [128, 20, 256], BF16),
    ("UE", [1024, HALF + 2], F32), ("UC", [1024, CTX + 2], F32),
    ("CKN", [1024, CTX], BF16), ("CKR", [64, CTX], BF16), ("CV", [8, 128, 2, 128], BF16),
]
XCH_SPECS = [
    ("XU", 1024, 2, F32), ("XG", 256, 256, BF16), ("XGV", 256, 256, BF16), ("XKR", 64, HALF, BF16),
    ("XKN0", 256, HALF, BF16), ("XKN1", 256, HALF, BF16), ("XKN2", 256, HALF, BF16), ("XKN3", 256, HALF, BF16),
    ("XV0", 256, HALF, BF16), ("XV1", 256, HALF, BF16), ("XV2", 256, HALF, BF16), ("XV3", 256, HALF, BF16),
]
PAIRS = [[0, 1], [2, 3], [4, 5], [6, 7]]
def build(mode="FUSED"):
    nc = bass.Bass("TRN2", target_bir_lowering=False)
    P = Prog(nc)
    dr = {}

    def din(name, shape, dt=F32):
        dr[name] = nc.dram_tensor(name, shape, dt, kind="ExternalInput").ap()

    wshape = dict(WEIGHT_SPECS)
    wkeys = []
    din("smalls", [128, NS])
    din("cosg", [128, TOK]); din("sing", [128, TOK])
    din("cosm", [64, TOK]); din("sinm", [64, TOK])
    din("masks", [128, 4 * 512 + 128], BF16)
    din("xT", [D, HALF]); din("cT", [D, CTX])
    for n, sh, dt in SCRATCH_SPECS:
        dr[n] = nc.dram_tensor(n, sh, dt).ap()
    for l in range(L):
        for n, r, c, dt in XCH_SPECS:
            dr[f"{n}i{l}"] = nc.dram_tensor(f"{n}i{l}", [r, c], dt).ap()
            dr[f"{n}o{l}"] = nc.dram_tensor(f"{n}o{l}", [2 * r, c], dt).ap()
    dr["outT"] = nc.dram_tensor("outT", [D, HALF], F32, kind="ExternalOutput").ap()
    DT = {k: T(v, k) for k, v in dr.items()}
    WT = T(None, "weights")

    ARENA_BYTES = 204 * 1024
    arena_ap = nc.alloc_sbuf_tensor("arena", [128, ARENA_BYTES // 4], F32).ap()
    A = Arena(P, arena_ap, ARENA_BYTES)
    smalls = A.tile(NS, F32, "smalls")
    modL = [A.tile(288, F32, "mod0"), A.tile(288, F32, "mod1")]
    curl = [0]

    def MOD():
        return modL[curl[0]]
    CA = A.tile(96, F32, "CA")
    CG = A.tile(96, F32, "CG")
    misc = A.tile(64, F32, "misc")
    scb = A.tile(32, BF16, "scb")
    ones = A.tile(128, BF16, "ones")
    masks = A.tile(4 * 512 + 128, BF16, "masks")
    sinkrow = A.tile(2 * 512, F32, "sinkrow")
    ring = [A.tile(SLOT, BF16, f"ring{i}") for i in range(NSLOT)]
    ringi = [0]
    hx, hx_all = A.chunks(16, 512, F32, "hx", shared_ds=True)
    HA = Arena(P, hx_all, 16 * 512 * 4)
    n_rstd = A.tile(512, F32, "n_rstd")
    n_tmp = Rot([A.tile(512, F32, f"n_tmp{i}") for i in range(3)])
    psum = [T(nc.alloc_psum_tensor(f"ps{i}", [128, 512], F32).ap(), f"ps{i}") for i in range(8)]
    ps_o, ps_d = psum[0], psum[1]
    psr = Rot(psum[2:])

    def sm(c0, c1=None):
        return smalls.ap[:, c0:(c0 + 1 if c1 is None else c1)]

    def wload(name, l, kc, c0, ncols, k0=0):
        slot = ring[ringi[0] % NSLOT]
        ringi[0] += 1
        view = slot.ap[:, 0:kc * ncols].rearrange("p (k n) -> p k n", k=kc)
        key = f"{name}@{l}"
        if key not in dr:
            dr[key] = nc.dram_tensor(f"{name}_L{l}", wshape[name][1:], F32, kind="ExternalInput").ap()
            wkeys.append((f"{name}_L{l}", name, l))
        srcv = dr[key].rearrange("(c p) n -> p c n", p=128)[:, k0:k0 + kc, c0:c0 + ncols]
        P.dma("gpsimd", slot, WT, view, srcv)
        return slot, view

    P.dma("sync", smalls, DT["smalls"], smalls.ap, dr["smalls"])
    P.dma("sync", masks, DT["masks"], masks.ap, dr["masks"])
    P.op("vector", lambda e: e.memset(ones.ap, 1.0), writes=[ones])

    ada_next = [0, 0]
    silu_done = [False]

    def ada_slabs(l, count, banks):
        b = l * LS
        if not silu_done[0]:
            P.op("scalar", lambda e: e.activation(scb.ap, sm(S_CVEC, S_CVEC + 32), ACT.Silu),
                 reads=[smalls], writes=[scb])
            silu_done[0] = True
        s0 = ada_next[l]
        if s0 >= 72:
            return
        pm = banks[(s0 // max(count, 1)) % len(banks)]
        done = 0
        for _ in range(count):
            s = ada_next[l]
            if s >= 72:
                break
            ada_next[l] += 1
            slot, v = wload("ada_w", l, 16, s * 256, 256)
            for q in range(2):
                n = s * 2 + q
                P.mm(pm, [(pm.ap[:, 2 * n:2 * n + 2], v[:, j, q * 128:(q + 1) * 128], scb.ap[:, 2 * j:2 * j + 2])
                          for j in range(16)], reads=[slot, scb], start=True, stop=True, fresh=(done == 0 and q == 0))
            done += 1
        c0, c1 = 4 * s0, 4 * (s0 + done)
        P.op("vector", lambda e, pm=pm, c0=c0, c1=c1, l=l, b=b: e.tensor_tensor(
            modL[l].ap[:, c0:c1], pm.ap[:, c0:c1], sm(b + S_ADAB + c0, b + S_ADAB + c1), op=ALU.add),
            reads=[pm, smalls, modL[l]], writes=[modL[l]])

    def layer_setup(l):
        b = l * LS
        while ada_next[l] < 72:
            ada_slabs(l, 2, [psum[6], psum[7]])
        curl[0] = l
        mod = MOD()
        for k, sn in enumerate((S_N1, S_NM, S_N2)):
            for w in range(2):
                m_scale = mod.ap.rearrange("p (m j w) -> p m j w", m=9, j=16)[:, 3 * k + 1, :, w]
                outv = CA.ap.rearrange("p (k j w) -> p k j w", k=3, j=16)[:, k, :, w]
                P.op("vector", lambda e, o=outv, i=m_scale, sn=sn: e.scalar_tensor_tensor(
                    o, i, 1.0, sm(b + sn, b + sn + 16), op0=ALU.add, op1=ALU.mult),
                    reads=[mod, smalls], writes=[CA])
                m_gate = mod.ap.rearrange("p (m j w) -> p m j w", m=9, j=16)[:, 3 * k + 2, :, w]
                outg = CG.ap.rearrange("p (k j w) -> p k j w", k=3, j=16)[:, k, :, w]
                gs = 1.0 if k == 1 else 0.5
                P.op("vector", lambda e, o=outg, i=m_gate, gs=gs: e.tensor_scalar(
                    o, i, gs, None, op0=ALU.mult), reads=[mod], writes=[CG])
        P.op("vector", lambda e: e.tensor_scalar(CA.ap, CA.ap, float(np.sqrt(D)), None, op0=ALU.mult),
             reads=[CA], writes=[CA])
        P.op("vector", lambda e: e.tensor_scalar(misc.ap[:, 0:4], sm(b + S_QN, b + S_QN + 4), float(np.sqrt(512)), None,
                                                 op0=ALU.mult), reads=[smalls], writes=[misc])
        P.op("vector", lambda e: e.tensor_scalar(misc.ap[:, 4:6], sm(b + S_KVN, b + S_KVN + 2), 16.0, None,
                                                 op0=ALU.mult), reads=[smalls, misc], writes=[misc])
        P.op("vector", lambda e: e.tensor_scalar(misc.ap[:, 8:24], sm(S_FIN, S_FIN + 16), float(np.sqrt(D)), None,
                                                 op0=ALU.mult), reads=[smalls, misc], writes=[misc])
        P.op("scalar", lambda e: e.activation(misc.ap[:, 24:32], sm(b + S_SINK, b + S_SINK + 8), ACT.Exp),
             reads=[smalls, misc], writes=[misc])
        for h in range(8):
            P.op("vector", lambda e, h=h: e.tensor_scalar(
                sinkrow.ap[:, h * 128:(h + 1) * 128], ones.ap[:, 0:128], misc.ap[:, 24 + h:25 + h], None, op0=ALU.mult),
                reads=[ones, misc, sinkrow], writes=[sinkrow])

    def Acol(k, j, w):
        i = (k * 16 + j) * 2 + w
        return CA.ap[:, i:i + 1]

    def Gcol(k, j, w):
        i = (k * 16 + j) * 2 + w
        return CG.ap[:, i:i + 1]

    def Bcol(m, j, w):
        i = (m * 16 + j) * 2 + w
        return MOD().ap[:, i:i + 1]

    def rmsnorm(xs, N, epscol, a_fn, b_fn, outs, consts, sq_tiles=None):
        sqs = outs if sq_tiles is None else sq_tiles
        pss = psr.next()
        nchunk = len(xs)
        for j, x in enumerate(xs):
            sq = sqs[j]
            if j % 2 == 0:
                P.op("scalar", lambda e, sq=sq, x=x: e.activation(sq.ap[:, :N], x.ap[:, :N], ACT.Square),
                     reads=[x], writes=[sq])
            else:
                P.op("vector", lambda e, sq=sq, x=x: e.tensor_tensor(sq.ap[:, :N], x.ap[:, :N], x.ap[:, :N], op=ALU.mult),
                     reads=[x], writes=[sq])
        for j in range(nchunk):
            sq = sqs[j]
            P.mm(pss, [(pss.ap[:, :N], ones.ap, sq.ap[:, :N])], reads=[ones, sq], start=(j == 0), stop=(j == nchunk - 1))
        rstd = n_rstd
        P.op("scalar", lambda e: e.activation(rstd.ap[:, :N], pss.ap[:, :N], ACT.Sqrt, bias=sm(epscol), scale=1.0),
             reads=[pss, smalls], writes=[rstd])
        P.op("vector", lambda e: e.reciprocal(rstd.ap[:, :N], rstd.ap[:, :N]), reads=[rstd], writes=[rstd])
        for j, x in enumerate(xs):
            o = outs[j]
            aj = a_fn(j)
            if b_fn is None:
                P.op("vector", lambda e, o=o, x=x, aj=aj: e.scalar_tensor_tensor(
                    o.ap[:, :N], x.ap[:, :N], aj, rstd.ap[:, :N], op0=ALU.mult, op1=ALU.mult),
                    reads=[x, rstd] + consts, writes=[o])
            else:
                bj = b_fn(j)
                t = n_tmp.next()
                P.op("vector", lambda e, t=t, x=x, aj=aj: e.scalar_tensor_tensor(
                    t.ap[:, :N], x.ap[:, :N], aj, rstd.ap[:, :N], op0=ALU.mult, op1=ALU.mult),
                    reads=[x, rstd] + consts, writes=[t])
                P.op("scalar", lambda e, t=t, o=o, bj=bj: e.activation(
                    o.ap[:, :N], t.ap[:, :N], ACT.Identity, bias=bj, scale=1.0),
                    reads=[t] + consts, writes=[o])

    def ffn(l, which, xn, N, w):
        m = A.mark()
        k = 0 if which == 1 else 2
        gu = "ffn1_w_gu" if which == 1 else "ffn2_w_gu"
        dn = "ffn1_w_down" if which == 1 else "ffn2_w_down"
        h, _ = A.chunks(FCH, 512, BF16, "h")
        sgs = Rot([A.tile(512, F32, "sg") for _ in range(2)])
        gb, ub = psum[0:4], psum[4:8]
        for s in range(11):
            for banks, c0 in ((gb, 0), (ub, DFF)):
                for kh in range(2):
                    slot, v = wload(gu, l, 8, c0 + s * 512, 512, k0=kh * 8)
                    for q in range(4):
                        p = banks[q]
                        P.mm(p, [(p.ap[:, :N], v[:, j, q * 128:(q + 1) * 128], xn[kh * 8 + j].ap[:, :N]) for j in range(8)],
                             reads=[slot] + xn[kh * 8:(kh + 1) * 8], start=(kh == 0), stop=(kh == 1))
            for q in range(4):
                fc = s * 4 + q
                pg, pu = gb[q], ub[q]
                sg = sgs.next()
                P.op("scalar", lambda e, sg=sg, pg=pg: e.activation(sg.ap[:, :N], pg.ap[:, :N], ACT.Silu),
                     reads=[pg], writes=[sg])
                P.op("vector", lambda e, sg=sg, pu=pu, fc=fc: e.tensor_tensor(
                    h[fc].ap[:, :N], sg.ap[:, :N], pu.ap[:, :N], op=ALU.mult), reads=[sg, pu], writes=[h[fc]])
        for dg in range(4):
            banks = psum[0:4] if dg % 2 == 0 else psum[4:8]
            f0 = 0
            sizes = [8, 8, 8, 8, 8, 4]
            for si, nf in enumerate(sizes):
                slot, v = wload(dn, l, nf, dg * 512, 512, k0=f0)
                for q in range(4):
                    p = banks[q]
                    P.mm(p, [(p.ap[:, :N], v[:, f, q * 128:(q + 1) * 128], h[f0 + f].ap[:, :N]) for f in range(nf)],
                         reads=[slot] + h[f0:f0 + nf], start=(si == 0), stop=(si == len(sizes) - 1))
                f0 += nf
            for q in range(4):
                dc = dg * 4 + q
                p = banks[q]
                P.op("vector", lambda e, p=p, dc=dc: e.scalar_tensor_tensor(
                    hx[dc].ap[:, :N], p.ap[:, :N], Gcol(k, dc, w), hx[dc].ap[:, :N], op0=ALU.mult, op1=ALU.add),
                    reads=[p, hx[dc], CG], writes=[hx[dc]])
        A.release(m)

    def load_hx(srcname, t0, N):
        v = dr[srcname].rearrange("(c p) t -> p c t", p=128)[:, :, t0:t0 + N]
        P.dma("sync", hx, DT[srcname], hx_all.rearrange("p (c t) -> p c t", c=16)[:, :, 0:N], v)

    def store_hx(t0, N):
        v = dr["HX"].rearrange("(c p) t -> p c t", p=128)[:, :, t0:t0 + N]
        P.dma("sync", DT["HX"], hx, v, hx_all.rearrange("p (c t) -> p c t", c=16)[:, :, 0:N])

    def rope_combine(p1, p2, cs, sn, out_ap, np_, N, tmps, out_t, eng_add="vector"):
        t1 = tmps.next()
        t2 = tmps.next()
        P.op("vector", lambda e: e.tensor_tensor(t1.ap[0:np_, :N], p1.ap[0:np_, :N], cs.ap[0:np_, :N], op=ALU.mult),
             reads=[p1, cs], writes=[t1])
        P.op("vector", lambda e: e.tensor_tensor(t2.ap[0:np_, :N], p2.ap[0:np_, :N], sn.ap[0:np_, :N], op=ALU.mult),
             reads=[p2, sn], writes=[t2])
        P.op(eng_add, lambda e: e.tensor_tensor(out_ap, t1.ap[0:np_, :N], t2.ap[0:np_, :N], op=ALU.add),
             reads=[t1, t2], writes=[out_t])

    accs = [(psum[0], psum[1]), (psum[2], psum[3])]
    ring4 = Rot(psum[4:7])

    def attn_stream(groups, PT):
        flat = []
        for gi, g in enumerate(groups):
            n = len(g["items"])
            for i, it in enumerate(g["items"]):
                flat.append((gi, i == 0, i == n - 1, g, it))

        def issue_S(k):
            it = flat[k][4]
            p = ring4.next()
            pairs = it["s_pairs"](p)
            rd = it["s_reads"]()
            if it["mask"] is not None:
                mk = it["mask"]
                pairs = pairs + [(p.ap, masks.ap[:, 2048:2176], masks.ap[:, mk * 512:(mk + 1) * 512])]
                rd = rd + [masks]
            P.mm(p, pairs, reads=rd)
            return p
        LOOK = 2
        pend = [issue_S(k) for k in range(min(LOOK, len(flat)))]
        for k in range(len(flat)):
            gi, fi, la, g, it = flat[k]
            if fi and g.get("prefetch") is not None:
                g["prefetch"]()
            p = pend.pop(0)
            if k + LOOK < len(flat):
                pend.append(issue_S(k + LOOK))
            N = g["N"]
            po, pd = accs[gi % 2]
            pt = PT.next()
            P.op("scalar", lambda e, pt=pt, p=p, N=N, sc=g["scale"]: e.activation(
                pt.ap[:, :N], p.ap[:, :N], ACT.Exp, scale=sc), reads=[p], writes=[pt])
            P.mm(po, [(po.ap[:, :N], it["v_lhsT"](), pt.ap[:, :N])], reads=it["v_reads"]() + [pt], start=fi, stop=la)
            pair = g.get("pair_den")
            if pair is None:
                P.mm(pd, [(pd.ap[:, :N], ones.ap, pt.ap[:, :N])], reads=[ones, pt], start=fi, stop=la)
            else:
                ci = it["ci"]
                if pair.get("pending") is not None:
                    d2p, stp = pair["pending"]
                    pair["pending"] = None
                    P.mm(pd, [(pd.ap[:, :N], ones.ap, d2p.ap[:, :N])], reads=[ones, d2p], start=stp, stop=False)
                if ci % 2 == 0 and not la:
                    pair["held"] = pt
                else:
                    if ci % 2 == 1:
                        p0 = pair["held"]
                        d2 = pair["buf"].next()
                        P.op("vector", lambda e, d2=d2, p0=p0, pt=pt, N=N: e.tensor_tensor(
                            d2.ap[:, :N], p0.ap[:, :N], pt.ap[:, :N], op=ALU.add), reads=[p0, pt], writes=[d2])
                    else:
                        d2 = pt
                    if la:
                        P.mm(pd, [(pd.ap[:, :N], ones.ap, d2.ap[:, :N])], reads=[ones, d2], start=(ci <= 1), stop=True)
                    else:
                        pair["pending"] = (d2, ci <= 1)
            if la:
                g["finish"](po, pd)
                if g.get("post") is not None:
                    g["post"]()

    def phase1_tile(l, ti):
        t0, N = TILES[ti]
        w = 1 if ti == 4 else 0
        mT = A.mark()
        xn, xn_all = A.chunks(16, 512, BF16, "xn")
        if l == 0:
            if w == 0:
                v = dr["xT"].rearrange("(c p) t -> p c t", p=128)[:, :, t0:t0 + N]
                P.dma("sync", hx, DT["xT"], hx_all.rearrange("p (c t) -> p c t", c=16)[:, :, 0:N], v)
            else:
                v = dr["cT"].rearrange("(c p) t -> p c t", p=128)
                P.dma("sync", hx, DT["cT"], hx_all.rearrange("p (c t) -> p c t", c=16)[:, :, 0:N], v)
        else:
            load_hx("HX", t0, N)
        rmsnorm(hx, N, S_EPS, lambda j: Acol(0, j, w), lambda j: Bcol(0, j, w), xn, [CA, MOD()])
        ffn(l, 1, xn, N, w)
        store_hx(t0, N)
        xm = xn
        rmsnorm(hx, N, S_EPS, lambda j: Acol(1, j, w), lambda j: Bcol(3, j, w), xm, [CA, MOD()])
        xmv = dr["XM"].rearrange("(c p) t -> p c t", p=128)[:, :, t0:t0 + N]
        P.dma("sync", DT["XM"], xm, xmv, xn_all.rearrange("p (c t) -> p c t", c=16)[:, :, 0:N])
        mJ = A.mark()
        stg = Rot([A.tile(512, BF16, "stg") for _ in range(3)])
        ft = Rot([A.tile(512, F32, "ft") for _ in range(4)])
        ust = Rot([A.tile(512, F32, "ust") for _ in range(2)])
        cosg = A.tile(512, F32, "cosg"); sing = A.tile(512, F32, "sing")
        cosm = A.tile(512, F32, "cosm"); sinm = A.tile(512, F32, "sinm")
        P.dma("sync", cosg, DT["cosg"], cosg.ap[:, :N], dr["cosg"][:, t0:t0 + N])
        P.dma("sync", sing, DT["sing"], sing.ap[:, :N], dr["sing"][:, t0:t0 + N])
        P.dma("sync", cosm, DT["cosm"], cosm.ap[0:64, :N], dr["cosm"][:, t0:t0 + N])
        P.dma("sync", sinm, DT["sinm"], sinm.ap[0:64, :N], dr["sinm"][:, t0:t0 + N])
        for s in range(4):
            gs, gv = wload("w_in", l, 16, C_GC + s * 256, 256)
            vs, vv = wload("w_in", l, 16, C_V + s * 256, 256)
            for q in range(2):
                jc = s * 2 + q
                pg = psr.next()
                P.mm(pg, [(pg.ap[:, :N], gv[:, j, q * 128:(q + 1) * 128], xm[j].ap[:, :N]) for j in range(16)], reads=[gs] + xm)
                pv = psr.next()
                P.mm(pv, [(pv.ap[:, :N], vv[:, j, q * 128:(q + 1) * 128], xm[j].ap[:, :N]) for j in range(16)], reads=[vs] + xm)
                t = ft.next()
                P.op("scalar", lambda e, t=t, pg=pg: e.activation(t.ap[:, :N], pg.ap[:, :N], ACT.Copy), reads=[pg], writes=[t])
                u = ust.next()
                P.op("vector", lambda e, u=u, t=t, pv=pv: e.tensor_tensor(u.ap[:, :N], t.ap[:, :N], pv.ap[:, :N], op=ALU.mult),
                     reads=[t, pv], writes=[u])
                if w == 0:
                    P.dma("sync", DT["UE"], u, dr["UE"][jc * 128:(jc + 1) * 128, 1 + t0:1 + t0 + N], u.ap[:, :N])
                    if ti == 0:
                        P.dma("sync", DT[f"XUi{l}"], u, dr[f"XUi{l}"][jc * 128:(jc + 1) * 128, 0:1], u.ap[:, 0:1], slow=True)
                    if ti == 3:
                        P.dma("sync", DT[f"XUi{l}"], u, dr[f"XUi{l}"][jc * 128:(jc + 1) * 128, 1:2], u.ap[:, N - 1:N], slow=True)
                else:
                    P.dma("sync", DT["UC"], u, dr["UC"][jc * 128:(jc + 1) * 128, 1:1 + N], u.ap[:, :N])
        ks, kv_ = wload("w_in", l, 16, C_CKV, 256)
        krs, krv = wload("w_in", l, 16, C_KR, 64)
        rs, rv = wload("winr", l, 16, 1280, 64)
        ckv, _ = A.chunks(2, 512, F32, "ckv")
        ckvn, _ = A.chunks(2, 512, BF16, "ckvn")
        for j2 in range(2):
            p = psr.next()
            P.mm(p, [(p.ap[:, :N], kv_[:, j, j2 * 128:(j2 + 1) * 128], xm[j].ap[:, :N]) for j in range(16)], reads=[ks] + xm)
            P.op("scalar", lambda e, p=p, j2=j2: e.activation(ckv[j2].ap[:, :N], p.ap[:, :N], ACT.Copy), reads=[p], writes=[ckv[j2]])
        rmsnorm(ckv, N, S_EPS + 2, lambda j: misc.ap[:, 4 + j:5 + j], None, ckvn, [misc])
        key0 = t0 if w == 0 else 0
        p1 = psr.next()
        P.mm(p1, [(p1.ap[0:64, :N], krv[:, j, 0:64], xm[j].ap[:, :N]) for j in range(16)], reads=[krs] + xm)
        p2 = psr.next()
        P.mm(p2, [(p2.ap[0:64, :N], rv[:, j, 0:64], xm[j].ap[:, :N]) for j in range(16)], reads=[rs] + xm)
        st = stg.next()
        rope_combine(p1, p2, cosm, sinm, st.ap[0:64, :N], 64, N, ft, st)
        krn = f"XKRi{l}" if w == 0 else "CKR"
        P.dma("sync", DT[krn], st, dr[krn][:, key0:key0 + N], st.ap[0:64, :N])
        kks, kkv = wload("wkvb_k", l, 2, 0, 1024)
        for h in range(8):
            p = psr.next()
            P.mm(p, [(p.ap[:, :N], kkv[:, j2, h * 128:(h + 1) * 128], ckvn[j2].ap[:, :N]) for j2 in range(2)], reads=[kks] + ckvn)
            st = stg.next()
            P.op("scalar", lambda e, st=st, p=p: e.activation(st.ap[:, :N], p.ap[:, :N], ACT.Copy), reads=[p], writes=[st])
            if w == 0:
                knn = f"XKN{h // 2}i{l}"
                P.dma("sync", DT[knn], st, dr[knn][(h % 2) * 128:(h % 2 + 1) * 128, key0:key0 + N], st.ap[:, :N])
            else:
                P.dma("sync", DT["CKN"], st, dr["CKN"][h * 128:(h + 1) * 128, key0:key0 + N], st.ap[:, :N])
        vvs, vvv = wload("wkvb_v", l, 2, 0, 1024)
        for tb in range(N // 128):
            c = key0 // 128 + tb
            for hh in range(2):
                p = psr.next()
                P.mm(p, [(p.ap[:, 0:512], ckvn[j2].ap[:, tb * 128:(tb + 1) * 128], vvv[:, j2, hh * 512:(hh + 1) * 512])
                         for j2 in range(2)], reads=[vvs] + ckvn)
                st = stg.next()
                P.op("vector", lambda e, st=st, p=p: e.tensor_copy(st.ap[:, 0:512], p.ap[:, 0:512]), reads=[p], writes=[st])
                if w == 0:
                    for q2 in range(2):
                        vn = f"XV{hh * 2 + q2}i{l}"
                        vdst = dr[vn].rearrange("(h p) (c d) -> h p c d", h=2, d=128)[:, :, c, :]
                        P.dma("sync", DT[vn], st, vdst.rearrange("h p d -> p h d"),
                              st.ap[:, q2 * 256:(q2 + 1) * 256].rearrange("p (h d) -> p h d", h=2))
                else:
                    vdst = dr["CV"][hh * 4:(hh + 1) * 4, :, c, :]
                    P.dma("sync", DT["CV"], st, vdst.rearrange("h p d -> p h d"),
                          st.ap[:, 0:512].rearrange("p (h d) -> p h d", h=4))
        gks, gkv = wload("w_in", l, 16, C_GK, 256)
        gvs, gvv = wload("w_in", l, 16, C_GV, 256)
        rs, rv = wload("winr", l, 16, 1024, 256)
        gkey0 = 128 + t0 if w == 0 else 128 + HALF + 128
        for g in range(2):
            p1 = psr.next()
            P.mm(p1, [(p1.ap[:, :N], gkv[:, j, g * 128:(g + 1) * 128], xm[j].ap[:, :N]) for j in range(16)], reads=[gks] + xm)
            p2 = psr.next()
            P.mm(p2, [(p2.ap[:, :N], rv[:, j, g * 128:(g + 1) * 128], xm[j].ap[:, :N]) for j in range(16)], reads=[rs] + xm)
            st = stg.next()
            rope_combine(p1, p2, cosg, sing, st.ap[:, :N], 128, N, ft, st)
            P.dma("sync", DT["GKE"], st, dr["GKE"][g, :, gkey0:gkey0 + N], st.ap[:, :N])
            if w == 0 and ti == 0:
                P.dma("sync", DT[f"XGi{l}"], st, dr[f"XGi{l}"][g * 128:(g + 1) * 128, 0:128], st.ap[:, 0:128])
            if w == 0 and ti == 3:
                P.dma("sync", DT[f"XGi{l}"], st, dr[f"XGi{l}"][g * 128:(g + 1) * 128, 128:256], st.ap[:, N - 128:N])
        for tb in range(N // 128):
            c = gkey0 // 128 + tb
            p = psr.next()
            P.mm(p, [(p.ap[:, 0:256], xm[j].ap[:, tb * 128:(tb + 1) * 128], gvv[:, j, 0:256]) for j in range(16)], reads=[gvs] + xm)
            st = stg.next()
            P.op("vector", lambda e, st=st, p=p: e.tensor_copy(st.ap[:, 0:256], p.ap[:, 0:256]), reads=[p], writes=[st])
            P.dma("sync", DT["GVE"], st, dr["GVE"][:, c, :], st.ap[:, 0:256])
            if w == 0 and ti == 0 and tb == 0:
                P.dma("sync", DT[f"XGVi{l}"], st, dr[f"XGVi{l}"][0:128, :], st.ap[:, 0:256])
            if w == 0 and ti == 3 and tb == N // 128 - 1:
                P.dma("sync", DT[f"XGVi{l}"], st, dr[f"XGVi{l}"][128:256, :], st.ap[:, 0:256])
        A.release(mJ)
        A.release(mT)

    def phase2_tile(l, ti, final):
        t0, N = TILES[ti]
        w = 1 if ti == 4 else 0
        mT = A.mark()
        xm, xm_all = A.chunks(16, 512, BF16, "xm", shared_ds=True)
        yc, _ = A.chunks(8, 512, BF16, "yc")
        ymla, _ = A.chunks(8, 512, BF16, "ymla")
        ygqa, ygqa_all = A.chunks(8, 512, BF16, "ygqa")
        P.dma("sync", xm, DT["XM"], xm_all.rearrange("p (c t) -> p c t", c=16)[:, :, 0:N], dr["XM"].rearrange("(c p) t -> p c t", p=128)[:, :, t0:t0 + N])
        m = A.mark()
        U, U_all = A.chunks(8, 516, F32, "U", shared_ds=True)
        if w == 0:
            uv = dr["UE"].rearrange("(c p) t -> p c t", p=128)[:, :, t0:t0 + N + 2]
        else:
            uv = dr["UC"].rearrange("(c p) t -> p c t", p=128)
        Uv = U_all.rearrange("p (c t) -> p c t", c=8)[:, :, 0:N + 2]
        P.dma("sync", U, DT["UE" if w == 0 else "UC"], Uv, uv)
        U3 = U_all.rearrange("p (c t) -> p c t", c=8)
        if w == 1:
            P.op("vector", lambda e: e.memset(U3[:, :, 0:1], 0.0), writes=U)
            P.op("vector", lambda e: e.memset(U3[:, :, N + 1:N + 2], 0.0), writes=U)
        else:
            xuo = dr[f"XUo{l}"].rearrange("(r c p) t -> r p c t", r=2, p=128)
            if t0 == 0:
                P.dma("sync", U, DT[f"XUo{l}"], U3[:, :, 0:1], xuo[0][:, :, 1:2], slow=True)
                P.op("vector", lambda e: e.tensor_scalar(U3[:, :, 0:1], U3[:, :, 0:1], sm(S_FLAG), None, op0=ALU.mult),
                     reads=[smalls] + U, writes=U)
            if t0 + N == HALF:
                P.dma("sync", U, DT[f"XUo{l}"], U3[:, :, N + 1:N + 2], xuo[1][:, :, 0:1], slow=True)
                P.op("vector", lambda e: e.tensor_scalar(U3[:, :, N + 1:N + 2], U3[:, :, N + 1:N + 2], sm(S_FLAG + 1), None, op0=ALU.mult),
                     reads=[smalls] + U, writes=U)
        cy = Rot([A.tile(512, F32, "cy") for _ in range(2)])
        cb = l * LS + S_CONV
        for s in range(4):
            gs, gv = wload("w_in", l, 16, C_GB + s * 256, 256)
            for q in range(2):
                jc = s * 2 + q
                pg = psr.next()
                P.mm(pg, [(pg.ap[:, :N], gv[:, j, q * 128:(q + 1) * 128], xm[j].ap[:, :N]) for j in range(16)], reads=[gs] + xm)
                y = cy.next()
                u = U[jc]
                P.op("vector", lambda e, y=y, u=u, jc=jc: e.tensor_scalar(
                    y.ap[:, :N], u.ap[:, 0:N], sm(cb + jc), None, op0=ALU.mult), reads=[u, smalls], writes=[y])
                P.op("vector", lambda e, y=y, u=u, jc=jc: e.scalar_tensor_tensor(
                    y.ap[:, :N], u.ap[:, 1:N + 1], sm(cb + 8 + jc), y.ap[:, :N], op0=ALU.mult, op1=ALU.add),
                    reads=[u, smalls, y], writes=[y])
                P.op("vector", lambda e, y=y, u=u, jc=jc: e.scalar_tensor_tensor(
                    y.ap[:, :N], u.ap[:, 2:N + 2], sm(cb + 16 + jc), y.ap[:, :N], op0=ALU.mult, op1=ALU.add),
                    reads=[u, smalls, y], writes=[y])
                P.op("vector", lambda e, y=y, pg=pg, jc=jc: e.tensor_tensor(
                    yc[jc].ap[:, :N], y.ap[:, :N], pg.ap[:, :N], op=ALU.mult), reads=[y, pg], writes=[yc[jc]])
        A.release(m)
        m = A.mark()
        P.barrier(hx)
        mh = HA.mark()
        qn, _ = HA.chunks(8, 512, BF16, "qn")
        qr, _ = HA.chunks(8, 512, BF16, "qr")
        m2 = A.mark()
        cq, _ = A.chunks(4, 512, F32, "cq")
        cqn, _ = A.chunks(4, 512, BF16, "cqn")
        cosm = A.tile(512, F32, "cosm"); sinm = A.tile(512, F32, "sinm")
        ft = Rot([A.tile(512, F32, "ft") for _ in range(4)])
        P.dma("sync", cosm, DT["cosm"], cosm.ap[0:64, :N], dr["cosm"][:, t0:t0 + N])
        P.dma("sync", sinm, DT["sinm"], sinm.ap[0:64, :N], dr["sinm"][:, t0:t0 + N])
        for c4 in range(4):
            if c4 % 2 == 0:
                qs, qv = wload("w_in", l, 16, C_CQ + c4 * 128, 256)
            p = psr.next()
            P.mm(p, [(p.ap[:, :N], qv[:, j, (c4 % 2) * 128:(c4 % 2 + 1) * 128], xm[j].ap[:, :N]) for j in range(16)], reads=[qs] + xm)
            P.op("scalar", lambda e, p=p, c4=c4: e.activation(cq[c4].ap[:, :N], p.ap[:, :N], ACT.Copy), reads=[p], writes=[cq[c4]])
        rmsnorm(cq, N, S_EPS + 1, lambda j: misc.ap[:, j:j + 1], None, cqn, [misc])
        ns_, nv = wload("wqb_n", l, 4, 0, 1024)
        r1s, r1v = wload("wqb_r", l, 4, 0, 512)
        r2s, r2v = wload("wqb_rr", l, 4, 0, 512)
        for h in range(8):
            p = psr.next()
            P.mm(p, [(p.ap[:, :N], nv[:, c4, h * 128:(h + 1) * 128], cqn[c4].ap[:, :N]) for c4 in range(4)], reads=[ns_] + cqn)
            P.op("scalar", lambda e, p=p, h=h: e.activation(qn[h].ap[:, :N], p.ap[:, :N], ACT.Copy), reads=[p], writes=[qn[h]])
            p1 = psr.next()
            P.mm(p1, [(p1.ap[0:64, :N], r1v[:, c4, h * 64:(h + 1) * 64], cqn[c4].ap[:, :N]) for c4 in range(4)], reads=[r1s] + cqn)
            p2 = psr.next()
            P.mm(p2, [(p2.ap[0:64, :N], r2v[:, c4, h * 64:(h + 1) * 64], cqn[c4].ap[:, :N]) for c4 in range(4)], reads=[r2s] + cqn)
            P.op("vector", lambda e, h=h: e.memset(qr[h].ap[64:128, :N], 0.0), writes=[qr[h]])
            rope_combine(p1, p2, cosm, sinm, qr[h].ap[0:64, :N], 64, N, ft, qr[h])
        A.release(m2)
        KN = Rot([A.tile(NKEY, BF16, f"KN{i}") for i in range(2)])
        VV = Rot([A.tile(NKEY, BF16, f"VV{i}") for i in range(2)])
        KR = HA.tile(NKEY, BF16, "KR")
        PT = Rot([A.tile(512, BF16, "PT") for _ in range(4)])
        pair_den = dict(held=None, pending=None, buf=Rot([A.tile(512, BF16, f"pd2{i}") for i in range(2)]))
        rden = A.tile(512, F32, "rden")
        kchunks = list(range(NKC)) if w == 0 else [NKC - 2, NKC - 1]
        kc0 = kchunks[0] * 128
        nk = len(kchunks) * 128
        P.op("vector", lambda e: e.memset(KR.ap[64:128, :], 0.0), writes=[KR])
        if w == 0:
            for r in range(2):
                P.dma("sync", KR, DT[f"XKRo{l}"], KR.ap[0:64, r * HALF:(r + 1) * HALF], dr[f"XKRo{l}"][r * 64:(r + 1) * 64, :])
        P.dma("sync", KR, DT["CKR"], KR.ap[0:64, 2 * HALF:2 * HALF + CTX], dr["CKR"])
        kns = [None] * 8
        vvs_ = [None] * 8

        def load_head(h):
            kn = KN.next()
            vv = VV.next()
            if w == 0:
                for r in range(2):
                    r0 = r * 256 + (h % 2) * 128
                    P.dma("sync", kn, DT[f"XKN{h // 2}o{l}"], kn.ap[:, r * HALF:(r + 1) * HALF],
                          dr[f"XKN{h // 2}o{l}"][r0:r0 + 128, :])
                    P.dma("sync", vv, DT[f"XV{h // 2}o{l}"], vv.ap[:, r * HALF:(r + 1) * HALF],
                          dr[f"XV{h // 2}o{l}"][r0:r0 + 128, :])
            P.dma("sync", kn, DT["CKN"], kn.ap[:, 2 * HALF:2 * HALF + CTX], dr["CKN"][h * 128:(h + 1) * 128, :])
            P.dma("sync", vv, DT["CV"], vv.ap[:, 2 * HALF:2 * HALF + CTX].rearrange("p (c d) -> p c d", d=128), dr["CV"][h])
            kns[h], vvs_[h] = kn, vv

        def mk_finish(h):
            def fin(po, pd):
                P.op("vector", lambda e: e.reciprocal(rden.ap[:, :N], pd.ap[:, :N]), reads=[pd], writes=[rden])
                P.op("vector", lambda e: e.tensor_tensor(ymla[h].ap[:, :N], po.ap[:, :N], rden.ap[:, :N], op=ALU.mult),
                     reads=[po, rden], writes=[ymla[h]])
            return fin

        def mk_item(h, c):
            def s_pairs(p):
                return [(p.ap[:, :N], kns[h].ap[:, c * 128:(c + 1) * 128], qn[h].ap[:, :N]),
                        (p.ap[:, :N], KR.ap[:, c * 128:(c + 1) * 128], qr[h].ap[:, :N])]
            return dict(s_pairs=s_pairs, s_reads=lambda: [kns[h], KR, qn[h], qr[h]], ci=c - kchunks[0],
                        v_lhsT=lambda: vvs_[h].ap[:, c * 128:(c + 1) * 128], v_reads=lambda: [vvs_[h]], mask=None)

        load_head(0)
        groups = []
        for h in range(8):
            groups.append(dict(N=N, scale=float(MLA_SCALE), items=[mk_item(h, c) for c in kchunks], finish=mk_finish(h), pair_den=pair_den,
                               prefetch=(lambda h=h: load_head(h + 1)) if h < 7 else None,
                               post=(lambda: ada_slabs(l + 1, 2, [psum[7]])) if l + 1 < L else None))
        attn_stream(groups, PT)
        HA.release(mh)
        A.release(m)
        m = A.mark()
        NB = N // 128
        qg = A.tile(8 * 512, BF16, "qg")
        cosg = A.tile(512, F32, "cosg"); sing = A.tile(512, F32, "sing")
        ft = Rot([A.tile(512, F32, "ft") for _ in range(4)])
        GK = A.tile(2 * GKEYS, BF16, "GK")
        GV = A.tile(20 * 256, BF16, "GV")
        PT = Rot([A.tile(512, BF16, "PTg") for _ in range(3)])
        dent = A.tile(512, F32, "dent")
        P.dma("sync", cosg, DT["cosg"], cosg.ap[:, :N], dr["cosg"][:, t0:t0 + N])
        P.dma("sync", sing, DT["sing"], sing.ap[:, :N], dr["sing"][:, t0:t0 + N])
        GK3 = GK.ap.rearrange("p (g k) -> p g k", g=2)
        GV3 = GV.ap.rearrange("p (c d) -> p c d", c=20)
        P.dma("sync", GK, DT["GKE"], GK3[:, :, 128:GKEYS], dr["GKE"].rearrange("g p k -> p g k")[:, :, 128:GKEYS])
        P.dma("sync", GV, DT["GVE"], GV3[:, 1:20, :], dr["GVE"][:, 1:20, :])
        if w == 0:
            xgo = dr[f"XGo{l}"].rearrange("(r g p) k -> r p g k", r=2, g=2)
            P.dma("sync", GK, DT[f"XGo{l}"], GK3[:, :, 0:128], xgo[0][:, :, 128:256])
            P.dma("sync", GK, DT[f"XGo{l}"], GK3[:, :, 128 + HALF:128 + HALF + 128], xgo[1][:, :, 0:128])
            P.dma("sync", GV, DT[f"XGVo{l}"], GV.ap[:, 0:256], dr[f"XGVo{l}"][128:256, :])
            P.dma("sync", GV, DT[f"XGVo{l}"], GV.ap[:, 17 * 256:18 * 256], dr[f"XGVo{l}"][256:384, :])
        for s in range(4):
            q1s, q1v = wload("w_in", l, 16, C_GQ + s * 256, 256)
            q2s, q2v = wload("winr", l, 16, s * 256, 256)
            for q in range(2):
                h = s * 2 + q
                p1 = psr.next()
                P.mm(p1, [(p1.ap[:, :N], q1v[:, j, q * 128:(q + 1) * 128], xm[j].ap[:, :N]) for j in range(16)], reads=[q1s] + xm)
                p2 = psr.next()
                P.mm(p2, [(p2.ap[:, :N], q2v[:, j, q * 128:(q + 1) * 128], xm[j].ap[:, :N]) for j in range(16)], reads=[q2s] + xm)
                outv = qg.ap.rearrange("p (b h i) -> p b h i", b=4, h=8)[:, 0:NB, h, :]
                t1 = ft.next(); t2 = ft.next()
                P.op("vector", lambda e, t1=t1, p1=p1: e.tensor_tensor(t1.ap[:, :N], p1.ap[:, :N], cosg.ap[:, :N], op=ALU.mult),
                     reads=[p1, cosg], writes=[t1])
                P.op("vector", lambda e, t2=t2, p2=p2: e.tensor_tensor(t2.ap[:, :N], p2.ap[:, :N], sing.ap[:, :N], op=ALU.mult),
                     reads=[p2, sing], writes=[t2])
                P.op("vector", lambda e, t1=t1, t2=t2, outv=outv: e.tensor_tensor(
                    outv, t1.ap[:, :N].rearrange("p (b i) -> p b i", i=128), t2.ap[:, :N].rearrange("p (b i) -> p b i", i=128),
                    op=ALU.add), reads=[t1, t2], writes=[qg])
        groups = []
        for qb in range(NB):
            nb = t0 // 128 + qb
            if w == 0:
                chunks = [(nb, 2 if nb == 0 else 0), (nb + 1, None), (nb + 2, 3 if nb == 15 else 1), (18, None), (19, None)]
            else:
                chunks = [(18, None), (19, None)]
            for g in range(2):
                rhs = qg.ap[:, qb * 1024 + g * 512: qb * 1024 + (g + 1) * 512]

                def mk_item(c, mk, g=g, rhs=rhs):
                    return dict(s_pairs=lambda p: [(p.ap, GK.ap[:, g * GKEYS + c * 128: g * GKEYS + (c + 1) * 128], rhs)],
                                s_reads=lambda: [GK, qg],
                                v_lhsT=lambda: GV.ap[:, c * 256 + g * 128: c * 256 + (g + 1) * 128],
                                v_reads=lambda: [GV], mask=mk)

                def fin(po, pd, g=g, qb=qb):
                    P.op("vector", lambda e: e.tensor_tensor(dent.ap, pd.ap, sinkrow.ap[:, g * 512:(g + 1) * 512], op=ALU.add),
                         reads=[pd, sinkrow], writes=[dent])
                    P.op("vector", lambda e: e.reciprocal(dent.ap, dent.ap), reads=[dent], writes=[dent])
                    outv = ygqa_all.rearrange("p (h t) -> p h t", h=8)[:, g * 4:(g + 1) * 4, qb * 128:(qb + 1) * 128]
                    P.op("vector", lambda e: e.tensor_tensor(
                        outv, po.ap.rearrange("p (h i) -> p h i", h=4), dent.ap.rearrange("p (h i) -> p h i", h=4), op=ALU.mult),
                        reads=[po, dent], writes=ygqa)
                groups.append(dict(N=512, scale=float(GQA_SCALE), items=[mk_item(c, mk) for c, mk in chunks], finish=fin, prefetch=None))
        attn_stream(groups, PT)
        A.release(m)
        m = A.mark()
        macc, _ = A.chunks(16, 512, F32, "macc")
        merged = yc + ymla
        sig = Rot([A.tile(512, F32, "sig") for _ in range(2)])
        tt = Rot([A.tile(512, F32, "tt") for _ in range(2)])
        for b, (bw, ys) in enumerate((("w_branch_conv", yc), ("w_branch_mla", ymla), ("w_branch_gqa", ygqa))):
            for dg in range(8):
                gs, gv = wload("w_in", l, 16, C_GATE + b * D + dg * 256, 256)
                bs, bv = wload(bw, l, 8, dg * 256, 256)
                for q in range(2):
                    dc = dg * 2 + q
                    pg = psr.next()
                    P.mm(pg, [(pg.ap[:, :N], gv[:, j, q * 128:(q + 1) * 128], xm[j].ap[:, :N]) for j in range(16)], reads=[gs] + xm)
                    pb = psr.next()
                    P.mm(pb, [(pb.ap[:, :N], bv[:, j, q * 128:(q + 1) * 128], ys[j].ap[:, :N]) for j in range(8)], reads=[bs] + ys)
                    sg = sig.next()
                    P.op("scalar", lambda e, sg=sg, pg=pg: e.activation(sg.ap[:, :N], pg.ap[:, :N], ACT.Sigmoid), reads=[pg], writes=[sg])
                    if b == 0:
                        P.op("vector", lambda e, sg=sg, pb=pb, dc=dc: e.tensor_tensor(
                            macc[dc].ap[:, :N], sg.ap[:, :N], pb.ap[:, :N], op=ALU.mult), reads=[sg, pb], writes=[macc[dc]])
                    else:
                        t = tt.next()
                        P.op("vector", lambda e, sg=sg, pb=pb, t=t: e.tensor_tensor(
                            t.ap[:, :N], sg.ap[:, :N], pb.ap[:, :N], op=ALU.mult), reads=[sg, pb], writes=[t])
                        o = macc[dc] if b == 1 else merged[dc]
                        P.op("vector", lambda e, t=t, o=o, dc=dc: e.tensor_tensor(
                            o.ap[:, :N], macc[dc].ap[:, :N], t.ap[:, :N], op=ALU.add), reads=[t, macc[dc]], writes=[o])
        load_hx_from(dr["HX"], DT["HX"], t0, N)
        for s in range(8):
            ws, wv = wload("w_out", l, 16, s * 256, 256)
            for q in range(2):
                dc = s * 2 + q
                p = psr.next()
                P.mm(p, [(p.ap[:, :N], wv[:, j, q * 128:(q + 1) * 128], merged[j].ap[:, :N]) for j in range(16)], reads=[ws] + merged)
                P.op("vector", lambda e, p=p, dc=dc: e.scalar_tensor_tensor(
                    hx[dc].ap[:, :N], p.ap[:, :N], Gcol(1, dc, w), hx[dc].ap[:, :N], op0=ALU.mult, op1=ALU.add),
                    reads=[p, hx[dc], CG], writes=[hx[dc]])
        A.release(m)
        A.release(mT)
        mT = A.mark()
        xn, _ = A.chunks(16, 512, BF16, "xn2")
        rmsnorm(hx, N, S_EPS, lambda j: Acol(2, j, w), lambda j: Bcol(6, j, w), xn, [CA, MOD()])
        ffn(l, 2, xn, N, w)
        if final:
            outs, outs_all = A.chunks(16, 512, F32, "fin")
            rmsnorm(hx, N, S_EPS, lambda j: misc.ap[:, 8 + j:9 + j], None, outs, [misc], sq_tiles=xn)
            ov = dr["outT"].rearrange("(c p) t -> p c t", p=128)[:, :, t0:t0 + N]
            P.dma("sync", DT["outT"], outs, ov, outs_all.rearrange("p (c t) -> p c t", c=16)[:, :, 0:N])
        else:
            store_hx(t0, N)
        A.release(mT)

    def DT_S(n):
        return DT[n + "_in"] if (n + "_in") in DT else DT[n]

    def load_hx_from(ap, t, t0, N):
        v = ap.rearrange("(c p) t -> p c t", p=128)[:, :, t0:t0 + N]
        P.dma("sync", hx, t, hx_all.rearrange("p (c t) -> p c t", c=16)[:, :, 0:N], v)

    for l in range(L):
        layer_setup(l)
        for ti in range(4):
            phase1_tile(l, ti)
        for n, r, c, dt in XCH_SPECS:
            P.cc(DT[f"{n}o{l}"], DT[f"{n}i{l}"], dr[f"{n}o{l}"], dr[f"{n}i{l}"], PAIRS)
        phase1_tile(l, 4)
        for ti in range(5 if l == 0 else 4):
            phase2_tile(l, ti, final=(l == L - 1))
    outs_t = [DT["outT"]]
    P.wait_all("sync", outs_t)
    for X in COMPUTE:
        P.wait_all(X, outs_t)
    P.emit()
    nc._wkeys = list(wkeys)
    build.last = (P.n_inst, len(P.dval), {e: len(P.waited[e]) for e in COMPUTE})
    return nc


def _rope_tables(half):
    def tab(dim_half, nfreq_dim):
        inv = (10000.0 ** (-np.arange(0, dim_half, 2, dtype=np.float32) / np.float32(dim_half))).astype(np.float32)
        return inv
    pos = np.arange(half * HALF, (half + 1) * HALF)
    row = (pos // 64).astype(np.float32)
    col = (pos % 64).astype(np.float32)

    def make(hd):
        inv = tab(hd, None)
        nf = hd // 2
        ar = row[None, :] * inv[:, None]
        ac = col[None, :] * inv[:, None]
        cr, sr, cc, sc = np.cos(ar), np.sin(ar), np.cos(ac), np.sin(ac)
        cos = np.concatenate([cr, cr, cc, cc], 0)
        sin = np.concatenate([-sr, sr, -sc, sc], 0)
        cosf = np.ones((2 * hd, TOK), np.float32)
        sinf = np.zeros((2 * hd, TOK), np.float32)
        cosf[:, :HALF] = cos
        sinf[:, :HALF] = sin
        return cosf, sinf
    cg, sg = make(64)
    cm, sm_ = make(32)
    return cg, sg, cm, sm_


def _fm(v):
    return np.ascontiguousarray(v.reshape(-1, 128).T)


def _prep_static(inp):
    f = lambda a: np.ascontiguousarray(np.asarray(a, dtype=np.float32))
    w_in = f(inp["w_in"])
    pg = np.concatenate([np.arange(32, 64), np.arange(0, 32), np.arange(96, 128), np.arange(64, 96)])
    pm = np.concatenate([np.arange(16, 32), np.arange(0, 16), np.arange(48, 64), np.arange(32, 48)])
    cols = []
    for h in range(8):
        cols.append(C_GQ + h * 128 + pg)
    for g in range(2):
        cols.append(C_GK + g * 128 + pg)
    cols.append(C_KR + pm)
    cols = np.concatenate(cols)
    wqb = f(inp["mla_w_qb"])
    wkvb = f(inp["mla_w_kvb"])
    cn = np.concatenate([h * 192 + np.arange(128) for h in range(8)])
    cr = np.concatenate([h * 192 + 128 + np.arange(64) for h in range(8)])
    crr = np.concatenate([h * 192 + 128 + pm for h in range(8)])
    ck = np.concatenate([h * 256 + np.arange(128) for h in range(8)])
    cv = np.concatenate([h * 256 + 128 + np.arange(128) for h in range(8)])
    W = {
        "ada_w": f(inp["ada_w"]), "ffn1_w_gu": f(inp["ffn1_w_gu"]), "ffn1_w_down": f(inp["ffn1_w_down"]),
        "w_in": w_in, "winr": np.ascontiguousarray(w_in[:, :, cols]),
        "wqb_n": np.ascontiguousarray(wqb[:, :, cn]), "wqb_r": np.ascontiguousarray(wqb[:, :, cr]),
        "wqb_rr": np.ascontiguousarray(wqb[:, :, crr]),
        "wkvb_k": np.ascontiguousarray(wkvb[:, :, ck]), "wkvb_v": np.ascontiguousarray(wkvb[:, :, cv]),
        "w_branch_conv": f(inp["w_branch_conv"]), "w_branch_mla": f(inp["w_branch_mla"]),
        "w_branch_gqa": f(inp["w_branch_gqa"]), "w_out": f(inp["w_out"]),
        "ffn2_w_gu": f(inp["ffn2_w_gu"]), "ffn2_w_down": f(inp["ffn2_w_down"]),
    }
    return W


def _smalls(inp, b, half):
    s = np.zeros((128, NS), np.float32)
    for l in range(L):
        o = l * LS
        ab = _fm(np.asarray(inp["ada_b"][l], np.float32))
        s[:, o + S_ADAB:o + S_ADAB + 288] = np.repeat(ab, 2, axis=1)
        s[:, o + S_N1:o + S_N1 + 16] = _fm(np.asarray(inp["ffn1_norm"][l], np.float32))
        s[:, o + S_NM:o + S_NM + 16] = _fm(np.asarray(inp["mix_norm"][l], np.float32))
        s[:, o + S_N2:o + S_N2 + 16] = _fm(np.asarray(inp["ffn2_norm"][l], np.float32))
        s[:, o + S_QN:o + S_QN + 4] = _fm(np.asarray(inp["mla_q_norm"][l], np.float32))
        s[:, o + S_KVN:o + S_KVN + 2] = _fm(np.asarray(inp["mla_kv_norm"][l], np.float32))
        cw = np.asarray(inp["conv_w"][l], np.float32)
        for k in range(3):
            s[:, o + S_CONV + k * 8:o + S_CONV + (k + 1) * 8] = _fm(cw[k])
        s[:, o + S_SINK:o + S_SINK + 8] = np.asarray(inp["gqa_sink"][l], np.float32)[None, :]
    s[:, S_FIN:S_FIN + 16] = _fm(np.asarray(inp["final_norm"], np.float32))
    cv = np.stack([_fm(np.asarray(inp["c"][b], np.float32)), _fm(np.asarray(inp["c_ctx"], np.float32))], axis=2)
    s[:, S_CVEC:S_CVEC + 32] = cv.reshape(128, 32)
    s[:, S_FLAG] = 0.0 if half == 0 else 1.0
    s[:, S_FLAG + 1] = 1.0 if half == 0 else 0.0
    s[:, S_EPS] = EPS * 2048
    s[:, S_EPS + 1] = EPS * 512
    s[:, S_EPS + 2] = EPS * 256
    return s


def _masks(half):
    j = np.arange(128)[:, None]
    i = np.arange(128)[None, :]
    prev = (j >= i).astype(np.float32)
    nxt = (j <= i).astype(np.float32)
    first = prev if half == 1 else np.zeros_like(prev)
    last = nxt if half == 0 else np.zeros_like(nxt)
    m = np.concatenate([np.tile(1.0 - x, (1, 4)) for x in (prev, nxt, first, last)]
                       + [-30000.0 * np.eye(128, dtype=np.float32)], axis=1)
    return m.astype(ml_dtypes.bfloat16)


_NC_CACHE = {}


def _get_nc():
    if "nc" not in _NC_CACHE:
        _NC_CACHE["nc"] = build()
    return _NC_CACHE["nc"]


def kernel(**inp):
    W = _prep_static(inp)
    x = np.asarray(inp["x"], np.float32)
    ctx = np.asarray(inp["ctx"], np.float32)
    nc = _get_nc()
    ws = {k: W[n][l] for k, n, l in nc._wkeys}
    ins = []
    for c in range(8):
        b, half = c // 2, c % 2
        cg, sg, cm, sm_ = _rope_tables(half)
        d = dict(ws)
        d["smalls"] = _smalls(inp, b, half)
        d["cosg"], d["sing"], d["cosm"], d["sinm"] = cg, sg, cm, sm_
        d["masks"] = _masks(half)
        d["xT"] = np.ascontiguousarray(x[b, half * HALF:(half + 1) * HALF, :].T)
        d["cT"] = np.ascontiguousarray(ctx[b].T)
        ins.append(d)
    r = run_bass_kernel_spmd(nc, ins, core_ids=list(range(8))).results
    out = np.zeros((4, SEQ, D), np.float32)
    for c in range(8):
        b, half = c // 2, c % 2
        out[b, half * HALF:(half + 1) * HALF, :] = np.asarray(r[c]["outT"]).T
    return out
```

```python
import numpy as np
import ml_dtypes
import concourse.bass as bass
import concourse.mybir as mybir
from concourse.bass_utils import run_bass_kernel_spmd

F32 = mybir.dt.float32
BF16 = mybir.dt.bfloat16
ACT = mybir.ActivationFunctionType
ALU = mybir.AluOpType

COMPUTE = ("tensor", "vector", "scalar", "gpsimd")
ALLENG = ("tensor", "vector", "scalar", "gpsimd", "sync")

D = 2048
NCH = 16
DFF = 5632
FCH = 44
L = 2
SEQ = 4096
HALF = 2048
CTX = 256
TOK = HALF + CTX
EPS = 1e-6
MLA_SCALE = 192 ** -0.5
GQA_SCALE = 128 ** -0.5
C_GB, C_GC, C_V = 0, 1024, 2048
C_CQ, C_CKV, C_KR = 3072, 3584, 3840
C_GQ, C_GK, C_GV = 3904, 4928, 5184
C_GATE = 5440
NKEY = 2 * HALF + CTX
NKC = NKEY // 128
GKEYS = 128 + HALF + 128 + CTX
TILES = [(0, 512), (512, 512), (1024, 512), (1536, 512), (2048, 256)]
SLOT = 4096
NSLOT = 8
LS = 374
S_ADAB, S_N1, S_NM, S_N2, S_QN, S_KVN, S_CONV, S_SINK = 0, 288, 304, 320, 336, 340, 342, 366
S_FIN = 2 * LS
S_CVEC = S_FIN + 16
S_FLAG = S_CVEC + 32
S_EPS = S_FLAG + 2
NS = S_EPS + 3


class T:
    __slots__ = ("ap", "w", "r", "ds", "name")

    def __init__(self, ap, name="", ds=None):
        self.ap = ap
        self.w = None
        self.r = {}
        self.ds = ds
        self.name = name


class Prog:
    def __init__(self, nc):
        self.nc = nc
        self.ops = {e: [] for e in ALLENG}
        self.cnt = {e: 0 for e in COMPUTE}
        self.known = {e: {} for e in ALLENG}
        self.waited = {e: set() for e in COMPUTE}
        self.dval = []
        self.free_ds = []
        self.n_inst = 0

    def new_ds(self):
        if self.free_ds:
            return self.free_ds.pop()
        self.dval.append(0)
        return len(self.dval) - 1

    def _emit_waits(self, X, evs):
        k = self.known[X]
        for ev in evs:
            if ev is None:
                continue
            if ev[0] == 'e':
                if ev[1] == X and X == "tensor":
                    continue
                key, val = ev[1], ev[2]
            else:
                key, val = ev[1] + 1000, self.dval[ev[1]]
            if k.get(key, 0) >= val:
                continue
            k[key] = val
            if ev[0] == 'e':
                self.waited[ev[1]].add(val)
                self.ops[X].append(('w', ev[1], val))
            else:
                self.ops[X].append(('wd', ev[1], val))

    @staticmethod
    def _deps(reads, writes):
        evs = []
        for t in reads:
            evs.append(t.w)
        for t in writes:
            evs.append(t.w)
            evs.extend(t.r.values())
        return evs

    @staticmethod
    def _mark(reads, writes, ev):
        key = ev[1] if ev[0] == 'e' else ev[1] + 1000
        for t in reads:
            t.r[key] = ev
        for t in writes:
            t.w = ev
            t.r = {}

    def op(self, X, fn, reads=(), writes=()):
        self._emit_waits(X, self._deps(reads, writes))
        self.cnt[X] += 1
        seq = self.cnt[X]
        ev = ('e', X, seq)
        self.ops[X].append(('i', fn, seq))
        self._mark(reads, writes, ev)
        self.n_inst += 1
        return ev

    def mm(self, ps, pairs, reads, start=True, stop=True, fresh=None):
        X = "tensor"
        if fresh is None:
            fresh = start
        self._emit_waits(X, self._deps(reads, [ps] if fresh else []))
        n = len(pairs)
        ev = None
        for i, (o, l, r) in enumerate(pairs):
            self.cnt[X] += 1
            seq = self.cnt[X]
            st = start and i == 0
            sp = stop and i == n - 1
            self.ops[X].append(('i', (lambda e, o=o, l=l, r=r, st=st, sp=sp:
                                      e.matmul(o, l, r, start=st, stop=sp)), seq))
            ev = ('e', X, seq)
        self.n_inst += n
        self._mark(reads, [ps], ev)
        return ev

    def dma(self, Q, dst, src, out_ap, in_ap, slow=False):
        dsts = dst if isinstance(dst, (list, tuple)) else [dst]
        srcs = src if isinstance(src, (list, tuple)) else [src]
        self._emit_waits(Q, self._deps(srcs, dsts))
        d0 = dsts[0]
        if d0.ds is None:
            d0.ds = self.new_ds()
        ds = d0.ds
        self.dval[ds] += 16
        ev = ('d', ds, self.dval[ds])
        self.ops[Q].append(('dma', out_ap, in_ap, ds, slow))
        self._mark(srcs, dsts, ev)
        self.n_inst += 1
        return ev

    def cc(self, dst, src, out_ap, in_ap, groups):
        Q = "gpsimd"
        self._emit_waits(Q, self._deps([src], [dst]))
        self.dval.append(1)
        ds = len(self.dval) - 1
        ev = ('d', ds, 1)
        self.ops[Q].append(('cc', out_ap, in_ap, ds, groups))
        self._mark([src], [dst], ev)
        self.n_inst += 1
        return ev

    def barrier(self, tiles):
        evs = []
        for t in tiles:
            evs.append(t.w)
            evs.extend(t.r.values())
        for X in ALLENG:
            if X == "gpsimd":
                continue
            self._emit_waits(X, evs)

    def wait_all(self, X, tiles):
        evs = []
        for t in tiles:
            evs.append(t.w)
            evs.extend(t.r.values())
        self._emit_waits(X, evs)

    def emit(self):
        nc = self.nc
        esem = {e: nc.alloc_semaphore(name=f"es_{e}") for e in COMPUTE}
        dsems = [nc.alloc_semaphore(name=f"ds_{i}") for i in range(len(self.dval))]
        rank = {}
        for e in COMPUTE:
            rank[e] = {q: i + 1 for i, q in enumerate(sorted(self.waited[e]))}

        def run(X, eng):
            myrank = rank.get(X, {})
            for o in self.ops[X]:
                k = o[0]
                if k == 'i':
                    ins = o[1](eng)
                    if o[2] in myrank:
                        ins.then_inc(esem[X], 1)
                elif k == 'w':
                    eng.wait_ge(esem[o[1]], rank[o[1]][o[2]])
                elif k == 'wd':
                    eng.wait_ge(dsems[o[1]], o[2])
                elif k == 'cc':
                    eng.collective_compute("AllGather", ALU.bypass, replica_groups=o[4],
                                           ins=[o[2]], outs=[o[1]]).then_inc(dsems[o[3]])
                elif o[4]:
                    eng.dma_start(out=o[1], in_=o[2], allow_slow_non_contiguous=True).then_inc(dsems[o[3]], 16)
                else:
                    eng.dma_start(out=o[1], in_=o[2]).then_inc(dsems[o[3]], 16)

        with nc.Block() as block:
            @block.tensor
            def _(e):
                run("tensor", e)

            @block.vector
            def _(e):
                run("vector", e)

            @block.scalar
            def _(e):
                run("scalar", e)

            @block.gpsimd
            def _(e):
                run("gpsimd", e)

            @block.sync
            def _(e):
                run("sync", e)


class Arena:
    def __init__(self, P, ap, nbytes):
        self.P = P
        self.ap = ap
        self.nbytes = nbytes
        self.top = 0
        self.live = []

    def _carve(self, nelem, dt):
        bpe = 4 if dt == F32 else 2
        nb = (nelem * bpe + 63) // 64 * 64
        assert self.top + nb <= self.nbytes, f"arena overflow {self.top}+{nb}>{self.nbytes}"
        a = self.ap[:, self.top // 4:(self.top + nb) // 4]
        if dt != F32:
            a = a.bitcast(dt)
        self.top += nb
        return a[:, 0:nelem]

    def tile(self, n, dt, name=""):
        t = T(self._carve(n, dt), name)
        self.live.append(t)
        return t

    def chunks(self, k, n, dt, name="", shared_ds=False):
        a = self._carve(k * n, dt)
        ds = self.P.new_ds() if shared_ds else None
        ts = [T(a[:, i * n:(i + 1) * n], f"{name}{i}", ds) for i in range(k)]
        self.live.extend(ts)
        return ts, a

    def mark(self):
        return (self.top, len(self.live))

    def release(self, m):
        dead = self.live[m[1]:]
        self.P.barrier(dead)
        freed = set()
        for t in dead:
            if t.ds is not None and t.ds not in freed:
                freed.add(t.ds)
                self.P.free_ds.append(t.ds)
            t.ds = None
        del self.live[m[1]:]
        self.top = m[0]


class Rot:
    def __init__(self, ts):
        self.ts = ts
        self.i = 0

    def next(self):
        t = self.ts[self.i % len(self.ts)]
        self.i += 1
        return t


WEIGHT_SPECS = [
    ("ada_w", [L, D, 9 * D]), ("ffn1_w_gu", [L, D, 2 * DFF]), ("ffn1_w_down", [L, DFF, D]),
    ("w_in", [L, D, 11584]), ("winr", [L, D, 1344]),
    ("wqb_n", [L, 512, 1024]), ("wqb_r", [L, 512, 512]), ("wqb_rr", [L, 512, 512]),
    ("wkvb_k", [L, 256, 1024]), ("wkvb_v", [L, 256, 1024]),
    ("w_branch_conv", [L, 1024, D]), ("w_branch_mla", [L, 1024, D]), ("w_branch_gqa", [L, 1024, D]),
    ("w_out", [L, D, D]), ("ffn2_w_gu", [L, D, 2 * DFF]), ("ffn2_w_down", [L, DFF, D]),
]
SCRATCH_SPECS = [
    ("HX", [D, TOK], F32), ("XM", [D, TOK], BF16),
    ("GKE", [2, 128, GKEYS], BF16), ("GVE", [128, 20, 256], BF16),
    ("UE", [1024, HALF + 2], F32), ("UC", [1024, CTX + 2], F32),
    ("CKN", [1024, CTX], BF16), ("CKR", [64, CTX], BF16), ("CV", [8, 128, 2, 128], BF16),
]
XCH_SPECS = [
    ("XU", 1024, 2, F32), ("XG", 256, 256, BF16), ("XGV", 256, 256, BF16), ("XKR", 64, HALF, BF16),
    ("XKN0", 256, HALF, BF16), ("XKN1", 256, HALF, BF16), ("XKN2", 256, HALF, BF16), ("XKN3", 256, HALF, BF16),
    ("XV0", 256, HALF, BF16), ("XV1", 256, HALF, BF16), ("XV2", 256, HALF, BF16), ("XV3", 256, HALF, BF16),
]
PAIRS = [[0, 1], [2, 3], [4, 5], [6, 7]]
def build(mode="FUSED"):
    nc = bass.Bass("TRN2", target_bir_lowering=False)
    P = Prog(nc)
    dr = {}

    def din(name, shape, dt=F32):
        dr[name] = nc.dram_tensor(name, shape, dt, kind="ExternalInput").ap()

    wshape = dict(WEIGHT_SPECS)
    wkeys = []
    din("smalls", [128, NS])
    din("cosg", [128, TOK]); din("sing", [128, TOK])
    din("cosm", [64, TOK]); din("sinm", [64, TOK])
    din("masks", [128, 4 * 512 + 128], BF16)
    din("xT", [D, HALF]); din("cT", [D, CTX])
    for n, sh, dt in SCRATCH_SPECS:
        dr[n] = nc.dram_tensor(n, sh, dt).ap()
    for l in range(L):
        for n, r, c, dt in XCH_SPECS:
            dr[f"{n}i{l}"] = nc.dram_tensor(f"{n}i{l}", [r, c], dt).ap()
            dr[f"{n}o{l}"] = nc.dram_tensor(f"{n}o{l}", [2 * r, c], dt).ap()
    dr["outT"] = nc.dram_tensor("outT", [D, HALF], F32, kind="ExternalOutput").ap()
    DT = {k: T(v, k) for k, v in dr.items()}
    WT = T(None, "weights")

    ARENA_BYTES = 204 * 1024
    arena_ap = nc.alloc_sbuf_tensor("arena", [128, ARENA_BYTES // 4], F32).ap()
    A = Arena(P, arena_ap, ARENA_BYTES)
    smalls = A.tile(NS, F32, "smalls")
    modL = [A.tile(288, F32, "mod0"), A.tile(288, F32, "mod1")]
    curl = [0]

    def MOD():
        return modL[curl[0]]
    CA = A.tile(96, F32, "CA")
    CG = A.tile(96, F32, "CG")
    misc = A.tile(64, F32, "misc")
    scb = A.tile(32, BF16, "scb")
    ones = A.tile(128, BF16, "ones")
    masks = A.tile(4 * 512 + 128, BF16, "masks")
    sinkrow = A.tile(2 * 512, F32, "sinkrow")
    ring = [A.tile(SLOT, BF16, f"ring{i}") for i in range(NSLOT)]
    ringi = [0]
    hx, hx_all = A.chunks(16, 512, F32, "hx", shared_ds=True)
    HA = Arena(P, hx_all, 16 * 512 * 4)
    n_rstd = A.tile(512, F32, "n_rstd")
    n_tmp = Rot([A.tile(512, F32, f"n_tmp{i}") for i in range(3)])
    psum = [T(nc.alloc_psum_tensor(f"ps{i}", [128, 512], F32).ap(), f"ps{i}") for i in range(8)]
    ps_o, ps_d = psum[0], psum[1]
    psr = Rot(psum[2:])

    def sm(c0, c1=None):
        return smalls.ap[:, c0:(c0 + 1 if c1 is None else c1)]

    def wload(name, l, kc, c0, ncols, k0=0):
        slot = ring[ringi[0] % NSLOT]
        ringi[0] += 1
        view = slot.ap[:, 0:kc * ncols].rearrange("p (k n) -> p k n", k=kc)
        key = f"{name}@{l}"
        if key not in dr:
            dr[key] = nc.dram_tensor(f"{name}_L{l}", wshape[name][1:], F32, kind="ExternalInput").ap()
            wkeys.append((f"{name}_L{l}", name, l))
        srcv = dr[key].rearrange("(c p) n -> p c n", p=128)[:, k0:k0 + kc, c0:c0 + ncols]
        P.dma("gpsimd", slot, WT, view, srcv)
        return slot, view

    P.dma("sync", smalls, DT["smalls"], smalls.ap, dr["smalls"])
    P.dma("sync", masks, DT["masks"], masks.ap, dr["masks"])
    P.op("vector", lambda e: e.memset(ones.ap, 1.0), writes=[ones])

    ada_next = [0, 0]
    silu_done = [False]

    def ada_slabs(l, count, banks):
        b = l * LS
        if not silu_done[0]:
            P.op("scalar", lambda e: e.activation(scb.ap, sm(S_CVEC, S_CVEC + 32), ACT.Silu),
                 reads=[smalls], writes=[scb])
            silu_done[0] = True
        s0 = ada_next[l]
        if s0 >= 72:
            return
        pm = banks[(s0 // max(count, 1)) % len(banks)]
        done = 0
        for _ in range(count):
            s = ada_next[l]
            if s >= 72:
                break
            ada_next[l] += 1
            slot, v = wload("ada_w", l, 16, s * 256, 256)
            for q in range(2):
                n = s * 2 + q
                P.mm(pm, [(pm.ap[:, 2 * n:2 * n + 2], v[:, j, q * 128:(q + 1) * 128], scb.ap[:, 2 * j:2 * j + 2])
                          for j in range(16)], reads=[slot, scb], start=True, stop=True, fresh=(done == 0 and q == 0))
            done += 1
        c0, c1 = 4 * s0, 4 * (s0 + done)
        P.op("vector", lambda e, pm=pm, c0=c0, c1=c1, l=l, b=b: e.tensor_tensor(
            modL[l].ap[:, c0:c1], pm.ap[:, c0:c1], sm(b + S_ADAB + c0, b + S_ADAB + c1), op=ALU.add),
            reads=[pm, smalls, modL[l]], writes=[modL[l]])

    def layer_setup(l):
        b = l * LS
        while ada_next[l] < 72:
            ada_slabs(l, 2, [psum[6], psum[7]])
        curl[0] = l
        mod = MOD()
        for k, sn in enumerate((S_N1, S_NM, S_N2)):
            for w in range(2):
                m_scale = mod.ap.rearrange("p (m j w) -> p m j w", m=9, j=16)[:, 3 * k + 1, :, w]
                outv = CA.ap.rearrange("p (k j w) -> p k j w", k=3, j=16)[:, k, :, w]
                P.op("vector", lambda e, o=outv, i=m_scale, sn=sn: e.scalar_tensor_tensor(
                    o, i, 1.0, sm(b + sn, b + sn + 16), op0=ALU.add, op1=ALU.mult),
                    reads=[mod, smalls], writes=[CA])
                m_gate = mod.ap.rearrange("p (m j w) -> p m j w", m=9, j=16)[:, 3 * k + 2, :, w]
                outg = CG.ap.rearrange("p (k j w) -> p k j w", k=3, j=16)[:, k, :, w]
                gs = 1.0 if k == 1 else 0.5
                P.op("vector", lambda e, o=outg, i=m_gate, gs=gs: e.tensor_scalar(
                    o, i, gs, None, op0=ALU.mult), reads=[mod], writes=[CG])
        P.op("vector", lambda e: e.tensor_scalar(CA.ap, CA.ap, float(np.sqrt(D)), None, op0=ALU.mult),
             reads=[CA], writes=[CA])
        P.op("vector", lambda e: e.tensor_scalar(misc.ap[:, 0:4], sm(b + S_QN, b + S_QN + 4), float(np.sqrt(512)), None,
                                                 op0=ALU.mult), reads=[smalls], writes=[misc])
        P.op("vector", lambda e: e.tensor_scalar(misc.ap[:, 4:6], sm(b + S_KVN, b + S_KVN + 2), 16.0, None,
                                                 op0=ALU.mult), reads=[smalls, misc], writes=[misc])
        P.op("vector", lambda e: e.tensor_scalar(misc.ap[:, 8:24], sm(S_FIN, S_FIN + 16), float(np.sqrt(D)), None,
                                                 op0=ALU.mult), reads=[smalls, misc], writes=[misc])
        P.op("scalar", lambda e: e.activation(misc.ap[:, 24:32], sm(b + S_SINK, b + S_SINK + 8), ACT.Exp),
             reads=[smalls, misc], writes=[misc])
        for h in range(8):
            P.op("vector", lambda e, h=h: e.tensor_scalar(
                sinkrow.ap[:, h * 128:(h + 1) * 128], ones.ap[:, 0:128], misc.ap[:, 24 + h:25 + h], None, op0=ALU.mult),
                reads=[ones, misc, sinkrow], writes=[sinkrow])

    def Acol(k, j, w):
        i = (k * 16 + j) * 2 + w
        return CA.ap[:, i:i + 1]

    def Gcol(k, j, w):
        i = (k * 16 + j) * 2 + w
        return CG.ap[:, i:i + 1]

    def Bcol(m, j, w):
        i = (m * 16 + j) * 2 + w
        return MOD().ap[:, i:i + 1]

    def rmsnorm(xs, N, epscol, a_fn, b_fn, outs, consts, sq_tiles=None):
        sqs = outs if sq_tiles is None else sq_tiles
        pss = psr.next()
        nchunk = len(xs)
        for j, x in enumerate(xs):
            sq = sqs[j]
            if j % 2 == 0:
                P.op("scalar", lambda e, sq=sq, x=x: e.activation(sq.ap[:, :N], x.ap[:, :N], ACT.Square),
                     reads=[x], writes=[sq])
            else:
                P.op("vector", lambda e, sq=sq, x=x: e.tensor_tensor(sq.ap[:, :N], x.ap[:, :N], x.ap[:, :N], op=ALU.mult),
                     reads=[x], writes=[sq])
        for j in range(nchunk):
            sq = sqs[j]
            P.mm(pss, [(pss.ap[:, :N], ones.ap, sq.ap[:, :N])], reads=[ones, sq], start=(j == 0), stop=(j == nchunk - 1))
        rstd = n_rstd
        P.op("scalar", lambda e: e.activation(rstd.ap[:, :N], pss.ap[:, :N], ACT.Sqrt, bias=sm(epscol), scale=1.0),
             reads=[pss, smalls], writes=[rstd])
        P.op("vector", lambda e: e.reciprocal(rstd.ap[:, :N], rstd.ap[:, :N]), reads=[rstd], writes=[rstd])
        for j, x in enumerate(xs):
            o = outs[j]
            aj = a_fn(j)
            if b_fn is None:
                P.op("vector", lambda e, o=o, x=x, aj=aj: e.scalar_tensor_tensor(
                    o.ap[:, :N], x.ap[:, :N], aj, rstd.ap[:, :N], op0=ALU.mult, op1=ALU.mult),
                    reads=[x, rstd] + consts, writes=[o])
            else:
                bj = b_fn(j)
                t = n_tmp.next()
                P.op("vector", lambda e, t=t, x=x, aj=aj: e.scalar_tensor_tensor(
                    t.ap[:, :N], x.ap[:, :N], aj, rstd.ap[:, :N], op0=ALU.mult, op1=ALU.mult),
                    reads=[x, rstd] + consts, writes=[t])
                P.op("scalar", lambda e, t=t, o=o, bj=bj: e.activation(
                    o.ap[:, :N], t.ap[:, :N], ACT.Identity, bias=bj, scale=1.0),
                    reads=[t] + consts, writes=[o])

    def ffn(l, which, xn, N, w):
        m = A.mark()
        k = 0 if which == 1 else 2
        gu = "ffn1_w_gu" if which == 1 else "ffn2_w_gu"
        dn = "ffn1_w_down" if which == 1 else "ffn2_w_down"
        h, _ = A.chunks(FCH, 512, BF16, "h")
        sgs = Rot([A.tile(512, F32, "sg") for _ in range(2)])
        gb, ub = psum[0:4], psum[4:8]
        for s in range(11):
            for banks, c0 in ((gb, 0), (ub, DFF)):
                for kh in range(2):
                    slot, v = wload(gu, l, 8, c0 + s * 512, 512, k0=kh * 8)
                    for q in range(4):
                        p = banks[q]
                        P.mm(p, [(p.ap[:, :N], v[:, j, q * 128:(q + 1) * 128], xn[kh * 8 + j].ap[:, :N]) for j in range(8)],
                             reads=[slot] + xn[kh * 8:(kh + 1) * 8], start=(kh == 0), stop=(kh == 1))
            for q in range(4):
                fc = s * 4 + q
                pg, pu = gb[q], ub[q]
                sg = sgs.next()
                P.op("scalar", lambda e, sg=sg, pg=pg: e.activation(sg.ap[:, :N], pg.ap[:, :N], ACT.Silu),
                     reads=[pg], writes=[sg])
                P.op("vector", lambda e, sg=sg, pu=pu, fc=fc: e.tensor_tensor(
                    h[fc].ap[:, :N], sg.ap[:, :N], pu.ap[:, :N], op=ALU.mult), reads=[sg, pu], writes=[h[fc]])
        for dg in range(4):
            banks = psum[0:4] if dg % 2 == 0 else psum[4:8]
            f0 = 0
            sizes = [8, 8, 8, 8, 8, 4]
            for si, nf in enumerate(sizes):
                slot, v = wload(dn, l, nf, dg * 512, 512, k0=f0)
                for q in range(4):
                    p = banks[q]
                    P.mm(p, [(p.ap[:, :N], v[:, f, q * 128:(q + 1) * 128], h[f0 + f].ap[:, :N]) for f in range(nf)],
                         reads=[slot] + h[f0:f0 + nf], start=(si == 0), stop=(si == len(sizes) - 1))
                f0 += nf
            for q in range(4):
                dc = dg * 4 + q
                p = banks[q]
                P.op("vector", lambda e, p=p, dc=dc: e.scalar_tensor_tensor(
                    hx[dc].ap[:, :N], p.ap[:, :N], Gcol(k, dc, w), hx[dc].ap[:, :N], op0=ALU.mult, op1=ALU.add),
                    reads=[p, hx[dc], CG], writes=[hx[dc]])
        A.release(m)

    def load_hx(srcname, t0, N):
        v = dr[srcname].rearrange("(c p) t -> p c t", p=128)[:, :, t0:t0 + N]
        P.dma("sync", hx, DT[srcname], hx_all.rearrange("p (c t) -> p c t", c=16)[:, :, 0:N], v)

    def store_hx(t0, N):
        v = dr["HX"].rearrange("(c p) t -> p c t", p=128)[:, :, t0:t0 + N]
        P.dma("sync", DT["HX"], hx, v, hx_all.rearrange("p (c t) -> p c t", c=16)[:, :, 0:N])

    def rope_combine(p1, p2, cs, sn, out_ap, np_, N, tmps, out_t, eng_add="vector"):
        t1 = tmps.next()
        t2 = tmps.next()
        P.op("vector", lambda e: e.tensor_tensor(t1.ap[0:np_, :N], p1.ap[0:np_, :N], cs.ap[0:np_, :N], op=ALU.mult),
             reads=[p1, cs], writes=[t1])
        P.op("vector", lambda e: e.tensor_tensor(t2.ap[0:np_, :N], p2.ap[0:np_, :N], sn.ap[0:np_, :N], op=ALU.mult),
             reads=[p2, sn], writes=[t2])
        P.op(eng_add, lambda e: e.tensor_tensor(out_ap, t1.ap[0:np_, :N], t2.ap[0:np_, :N], op=ALU.add),
             reads=[t1, t2], writes=[out_t])

    accs = [(psum[0], psum[1]), (psum[2], psum[3])]
    ring4 = Rot(psum[4:7])

    def attn_stream(groups, PT):
        flat = []
        for gi, g in enumerate(groups):
            n = len(g["items"])
            for i, it in enumerate(g["items"]):
                flat.append((gi, i == 0, i == n - 1, g, it))

        def issue_S(k):
            it = flat[k][4]
            p = ring4.next()
            pairs = it["s_pairs"](p)
            rd = it["s_reads"]()
            if it["mask"] is not None:
                mk = it["mask"]
                pairs = pairs + [(p.ap, masks.ap[:, 2048:2176], masks.ap[:, mk * 512:(mk + 1) * 512])]
                rd = rd + [masks]
            P.mm(p, pairs, reads=rd)
            return p
        LOOK = 2
        pend = [issue_S(k) for k in range(min(LOOK, len(flat)))]
        for k in range(len(flat)):
            gi, fi, la, g, it = flat[k]
            if fi and g.get("prefetch") is not None:
                g["prefetch"]()
            p = pend.pop(0)
            if k + LOOK < len(flat):
                pend.append(issue_S(k + LOOK))
            N = g["N"]
            po, pd = accs[gi % 2]
            pt = PT.next()
            P.op("scalar", lambda e, pt=pt, p=p, N=N, sc=g["scale"]: e.activation(
                pt.ap[:, :N], p.ap[:, :N], ACT.Exp, scale=sc), reads=[p], writes=[pt])
            P.mm(po, [(po.ap[:, :N], it["v_lhsT"](), pt.ap[:, :N])], reads=it["v_reads"]() + [pt], start=fi, stop=la)
            pair = g.get("pair_den")
            if pair is None:
                P.mm(pd, [(pd.ap[:, :N], ones.ap, pt.ap[:, :N])], reads=[ones, pt], start=fi, stop=la)
            else:
                ci = it["ci"]
                if pair.get("pending") is not None:
                    d2p, stp = pair["pending"]
                    pair["pending"] = None
                    P.mm(pd, [(pd.ap[:, :N], ones.ap, d2p.ap[:, :N])], reads=[ones, d2p], start=stp, stop=False)
                if ci % 2 == 0 and not la:
                    pair["held"] = pt
                else:
                    if ci % 2 == 1:
                        p0 = pair["held"]
                        d2 = pair["buf"].next()
                        P.op("vector", lambda e, d2=d2, p0=p0, pt=pt, N=N: e.tensor_tensor(
                            d2.ap[:, :N], p0.ap[:, :N], pt.ap[:, :N], op=ALU.add), reads=[p0, pt], writes=[d2])
                    else:
                        d2 = pt
                    if la:
                        P.mm(pd, [(pd.ap[:, :N], ones.ap, d2.ap[:, :N])], reads=[ones, d2], start=(ci <= 1), stop=True)
                    else:
                        pair["pending"] = (d2, ci <= 1)
            if la:
                g["finish"](po, pd)
                if g.get("post") is not None:
                    g["post"]()

    def phase1_tile(l, ti):
        t0, N = TILES[ti]
        w = 1 if ti == 4 else 0
        mT = A.mark()
        xn, xn_all = A.chunks(16, 512, BF16, "xn")
        if l == 0:
            if w == 0:
                v = dr["xT"].rearrange("(c p) t -> p c t", p=128)[:, :, t0:t0 + N]
                P.dma("sync", hx, DT["xT"], hx_all.rearrange("p (c t) -> p c t", c=16)[:, :, 0:N], v)
            else:
                v = dr["cT"].rearrange("(c p) t -> p c t", p=128)
                P.dma("sync", hx, DT["cT"], hx_all.rearrange("p (c t) -> p c t", c=16)[:, :, 0:N], v)
        else:
            load_hx("HX", t0, N)
        rmsnorm(hx, N, S_EPS, lambda j: Acol(0, j, w), lambda j: Bcol(0, j, w), xn, [CA, MOD()])
        ffn(l, 1, xn, N, w)
        store_hx(t0, N)
        xm = xn
        rmsnorm(hx, N, S_EPS, lambda j: Acol(1, j, w), lambda j: Bcol(3, j, w), xm, [CA, MOD()])
        xmv = dr["XM"].rearrange("(c p) t -> p c t", p=128)[:, :, t0:t0 + N]
        P.dma("sync", DT["XM"], xm, xmv, xn_all.rearrange("p (c t) -> p c t", c=16)[:, :, 0:N])
        mJ = A.mark()
        stg = Rot([A.tile(512, BF16, "stg") for _ in range(3)])
        ft = Rot([A.tile(512, F32, "ft") for _ in range(4)])
        ust = Rot([A.tile(512, F32, "ust") for _ in range(2)])
        cosg = A.tile(512, F32, "cosg"); sing = A.tile(512, F32, "sing")
        cosm = A.tile(512, F32, "cosm"); sinm = A.tile(512, F32, "sinm")
        P.dma("sync", cosg, DT["cosg"], cosg.ap[:, :N], dr["cosg"][:, t0:t0 + N])
        P.dma("sync", sing, DT["sing"], sing.ap[:, :N], dr["sing"][:, t0:t0 + N])
        P.dma("sync", cosm, DT["cosm"], cosm.ap[0:64, :N], dr["cosm"][:, t0:t0 + N])
        P.dma("sync", sinm, DT["sinm"], sinm.ap[0:64, :N], dr["sinm"][:, t0:t0 + N])
        for s in range(4):
            gs, gv = wload("w_in", l, 16, C_GC + s * 256, 256)
            vs, vv = wload("w_in", l, 16, C_V + s * 256, 256)
            for q in range(2):
                jc = s * 2 + q
                pg = psr.next()
                P.mm(pg, [(pg.ap[:, :N], gv[:, j, q * 128:(q + 1) * 128], xm[j].ap[:, :N]) for j in range(16)], reads=[gs] + xm)
                pv = psr.next()
                P.mm(pv, [(pv.ap[:, :N], vv[:, j, q * 128:(q + 1) * 128], xm[j].ap[:, :N]) for j in range(16)], reads=[vs] + xm)
                t = ft.next()
                P.op("scalar", lambda e, t=t, pg=pg: e.activation(t.ap[:, :N], pg.ap[:, :N], ACT.Copy), reads=[pg], writes=[t])
                u = ust.next()
                P.op("vector", lambda e, u=u, t=t, pv=pv: e.tensor_tensor(u.ap[:, :N], t.ap[:, :N], pv.ap[:, :N], op=ALU.mult),
                     reads=[t, pv], writes=[u])
                if w == 0:
                    P.dma("sync", DT["UE"], u, dr["UE"][jc * 128:(jc + 1) * 128, 1 + t0:1 + t0 + N], u.ap[:, :N])
                    if ti == 0:
                        P.dma("sync", DT[f"XUi{l}"], u, dr[f"XUi{l}"][jc * 128:(jc + 1) * 128, 0:1], u.ap[:, 0:1], slow=True)
                    if ti == 3:
                        P.dma("sync", DT[f"XUi{l}"], u, dr[f"XUi{l}"][jc * 128:(jc + 1) * 128, 1:2], u.ap[:, N - 1:N], slow=True)
                else:
                    P.dma("sync", DT["UC"], u, dr["UC"][jc * 128:(jc + 1) * 128, 1:1 + N], u.ap[:, :N])
        ks, kv_ = wload("w_in", l, 16, C_CKV, 256)
        krs, krv = wload("w_in", l, 16, C_KR, 64)
        rs, rv = wload("winr", l, 16, 1280, 64)
        ckv, _ = A.chunks(2, 512, F32, "ckv")
        ckvn, _ = A.chunks(2, 512, BF16, "ckvn")
        for j2 in range(2):
            p = psr.next()
            P.mm(p, [(p.ap[:, :N], kv_[:, j, j2 * 128:(j2 + 1) * 128], xm[j].ap[:, :N]) for j in range(16)], reads=[ks] + xm)
            P.op("scalar", lambda e, p=p, j2=j2: e.activation(ckv[j2].ap[:, :N], p.ap[:, :N], ACT.Copy), reads=[p], writes=[ckv[j2]])
        rmsnorm(ckv, N, S_EPS + 2, lambda j: misc.ap[:, 4 + j:5 + j], None, ckvn, [misc])
        key0 = t0 if w == 0 else 0
        p1 = psr.next()
        P.mm(p1, [(p1.ap[0:64, :N], krv[:, j, 0:64], xm[j].ap[:, :N]) for j in range(16)], reads=[krs] + xm)
        p2 = psr.next()
        P.mm(p2, [(p2.ap[0:64, :N], rv[:, j, 0:64], xm[j].ap[:, :N]) for j in range(16)], reads=[rs] + xm)
        st = stg.next()
        rope_combine(p1, p2, cosm, sinm, st.ap[0:64, :N], 64, N, ft, st)
        krn = f"XKRi{l}" if w == 0 else "CKR"
        P.dma("sync", DT[krn], st, dr[krn][:, key0:key0 + N], st.ap[0:64, :N])
        kks, kkv = wload("wkvb_k", l, 2, 0, 1024)
        for h in range(8):
            p = psr.next()
            P.mm(p, [(p.ap[:, :N], kkv[:, j2, h * 128:(h + 1) * 128], ckvn[j2].ap[:, :N]) for j2 in range(2)], reads=[kks] + ckvn)
            st = stg.next()
            P.op("scalar", lambda e, st=st, p=p: e.activation(st.ap[:, :N], p.ap[:, :N], ACT.Copy), reads=[p], writes=[st])
            if w == 0:
                knn = f"XKN{h // 2}i{l}"
                P.dma("sync", DT[knn], st, dr[knn][(h % 2) * 128:(h % 2 + 1) * 128, key0:key0 + N], st.ap[:, :N])
            else:
                P.dma("sync", DT["CKN"], st, dr["CKN"][h * 128:(h + 1) * 128, key0:key0 + N], st.ap[:, :N])
        vvs, vvv = wload("wkvb_v", l, 2, 0, 1024)
        for tb in range(N // 128):
            c = key0 // 128 + tb
            for hh in range(2):
                p = psr.next()
                P.mm(p, [(p.ap[:, 0:512], ckvn[j2].ap[:, tb * 128:(tb + 1) * 128], vvv[:, j2, hh * 512:(hh + 1) * 512])
                         for j2 in range(2)], reads=[vvs] + ckvn)
                st = stg.next()
                P.op("vector", lambda e, st=st, p=p: e.tensor_copy(st.ap[:, 0:512], p.ap[:, 0:512]), reads=[p], writes=[st])
                if w == 0:
                    for q2 in range(2):
                        vn = f"XV{hh * 2 + q2}i{l}"
                        vdst = dr[vn].rearrange("(h p) (c d) -> h p c d", h=2, d=128)[:, :, c, :]
                        P.dma("sync", DT[vn], st, vdst.rearrange("h p d -> p h d"),
                              st.ap[:, q2 * 256:(q2 + 1) * 256].rearrange("p (h d) -> p h d", h=2))
                else:
                    vdst = dr["CV"][hh * 4:(hh + 1) * 4, :, c, :]
                    P.dma("sync", DT["CV"], st, vdst.rearrange("h p d -> p h d"),
                          st.ap[:, 0:512].rearrange("p (h d) -> p h d", h=4))
        gks, gkv = wload("w_in", l, 16, C_GK, 256)
        gvs, gvv = wload("w_in", l, 16, C_GV, 256)
        rs, rv = wload("winr", l, 16, 1024, 256)
        gkey0 = 128 + t0 if w == 0 else 128 + HALF + 128
        for g in range(2):
            p1 = psr.next()
            P.mm(p1, [(p1.ap[:, :N], gkv[:, j, g * 128:(g + 1) * 128], xm[j].ap[:, :N]) for j in range(16)], reads=[gks] + xm)
            p2 = psr.next()
            P.mm(p2, [(p2.ap[:, :N], rv[:, j, g * 128:(g + 1) * 128], xm[j].ap[:, :N]) for j in range(16)], reads=[rs] + xm)
            st = stg.next()
            rope_combine(p1, p2, cosg, sing, st.ap[:, :N], 128, N, ft, st)
            P.dma("sync", DT["GKE"], st, dr["GKE"][g, :, gkey0:gkey0 + N], st.ap[:, :N])
            if w == 0 and ti == 0:
                P.dma("sync", DT[f"XGi{l}"], st, dr[f"XGi{l}"][g * 128:(g + 1) * 128, 0:128], st.ap[:, 0:128])
            if w == 0 and ti == 3:
                P.dma("sync", DT[f"XGi{l}"], st, dr[f"XGi{l}"][g * 128:(g + 1) * 128, 128:256], st.ap[:, N - 128:N])
        for tb in range(N // 128):
            c = gkey0 // 128 + tb
            p = psr.next()
            P.mm(p, [(p.ap[:, 0:256], xm[j].ap[:, tb * 128:(tb + 1) * 128], gvv[:, j, 0:256]) for j in range(16)], reads=[gvs] + xm)
            st = stg.next()
            P.op("vector", lambda e, st=st, p=p: e.tensor_copy(st.ap[:, 0:256], p.ap[:, 0:256]), reads=[p], writes=[st])
            P.dma("sync", DT["GVE"], st, dr["GVE"][:, c, :], st.ap[:, 0:256])
            if w == 0 and ti == 0 and tb == 0:
                P.dma("sync", DT[f"XGVi{l}"], st, dr[f"XGVi{l}"][0:128, :], st.ap[:, 0:256])
            if w == 0 and ti == 3 and tb == N // 128 - 1:
                P.dma("sync", DT[f"XGVi{l}"], st, dr[f"XGVi{l}"][128:256, :], st.ap[:, 0:256])
        A.release(mJ)
        A.release(mT)

    def phase2_tile(l, ti, final):
        t0, N = TILES[ti]
        w = 1 if ti == 4 else 0
        mT = A.mark()
        xm, xm_all = A.chunks(16, 512, BF16, "xm", shared_ds=True)
        yc, _ = A.chunks(8, 512, BF16, "yc")
        ymla, _ = A.chunks(8, 512, BF16, "ymla")
        ygqa, ygqa_all = A.chunks(8, 512, BF16, "ygqa")
        P.dma("sync", xm, DT["XM"], xm_all.rearrange("p (c t) -> p c t", c=16)[:, :, 0:N], dr["XM"].rearrange("(c p) t -> p c t", p=128)[:, :, t0:t0 + N])
        m = A.mark()
        U, U_all = A.chunks(8, 516, F32, "U", shared_ds=True)
        if w == 0:
            uv = dr["UE"].rearrange("(c p) t -> p c t", p=128)[:, :, t0:t0 + N + 2]
        else:
            uv = dr["UC"].rearrange("(c p) t -> p c t", p=128)
        Uv = U_all.rearrange("p (c t) -> p c t", c=8)[:, :, 0:N + 2]
        P.dma("sync", U, DT["UE" if w == 0 else "UC"], Uv, uv)
        U3 = U_all.rearrange("p (c t) -> p c t", c=8)
        if w == 1:
            P.op("vector", lambda e: e.memset(U3[:, :, 0:1], 0.0), writes=U)
            P.op("vector", lambda e: e.memset(U3[:, :, N + 1:N + 2], 0.0), writes=U)
        else:
            xuo = dr[f"XUo{l}"].rearrange("(r c p) t -> r p c t", r=2, p=128)
            if t0 == 0:
                P.dma("sync", U, DT[f"XUo{l}"], U3[:, :, 0:1], xuo[0][:, :, 1:2], slow=True)
                P.op("vector", lambda e: e.tensor_scalar(U3[:, :, 0:1], U3[:, :, 0:1], sm(S_FLAG), None, op0=ALU.mult),
                     reads=[smalls] + U, writes=U)
            if t0 + N == HALF:
                P.dma("sync", U, DT[f"XUo{l}"], U3[:, :, N + 1:N + 2], xuo[1][:, :, 0:1], slow=True)
                P.op("vector", lambda e: e.tensor_scalar(U3[:, :, N + 1:N + 2], U3[:, :, N + 1:N + 2], sm(S_FLAG + 1), None, op0=ALU.mult),
                     reads=[smalls] + U, writes=U)
        cy = Rot([A.tile(512, F32, "cy") for _ in range(2)])
        cb = l * LS + S_CONV
        for s in range(4):
            gs, gv = wload("w_in", l, 16, C_GB + s * 256, 256)
            for q in range(2):
                jc = s * 2 + q
                pg = psr.next()
                P.mm(pg, [(pg.ap[:, :N], gv[:, j, q * 128:(q + 1) * 128], xm[j].ap[:, :N]) for j in range(16)], reads=[gs] + xm)
                y = cy.next()
                u = U[jc]
                P.op("vector", lambda e, y=y, u=u, jc=jc: e.tensor_scalar(
                    y.ap[:, :N], u.ap[:, 0:N], sm(cb + jc), None, op0=ALU.mult), reads=[u, smalls], writes=[y])
                P.op("vector", lambda e, y=y, u=u, jc=jc: e.scalar_tensor_tensor(
                    y.ap[:, :N], u.ap[:, 1:N + 1], sm(cb + 8 + jc), y.ap[:, :N], op0=ALU.mult, op1=ALU.add),
                    reads=[u, smalls, y], writes=[y])
                P.op("vector", lambda e, y=y, u=u, jc=jc: e.scalar_tensor_tensor(
                    y.ap[:, :N], u.ap[:, 2:N + 2], sm(cb + 16 + jc), y.ap[:, :N], op0=ALU.mult, op1=ALU.add),
                    reads=[u, smalls, y], writes=[y])
                P.op("vector", lambda e, y=y, pg=pg, jc=jc: e.tensor_tensor(
                    yc[jc].ap[:, :N], y.ap[:, :N], pg.ap[:, :N], op=ALU.mult), reads=[y, pg], writes=[yc[jc]])
        A.release(m)
        m = A.mark()
        P.barrier(hx)
        mh = HA.mark()
        qn, _ = HA.chunks(8, 512, BF16, "qn")
        qr, _ = HA.chunks(8, 512, BF16, "qr")
        KR = HA.tile(NKEY, BF16, "KR")
        kn0 = A.tile(NKEY, BF16, "KN0")
        kchunks = list(range(NKC)) if w == 0 else [NKC - 2, NKC - 1]
        kc0 = kchunks[0] * 128
        nk = len(kchunks) * 128
        P.op("vector", lambda e: e.memset(KR.ap[64:128, :], 0.0), writes=[KR])
        if w == 0:
            for r in range(2):
                P.dma("sync", KR, DT[f"XKRo{l}"], KR.ap[0:64, r * HALF:(r + 1) * HALF], dr[f"XKRo{l}"][r * 64:(r + 1) * 64, :])
        P.dma("sync", KR, DT["CKR"], KR.ap[0:64, 2 * HALF:2 * HALF + CTX], dr["CKR"])

        def load_kn(h, kn):
            if w == 0:
                for r in range(2):
                    r0 = r * 256 + (h % 2) * 128
                    P.dma("sync", kn, DT[f"XKN{h // 2}o{l}"], kn.ap[:, r * HALF:(r + 1) * HALF],
                          dr[f"XKN{h // 2}o{l}"][r0:r0 + 128, :])
            P.dma("sync", kn, DT["CKN"], kn.ap[:, 2 * HALF:2 * HALF + CTX], dr["CKN"][h * 128:(h + 1) * 128, :])
        load_kn(0, kn0)
        m2 = A.mark()
        cq, _ = A.chunks(4, 512, F32, "cq")
        cqn, _ = A.chunks(4, 512, BF16, "cqn")
        cosm = A.tile(512, F32, "cosm"); sinm = A.tile(512, F32, "sinm")
        ft = Rot([A.tile(512, F32, "ft") for _ in range(4)])
        P.dma("sync", cosm, DT["cosm"], cosm.ap[0:64, :N], dr["cosm"][:, t0:t0 + N])
        P.dma("sync", sinm, DT["sinm"], sinm.ap[0:64, :N], dr["sinm"][:, t0:t0 + N])
        for c4 in range(4):
            if c4 % 2 == 0:
                qs, qv = wload("w_in", l, 16, C_CQ + c4 * 128, 256)
            p = psr.next()
            P.mm(p, [(p.ap[:, :N], qv[:, j, (c4 % 2) * 128:(c4 % 2 + 1) * 128], xm[j].ap[:, :N]) for j in range(16)], reads=[qs] + xm)
            P.op("scalar", lambda e, p=p, c4=c4: e.activation(cq[c4].ap[:, :N], p.ap[:, :N], ACT.Copy), reads=[p], writes=[cq[c4]])
        rmsnorm(cq, N, S_EPS + 1, lambda j: misc.ap[:, j:j + 1], None, cqn, [misc])
        ns_, nv = wload("wqb_n", l, 4, 0, 1024)
        r1s, r1v = wload("wqb_r", l, 4, 0, 512)
        r2s, r2v = wload("wqb_rr", l, 4, 0, 512)
        for h in range(8):
            p = psr.next()
            P.mm(p, [(p.ap[:, :N], nv[:, c4, h * 128:(h + 1) * 128], cqn[c4].ap[:, :N]) for c4 in range(4)], reads=[ns_] + cqn)
            P.op("scalar", lambda e, p=p, h=h: e.activation(qn[h].ap[:, :N], p.ap[:, :N], ACT.Copy), reads=[p], writes=[qn[h]])
            p1 = psr.next()
            P.mm(p1, [(p1.ap[0:64, :N], r1v[:, c4, h * 64:(h + 1) * 64], cqn[c4].ap[:, :N]) for c4 in range(4)], reads=[r1s] + cqn)
            p2 = psr.next()
            P.mm(p2, [(p2.ap[0:64, :N], r2v[:, c4, h * 64:(h + 1) * 64], cqn[c4].ap[:, :N]) for c4 in range(4)], reads=[r2s] + cqn)
            P.op("vector", lambda e, h=h: e.memset(qr[h].ap[64:128, :N], 0.0), writes=[qr[h]])
            rope_combine(p1, p2, cosm, sinm, qr[h].ap[0:64, :N], 64, N, ft, qr[h])
        A.release(m2)
        KN = Rot([kn0, A.tile(NKEY, BF16, "KN1")])
        VV = Rot([A.tile(NKEY, BF16, f"VV{i}") for i in range(2)])
        PT = Rot([A.tile(512, BF16, "PT") for _ in range(4)])
        pair_den = dict(held=None, pending=None, buf=Rot([A.tile(512, BF16, f"pd2{i}") for i in range(2)]))
        rden = A.tile(512, F32, "rden")
        kns = [None] * 8
        vvs_ = [None] * 8

        def load_head(h):
            kn = KN.next()
            vv = VV.next()
            if h > 0:
                load_kn(h, kn)
            if w == 0:
                for r in range(2):
                    r0 = r * 256 + (h % 2) * 128
                    P.dma("sync", vv, DT[f"XV{h // 2}o{l}"], vv.ap[:, r * HALF:(r + 1) * HALF],
                          dr[f"XV{h // 2}o{l}"][r0:r0 + 128, :])
            P.dma("sync", vv, DT["CV"], vv.ap[:, 2 * HALF:2 * HALF + CTX].rearrange("p (c d) -> p c d", d=128), dr["CV"][h])
            kns[h], vvs_[h] = kn, vv

        def mk_finish(h):
            def fin(po, pd):
                P.op("vector", lambda e: e.reciprocal(rden.ap[:, :N], pd.ap[:, :N]), reads=[pd], writes=[rden])
                P.op("vector", lambda e: e.tensor_tensor(ymla[h].ap[:, :N], po.ap[:, :N], rden.ap[:, :N], op=ALU.mult),
                     reads=[po, rden], writes=[ymla[h]])
            return fin

        def mk_item(h, c):
            def s_pairs(p):
                return [(p.ap[:, :N], kns[h].ap[:, c * 128:(c + 1) * 128], qn[h].ap[:, :N]),
                        (p.ap[:, :N], KR.ap[:, c * 128:(c + 1) * 128], qr[h].ap[:, :N])]
            return dict(s_pairs=s_pairs, s_reads=lambda: [kns[h], KR, qn[h], qr[h]], ci=c - kchunks[0],
                        v_lhsT=lambda: vvs_[h].ap[:, c * 128:(c + 1) * 128], v_reads=lambda: [vvs_[h]], mask=None)

        load_head(0)
        groups = []
        for h in range(8):
            groups.append(dict(N=N, scale=float(MLA_SCALE), items=[mk_item(h, c) for c in kchunks], finish=mk_finish(h), pair_den=pair_den,
                               prefetch=(lambda h=h: load_head(h + 1)) if h < 7 else None,
                               post=(lambda: ada_slabs(l + 1, 2, [psum[7]])) if l + 1 < L else None))
        attn_stream(groups, PT)
        HA.release(mh)
        A.release(m)
        m = A.mark()
        NB = N // 128
        qg = A.tile(8 * 512, BF16, "qg")
        cosg = A.tile(512, F32, "cosg"); sing = A.tile(512, F32, "sing")
        ft = Rot([A.tile(512, F32, "ft") for _ in range(4)])
        GK = A.tile(2 * GKEYS, BF16, "GK")
        GV = A.tile(20 * 256, BF16, "GV")
        PT = Rot([A.tile(512, BF16, "PTg") for _ in range(3)])
        dent = A.tile(512, F32, "dent")
        P.dma("sync", cosg, DT["cosg"], cosg.ap[:, :N], dr["cosg"][:, t0:t0 + N])
        P.dma("sync", sing, DT["sing"], sing.ap[:, :N], dr["sing"][:, t0:t0 + N])
        GK3 = GK.ap.rearrange("p (g k) -> p g k", g=2)
        GV3 = GV.ap.rearrange("p (c d) -> p c d", c=20)
        P.dma("sync", GK, DT["GKE"], GK3[:, :, 128:GKEYS], dr["GKE"].rearrange("g p k -> p g k")[:, :, 128:GKEYS])
        P.dma("sync", GV, DT["GVE"], GV3[:, 1:20, :], dr["GVE"][:, 1:20, :])
        if w == 0:
            xgo = dr[f"XGo{l}"].rearrange("(r g p) k -> r p g k", r=2, g=2)
            P.dma("sync", GK, DT[f"XGo{l}"], GK3[:, :, 0:128], xgo[0][:, :, 128:256])
            P.dma("sync", GK, DT[f"XGo{l}"], GK3[:, :, 128 + HALF:128 + HALF + 128], xgo[1][:, :, 0:128])
            P.dma("sync", GV, DT[f"XGVo{l}"], GV.ap[:, 0:256], dr[f"XGVo{l}"][128:256, :])
            P.dma("sync", GV, DT[f"XGVo{l}"], GV.ap[:, 17 * 256:18 * 256], dr[f"XGVo{l}"][256:384, :])
        for s in range(4):
            q1s, q1v = wload("w_in", l, 16, C_GQ + s * 256, 256)
            q2s, q2v = wload("winr", l, 16, s * 256, 256)
            for q in range(2):
                h = s * 2 + q
                p1 = psr.next()
                P.mm(p1, [(p1.ap[:, :N], q1v[:, j, q * 128:(q + 1) * 128], xm[j].ap[:, :N]) for j in range(16)], reads=[q1s] + xm)
                p2 = psr.next()
                P.mm(p2, [(p2.ap[:, :N], q2v[:, j, q * 128:(q + 1) * 128], xm[j].ap[:, :N]) for j in range(16)], reads=[q2s] + xm)
                outv = qg.ap.rearrange("p (b h i) -> p b h i", b=4, h=8)[:, 0:NB, h, :]
                t1 = ft.next(); t2 = ft.next()
                P.op("vector", lambda e, t1=t1, p1=p1: e.tensor_tensor(t1.ap[:, :N], p1.ap[:, :N], cosg.ap[:, :N], op=ALU.mult),
                     reads=[p1, cosg], writes=[t1])
                P.op("vector", lambda e, t2=t2, p2=p2: e.tensor_tensor(t2.ap[:, :N], p2.ap[:, :N], sing.ap[:, :N], op=ALU.mult),
                     reads=[p2, sing], writes=[t2])
                P.op("vector", lambda e, t1=t1, t2=t2, outv=outv: e.tensor_tensor(
                    outv, t1.ap[:, :N].rearrange("p (b i) -> p b i", i=128), t2.ap[:, :N].rearrange("p (b i) -> p b i", i=128),
                    op=ALU.add), reads=[t1, t2], writes=[qg])
        groups = []
        for qb in range(NB):
            nb = t0 // 128 + qb
            if w == 0:
                chunks = [(nb, 2 if nb == 0 else 0), (nb + 1, None), (nb + 2, 3 if nb == 15 else 1), (18, None), (19, None)]
            else:
                chunks = [(18, None), (19, None)]
            for g in range(2):
                rhs = qg.ap[:, qb * 1024 + g * 512: qb * 1024 + (g + 1) * 512]

                def mk_item(c, mk, g=g, rhs=rhs):
                    return dict(s_pairs=lambda p: [(p.ap, GK.ap[:, g * GKEYS + c * 128: g * GKEYS + (c + 1) * 128], rhs)],
                                s_reads=lambda: [GK, qg],
                                v_lhsT=lambda: GV.ap[:, c * 256 + g * 128: c * 256 + (g + 1) * 128],
                                v_reads=lambda: [GV], mask=mk)

                def fin(po, pd, g=g, qb=qb):
                    P.op("vector", lambda e: e.tensor_tensor(dent.ap, pd.ap, sinkrow.ap[:, g * 512:(g + 1) * 512], op=ALU.add),
                         reads=[pd, sinkrow], writes=[dent])
                    P.op("vector", lambda e: e.reciprocal(dent.ap, dent.ap), reads=[dent], writes=[dent])
                    outv = ygqa_all.rearrange("p (h t) -> p h t", h=8)[:, g * 4:(g + 1) * 4, qb * 128:(qb + 1) * 128]
                    P.op("vector", lambda e: e.tensor_tensor(
                        outv, po.ap.rearrange("p (h i) -> p h i", h=4), dent.ap.rearrange("p (h i) -> p h i", h=4), op=ALU.mult),
                        reads=[po, dent], writes=ygqa)
                groups.append(dict(N=512, scale=float(GQA_SCALE), items=[mk_item(c, mk) for c, mk in chunks], finish=fin, prefetch=None))
        attn_stream(groups, PT)
        A.release(m)
        m = A.mark()
        macc, _ = A.chunks(16, 512, F32, "macc")
        merged = yc + ymla
        sig = Rot([A.tile(512, F32, "sig") for _ in range(2)])
        tt = Rot([A.tile(512, F32, "tt") for _ in range(2)])
        for b, (bw, ys) in enumerate((("w_branch_conv", yc), ("w_branch_mla", ymla), ("w_branch_gqa", ygqa))):
            for dg in range(8):
                gs, gv = wload("w_in", l, 16, C_GATE + b * D + dg * 256, 256)
                bs, bv = wload(bw, l, 8, dg * 256, 256)
                for q in range(2):
                    dc = dg * 2 + q
                    pg = psr.next()
                    P.mm(pg, [(pg.ap[:, :N], gv[:, j, q * 128:(q + 1) * 128], xm[j].ap[:, :N]) for j in range(16)], reads=[gs] + xm)
                    pb = psr.next()
                    P.mm(pb, [(pb.ap[:, :N], bv[:, j, q * 128:(q + 1) * 128], ys[j].ap[:, :N]) for j in range(8)], reads=[bs] + ys)
                    sg = sig.next()
                    P.op("scalar", lambda e, sg=sg, pg=pg: e.activation(sg.ap[:, :N], pg.ap[:, :N], ACT.Sigmoid), reads=[pg], writes=[sg])
                    if b == 0:
                        P.op("vector", lambda e, sg=sg, pb=pb, dc=dc: e.tensor_tensor(
                            macc[dc].ap[:, :N], sg.ap[:, :N], pb.ap[:, :N], op=ALU.mult), reads=[sg, pb], writes=[macc[dc]])
                    else:
                        t = tt.next()
                        P.op("vector", lambda e, sg=sg, pb=pb, t=t: e.tensor_tensor(
                            t.ap[:, :N], sg.ap[:, :N], pb.ap[:, :N], op=ALU.mult), reads=[sg, pb], writes=[t])
                        o = macc[dc] if b == 1 else merged[dc]
                        P.op("vector", lambda e, t=t, o=o, dc=dc: e.tensor_tensor(
                            o.ap[:, :N], macc[dc].ap[:, :N], t.ap[:, :N], op=ALU.add), reads=[t, macc[dc]], writes=[o])
        load_hx_from(dr["HX"], DT["HX"], t0, N)
        for s in range(8):
            ws, wv = wload("w_out", l, 16, s * 256, 256)
            for q in range(2):
                dc = s * 2 + q
                p = psr.next()
                P.mm(p, [(p.ap[:, :N], wv[:, j, q * 128:(q + 1) * 128], merged[j].ap[:, :N]) for j in range(16)], reads=[ws] + merged)
                P.op("vector", lambda e, p=p, dc=dc: e.scalar_tensor_tensor(
                    hx[dc].ap[:, :N], p.ap[:, :N], Gcol(1, dc, w), hx[dc].ap[:, :N], op0=ALU.mult, op1=ALU.add),
                    reads=[p, hx[dc], CG], writes=[hx[dc]])
        A.release(m)
        A.release(mT)
        mT = A.mark()
        xn, _ = A.chunks(16, 512, BF16, "xn2")
        rmsnorm(hx, N, S_EPS, lambda j: Acol(2, j, w), lambda j: Bcol(6, j, w), xn, [CA, MOD()])
        ffn(l, 2, xn, N, w)
        if final:
            outs, outs_all = A.chunks(16, 512, F32, "fin")
            rmsnorm(hx, N, S_EPS, lambda j: misc.ap[:, 8 + j:9 + j], None, outs, [misc], sq_tiles=xn)
            ov = dr["outT"].rearrange("(c p) t -> p c t", p=128)[:, :, t0:t0 + N]
            P.dma("sync", DT["outT"], outs, ov, outs_all.rearrange("p (c t) -> p c t", c=16)[:, :, 0:N])
        else:
            store_hx(t0, N)
        A.release(mT)

    def DT_S(n):
        return DT[n + "_in"] if (n + "_in") in DT else DT[n]

    def load_hx_from(ap, t, t0, N):
        v = ap.rearrange("(c p) t -> p c t", p=128)[:, :, t0:t0 + N]
        P.dma("sync", hx, t, hx_all.rearrange("p (c t) -> p c t", c=16)[:, :, 0:N], v)

    for l in range(L):
        layer_setup(l)
        for ti in range(4):
            phase1_tile(l, ti)
        for n, r, c, dt in XCH_SPECS:
            P.cc(DT[f"{n}o{l}"], DT[f"{n}i{l}"], dr[f"{n}o{l}"], dr[f"{n}i{l}"], PAIRS)
        phase1_tile(l, 4)
        for ti in range(5 if l == 0 else 4):
            phase2_tile(l, ti, final=(l == L - 1))
    outs_t = [DT["outT"]]
    P.wait_all("sync", outs_t)
    for X in COMPUTE:
        P.wait_all(X, outs_t)
    P.emit()
    nc._wkeys = list(wkeys)
    build.last = (P.n_inst, len(P.dval), {e: len(P.waited[e]) for e in COMPUTE})
    return nc


def _rope_tables(half):
    def tab(dim_half, nfreq_dim):
        inv = (10000.0 ** (-np.arange(0, dim_half, 2, dtype=np.float32) / np.float32(dim_half))).astype(np.float32)
        return inv
    pos = np.arange(half * HALF, (half + 1) * HALF)
    row = (pos // 64).astype(np.float32)
    col = (pos % 64).astype(np.float32)

    def make(hd):
        inv = tab(hd, None)
        nf = hd // 2
        ar = row[None, :] * inv[:, None]
        ac = col[None, :] * inv[:, None]
        cr, sr, cc, sc = np.cos(ar), np.sin(ar), np.cos(ac), np.sin(ac)
        cos = np.concatenate([cr, cr, cc, cc], 0)
        sin = np.concatenate([-sr, sr, -sc, sc], 0)
        cosf = np.ones((2 * hd, TOK), np.float32)
        sinf = np.zeros((2 * hd, TOK), np.float32)
        cosf[:, :HALF] = cos
        sinf[:, :HALF] = sin
        return cosf, sinf
    cg, sg = make(64)
    cm, sm_ = make(32)
    return cg, sg, cm, sm_


def _fm(v):
    return np.ascontiguousarray(v.reshape(-1, 128).T)


def _prep_static(inp):
    f = lambda a: np.ascontiguousarray(np.asarray(a, dtype=np.float32))
    w_in = f(inp["w_in"])
    pg = np.concatenate([np.arange(32, 64), np.arange(0, 32), np.arange(96, 128), np.arange(64, 96)])
    pm = np.concatenate([np.arange(16, 32), np.arange(0, 16), np.arange(48, 64), np.arange(32, 48)])
    cols = []
    for h in range(8):
        cols.append(C_GQ + h * 128 + pg)
    for g in range(2):
        cols.append(C_GK + g * 128 + pg)
    cols.append(C_KR + pm)
    cols = np.concatenate(cols)
    wqb = f(inp["mla_w_qb"])
    wkvb = f(inp["mla_w_kvb"])
    cn = np.concatenate([h * 192 + np.arange(128) for h in range(8)])
    cr = np.concatenate([h * 192 + 128 + np.arange(64) for h in range(8)])
    crr = np.concatenate([h * 192 + 128 + pm for h in range(8)])
    ck = np.concatenate([h * 256 + np.arange(128) for h in range(8)])
    cv = np.concatenate([h * 256 + 128 + np.arange(128) for h in range(8)])
    W = {
        "ada_w": f(inp["ada_w"]), "ffn1_w_gu": f(inp["ffn1_w_gu"]), "ffn1_w_down": f(inp["ffn1_w_down"]),
        "w_in": w_in, "winr": np.ascontiguousarray(w_in[:, :, cols]),
        "wqb_n": np.ascontiguousarray(wqb[:, :, cn]), "wqb_r": np.ascontiguousarray(wqb[:, :, cr]),
        "wqb_rr": np.ascontiguousarray(wqb[:, :, crr]),
        "wkvb_k": np.ascontiguousarray(wkvb[:, :, ck]), "wkvb_v": np.ascontiguousarray(wkvb[:, :, cv]),
        "w_branch_conv": f(inp["w_branch_conv"]), "w_branch_mla": f(inp["w_branch_mla"]),
        "w_branch_gqa": f(inp["w_branch_gqa"]), "w_out": f(inp["w_out"]),
        "ffn2_w_gu": f(inp["ffn2_w_gu"]), "ffn2_w_down": f(inp["ffn2_w_down"]),
    }
    return W


def _smalls(inp, b, half):
    s = np.zeros((128, NS), np.float32)
    for l in range(L):
        o = l * LS
        ab = _fm(np.asarray(inp["ada_b"][l], np.float32))
        s[:, o + S_ADAB:o + S_ADAB + 288] = np.repeat(ab, 2, axis=1)
        s[:, o + S_N1:o + S_N1 + 16] = _fm(np.asarray(inp["ffn1_norm"][l], np.float32))
        s[:, o + S_NM:o + S_NM + 16] = _fm(np.asarray(inp["mix_norm"][l], np.float32))
        s[:, o + S_N2:o + S_N2 + 16] = _fm(np.asarray(inp["ffn2_norm"][l], np.float32))
        s[:, o + S_QN:o + S_QN + 4] = _fm(np.asarray(inp["mla_q_norm"][l], np.float32))
        s[:, o + S_KVN:o + S_KVN + 2] = _fm(np.asarray(inp["mla_kv_norm"][l], np.float32))
        cw = np.asarray(inp["conv_w"][l], np.float32)
        for k in range(3):
            s[:, o + S_CONV + k * 8:o + S_CONV + (k + 1) * 8] = _fm(cw[k])
        s[:, o + S_SINK:o + S_SINK + 8] = np.asarray(inp["gqa_sink"][l], np.float32)[None, :]
    s[:, S_FIN:S_FIN + 16] = _fm(np.asarray(inp["final_norm"], np.float32))
    cv = np.stack([_fm(np.asarray(inp["c"][b], np.float32)), _fm(np.asarray(inp["c_ctx"], np.float32))], axis=2)
    s[:, S_CVEC:S_CVEC + 32] = cv.reshape(128, 32)
    s[:, S_FLAG] = 0.0 if half == 0 else 1.0
    s[:, S_FLAG + 1] = 1.0 if half == 0 else 0.0
    s[:, S_EPS] = EPS * 2048
    s[:, S_EPS + 1] = EPS * 512
    s[:, S_EPS + 2] = EPS * 256
    return s


def _masks(half):
    j = np.arange(128)[:, None]
    i = np.arange(128)[None, :]
    prev = (j >= i).astype(np.float32)
    nxt = (j <= i).astype(np.float32)
    first = prev if half == 1 else np.zeros_like(prev)
    last = nxt if half == 0 else np.zeros_like(nxt)
    m = np.concatenate([np.tile(1.0 - x, (1, 4)) for x in (prev, nxt, first, last)]
                       + [-30000.0 * np.eye(128, dtype=np.float32)], axis=1)
    return m.astype(ml_dtypes.bfloat16)


_NC_CACHE = {}


def _get_nc():
    if "nc" not in _NC_CACHE:
        _NC_CACHE["nc"] = build()
    return _NC_CACHE["nc"]


def kernel(**inp):
    W = _prep_static(inp)
    x = np.asarray(inp["x"], np.float32)
    ctx = np.asarray(inp["ctx"], np.float32)
    nc = _get_nc()
    ws = {k: W[n][l] for k, n, l in nc._wkeys}
    ins = []
    for c in range(8):
        b, half = c // 2, c % 2
        cg, sg, cm, sm_ = _rope_tables(half)
        d = dict(ws)
        d["smalls"] = _smalls(inp, b, half)
        d["cosg"], d["sing"], d["cosm"], d["sinm"] = cg, sg, cm, sm_
        d["masks"] = _masks(half)
        d["xT"] = np.ascontiguousarray(x[b, half * HALF:(half + 1) * HALF, :].T)
        d["cT"] = np.ascontiguousarray(ctx[b].T)
        ins.append(d)
    r = run_bass_kernel_spmd(nc, ins, core_ids=list(range(8))).results
    out = np.zeros((4, SEQ, D), np.float32)
    for c in range(8):
        b, half = c // 2, c % 2
        out[b, half * HALF:(half + 1) * HALF, :] = np.asarray(r[c]["outT"]).T
    return out
```
